# Optimizing a Trainium2 kernel written in Bass

```python
import math
import numpy as np
import jax
import jax.numpy as jnp
from jax import lax


D_MODEL = 4096
BATCH = 1
SEQ = 16384
DEPTH = 2

MEM_LEN = 256
N_EVEN = (DEPTH + 1) // 2
N_ODD = DEPTH // 2
EPS = 1e-6

RET_HEADS = 8
RET_DIM = D_MODEL // 2 // RET_HEADS
RET_WIDTH = RET_HEADS * RET_DIM
RET_CHUNK = 128
ROPE_BASE = 10000.0
FOX_HEAD_DIM = 128
FOX_WIDTH = D_MODEL // 2
FOX_HEADS = FOX_WIDTH // FOX_HEAD_DIM
FOX_BLOCK = 128
EVEN_IN = 4 * RET_WIDTH + 3 * FOX_WIDTH + FOX_HEADS
EVEN_MIX = RET_WIDTH + FOX_WIDTH
SSM_INNER = 2 * D_MODEL
SSM_HEAD_DIM = 64
SSM_HEADS = SSM_INNER // SSM_HEAD_DIM
SSM_STATE = 128
SSM_GROUPS = 8
SSM_HPG = SSM_HEADS // SSM_GROUPS
SSM_CONV = 4
SSM_CHUNK = 128
SSM_CONV_DIM = SSM_INNER + 2 * SSM_GROUPS * SSM_STATE
ODD_IN = SSM_INNER + SSM_CONV_DIM + SSM_HEADS
XA_HEADS = 4
XA_HEAD_DIM = 128
XA_WIDTH = XA_HEADS * XA_HEAD_DIM
D_FF = ((8 * D_MODEL + 3 * 256 - 1) // (3 * 256)) * 256

kernel_name = "retention_fox_mamba2_hybrid"


def _normalize(x):
    xf = x.astype(jnp.float32)
    return xf * lax.rsqrt(jnp.mean(xf * xf, axis=-1, keepdims=True) + EPS)


def rms_norm(x, g):
    return (_normalize(x) * g.astype(jnp.float32)).astype(x.dtype)


def rotary(x, pos):
    half = x.shape[-1] // 2
    inv = ROPE_BASE ** (-jnp.arange(half, dtype=jnp.float32) / half)
    ang = pos.astype(jnp.float32)[:, None] * inv[None, :]
    cos = jnp.cos(ang)[None, :, None, :]
    sin = jnp.sin(ang)[None, :, None, :]
    x1, x2 = x[..., :half], x[..., half:]
    return jnp.concatenate([x1 * cos - x2 * sin, x1 * sin + x2 * cos], axis=-1)


def retention_chunkwise(q, k, v):
    b, s, h, dk = q.shape
    dv = v.shape[-1]
    L = RET_CHUNK
    n = s // L
    log_g = jnp.log1p(-jnp.exp2(-5.0 - jnp.arange(h, dtype=jnp.float32)))
    idx = jnp.arange(L, dtype=jnp.float32)
    diff = idx[:, None] - idx[None, :]
    causal = diff >= 0
    decay_in = jnp.where(causal[None], jnp.exp(log_g[:, None, None] * jnp.where(causal, diff, 0.0)[None]), 0.0)
    q_decay = jnp.exp(log_g[None, :] * (idx[:, None] + 1.0))[None, :, :, None]
    k_decay = jnp.exp(log_g[None, :] * (L - 1.0 - idx[:, None]))[None, :, :, None]
    chunk_decay = jnp.exp(log_g * L)[None, :, None, None]

    def to_chunks(t):
        return jnp.moveaxis(t.reshape(b, n, L, h, t.shape[-1]), 1, 0)

    def step(state, inp):
        qc, kc, vc = inp
        scores = jnp.einsum('bthd,bshd->bhts', qc, kc) * decay_in
        inner = jnp.einsum('bhts,bshe->bthe', scores, vc)
        cross = jnp.einsum('bthd,bhde->bthe', qc, state) * q_decay
        new_state = state * chunk_decay + jnp.einsum('bshd,bshe->bhde', kc * k_decay, vc)
        return new_state, inner + cross

    state0 = jnp.zeros((b, h, dk, dv), jnp.float32)
    _, out = lax.scan(step, state0, (to_chunks(q), to_chunks(k), to_chunks(v)))
    return jnp.moveaxis(out, 0, 1).reshape(b, s, h, dv)


def forgetting_attention(q, k, v, log_f):
    b, s, h, d = q.shape
    c = jnp.cumsum(log_f, axis=1)
    ct = jnp.transpose(c, (0, 2, 1))
    nb = s // FOX_BLOCK
    qb = jnp.moveaxis(q.reshape(b, nb, FOX_BLOCK, h, d), 1, 0)
    cb = jnp.moveaxis(c.reshape(b, nb, FOX_BLOCK, h), 1, 0)
    starts = jnp.arange(nb) * FOX_BLOCK
    kpos = jnp.arange(s)
    scale = d ** -0.5

    def block(args):
        qi, ci, st = args
        logits = jnp.einsum('bqhd,bkhd->bhqk', qi, k) * scale \
            + jnp.transpose(ci, (0, 2, 1))[..., None] - ct[:, :, None, :]
        qpos = st + jnp.arange(FOX_BLOCK)
        mask = kpos[None, :] <= qpos[:, None]
        p = jax.nn.softmax(jnp.where(mask, logits, -jnp.inf), axis=-1)
        return jnp.einsum('bhqk,bkhd->bqhd', p, v)

    out = lax.map(block, (qb, cb, starts))
    return jnp.moveaxis(out, 0, 1).reshape(b, s, h, d)


def retention_fox_mixer(h, w_in, b_f, w_out):
    b, s, _ = h.shape
    f32 = jnp.float32
    proj = h @ w_in
    sizes = [RET_WIDTH] * 4 + [FOX_WIDTH] * 3 + [FOX_HEADS]
    offs = np.cumsum(sizes)[:-1].tolist()
    rq, rk, rv, rg, fq, fk, fv, ff = jnp.split(proj, offs, axis=-1)
    pos = jnp.arange(s)
    rq = rotary(rq.astype(f32).reshape(b, s, RET_HEADS, RET_DIM), pos)
    rk = rotary(rk.astype(f32).reshape(b, s, RET_HEADS, RET_DIM), pos) * (RET_DIM ** -0.5)
    rv = rv.astype(f32).reshape(b, s, RET_HEADS, RET_DIM)
    ret = _normalize(retention_chunkwise(rq, rk, rv)).reshape(b, s, RET_WIDTH)
    ret = jax.nn.silu(rg.astype(f32)) * ret
    log_f = jax.nn.log_sigmoid(ff.astype(f32) + b_f.astype(f32))
    fox = forgetting_attention(fq.astype(f32).reshape(b, s, FOX_HEADS, FOX_HEAD_DIM),
                               fk.astype(f32).reshape(b, s, FOX_HEADS, FOX_HEAD_DIM),
                               fv.astype(f32).reshape(b, s, FOX_HEADS, FOX_HEAD_DIM),
                               log_f).reshape(b, s, FOX_WIDTH)
    mixed = jnp.concatenate([ret, fox], axis=-1).astype(h.dtype)
    return mixed @ w_out


def causal_depthwise_conv(x, w, bias):
    c = x.shape[-1]
    y = lax.conv_general_dilated(x, w[:, None, :].astype(x.dtype), window_strides=(1,),
                                 padding=[(SSM_CONV - 1, 0)],
                                 dimension_numbers=('NWC', 'WIO', 'NWC'),
                                 feature_group_count=c)
    return y + bias.astype(x.dtype)


def ssd_chunked(x, dt, a, bm, cm):
    b, s, g, e, p = x.shape
    n = bm.shape[-1]
    L = SSM_CHUNK
    nc = s // L
    xdt = x * dt[..., None]
    da = dt * a

    def chunks(t):
        return jnp.moveaxis(t.reshape((b, nc, L) + t.shape[2:]), 1, 0)

    idx = jnp.arange(L)
    causal = idx[:, None] >= idx[None, :]

    def step(state, inp):
        xc, dac, bc, cc = inp
        acs = jnp.cumsum(dac, axis=1)
        acs_t = jnp.moveaxis(acs, 1, -1)
        seg = acs_t[..., :, None] - acs_t[..., None, :]
        decay = jnp.exp(jnp.where(causal, seg, -jnp.inf))
        cb = jnp.einsum('btgn,bsgn->bgts', cc, bc)
        y_diag = jnp.einsum('bgets,bsgep->btgep', cb[:, :, None] * decay, xc)
        y_off = jnp.einsum('btgn,bgepn->btgep', cc, state) * jnp.exp(acs)[..., None]
        tail = jnp.exp(acs[:, -1:] - acs)
        new_state = state * jnp.exp(acs[:, -1])[..., None, None] \
            + jnp.einsum('bsgn,bsgep->bgepn', bc, xc * tail[..., None])
        return new_state, y_diag + y_off

    state0 = jnp.zeros((b, g, e, p, n), jnp.float32)
    _, y = lax.scan(step, state0, (chunks(xdt), chunks(da), chunks(bm), chunks(cm)))
    return jnp.moveaxis(y, 0, 1).reshape(b, s, g, e, p)


def mamba2_mixer(h, w_in, conv_w, conv_b, dt_bias, a_log, d_skip, gn_w, w_out):
    b, s, _ = h.shape
    f32 = jnp.float32
    proj = h @ w_in
    z, xbc, dt = jnp.split(proj, [SSM_INNER, SSM_INNER + SSM_CONV_DIM], axis=-1)
    xbc = jax.nn.silu(causal_depthwise_conv(xbc, conv_w, conv_b))
    xs, bm, cm = jnp.split(xbc, [SSM_INNER, SSM_INNER + SSM_GROUPS * SSM_STATE], axis=-1)
    xs = xs.astype(f32).reshape(b, s, SSM_GROUPS, SSM_HPG, SSM_HEAD_DIM)
    bm = bm.astype(f32).reshape(b, s, SSM_GROUPS, SSM_STATE)
    cm = cm.astype(f32).reshape(b, s, SSM_GROUPS, SSM_STATE)
    dt = jax.nn.softplus(dt.astype(f32) + dt_bias.astype(f32)).reshape(b, s, SSM_GROUPS, SSM_HPG)
    a = -jnp.exp(a_log.astype(f32)).reshape(SSM_GROUPS, SSM_HPG)
    y = ssd_chunked(xs, dt, a, bm, cm) + xs * d_skip.astype(f32).reshape(SSM_GROUPS, SSM_HPG)[..., None]
    y = y.reshape(b, s, SSM_INNER) * jax.nn.silu(z.astype(f32))
    y = _normalize(y.reshape(b, s, SSM_GROUPS, SSM_INNER // SSM_GROUPS)).reshape(b, s, SSM_INNER)
    y = y * gn_w.astype(f32)
    return y.astype(h.dtype) @ w_out


def memory_cross_attention(h, mem_n, wq, wk, wv, wo):
    b, s, _ = h.shape
    m = mem_n.shape[1]
    q = (h @ wq).reshape(b, s, XA_HEADS, XA_HEAD_DIM)
    k = (mem_n @ wk).reshape(b, m, XA_HEADS, XA_HEAD_DIM)
    v = (mem_n @ wv).reshape(b, m, XA_HEADS, XA_HEAD_DIM)
    logits = jnp.einsum('bqhd,bkhd->bhqk', q, k).astype(jnp.float32) * (XA_HEAD_DIM ** -0.5)
    p = jax.nn.softmax(logits, axis=-1).astype(v.dtype)
    o = jnp.einsum('bhqk,bkhd->bqhd', p, v).reshape(b, s, XA_WIDTH)
    return o @ wo


def swiglu(h, wg, wu, wd):
    return (jax.nn.silu(h @ wg) * (h @ wu)) @ wd


def setup_inputs(seed: int = 0) -> dict:
    key = jax.random.key(seed)
    ks = iter(jax.random.split(key, 48))
    f32 = jnp.float32

    def dense(shape, fan_in):
        return jax.random.normal(next(ks), shape, f32) * (fan_in ** -0.5)

    def gain(shape):
        return 1.0 + 0.02 * jax.random.normal(next(ks), shape, f32)

    def small(shape):
        return 0.01 * jax.random.normal(next(ks), shape, f32)

    x = jax.random.normal(next(ks), (BATCH, SEQ, D_MODEL), f32)
    mem = jax.random.normal(next(ks), (BATCH, MEM_LEN, D_MODEL), f32)
    mem_norm = gain((D_MODEL,))
    ev_mix_norm = gain((N_EVEN, D_MODEL))
    ev_w_in = dense((N_EVEN, D_MODEL, EVEN_IN), D_MODEL)
    ev_b_f = jax.random.uniform(next(ks), (N_EVEN, FOX_HEADS), f32, 1.0, 6.0)
    ev_w_out = dense((N_EVEN, EVEN_MIX, D_MODEL), EVEN_MIX)
    od_mix_norm = gain((N_ODD, D_MODEL))
    od_w_in = dense((N_ODD, D_MODEL, ODD_IN), D_MODEL)
    od_conv_w = dense((N_ODD, SSM_CONV, SSM_CONV_DIM), SSM_CONV)
    od_conv_b = small((N_ODD, SSM_CONV_DIM))
    u = jax.random.uniform(next(ks), (N_ODD, SSM_HEADS), f32)
    dt0 = jnp.exp(u * (math.log(0.1) - math.log(0.001)) + math.log(0.001))
    od_dt_bias = dt0 + jnp.log(-jnp.expm1(-dt0))
    od_a_log = jnp.log(jax.random.uniform(next(ks), (N_ODD, SSM_HEADS), f32, 1.0, 16.0))
    od_d_skip = gain((N_ODD, SSM_HEADS))
    od_gn_w = gain((N_ODD, SSM_INNER))
    od_w_out = dense((N_ODD, SSM_INNER, D_MODEL), SSM_INNER)
    xa_norm = gain((DEPTH, D_MODEL))
    xa_wq = dense((DEPTH, D_MODEL, XA_WIDTH), D_MODEL)
    xa_wk = dense((DEPTH, D_MODEL, XA_WIDTH), D_MODEL)
    xa_wv = dense((DEPTH, D_MODEL, XA_WIDTH), D_MODEL)
    xa_wo = dense((DEPTH, XA_WIDTH, D_MODEL), XA_WIDTH)
    ffn_norm = gain((DEPTH, D_MODEL))
    ffn_w_gate = dense((DEPTH, D_MODEL, D_FF), D_MODEL)
    ffn_w_up = dense((DEPTH, D_MODEL, D_FF), D_MODEL)
    ffn_w_down = dense((DEPTH, D_FF, D_MODEL), D_FF)
    final_norm = gain((D_MODEL,))
    return {"x": x, "mem": mem, "mem_norm": mem_norm,
            "ev_mix_norm": ev_mix_norm, "ev_w_in": ev_w_in, "ev_b_f": ev_b_f, "ev_w_out": ev_w_out,
            "od_mix_norm": od_mix_norm, "od_w_in": od_w_in, "od_conv_w": od_conv_w, "od_conv_b": od_conv_b,
            "od_dt_bias": od_dt_bias, "od_a_log": od_a_log, "od_d_skip": od_d_skip, "od_gn_w": od_gn_w,
            "od_w_out": od_w_out,
            "xa_norm": xa_norm, "xa_wq": xa_wq, "xa_wk": xa_wk, "xa_wv": xa_wv, "xa_wo": xa_wo,
            "ffn_norm": ffn_norm, "ffn_w_gate": ffn_w_gate, "ffn_w_up": ffn_w_up, "ffn_w_down": ffn_w_down,
            "final_norm": final_norm}


def reference(x, mem, mem_norm,
              ev_mix_norm, ev_w_in, ev_b_f, ev_w_out,
              od_mix_norm, od_w_in, od_conv_w, od_conv_b, od_dt_bias, od_a_log, od_d_skip, od_gn_w, od_w_out,
              xa_norm, xa_wq, xa_wk, xa_wv, xa_wo,
              ffn_norm, ffn_w_gate, ffn_w_up, ffn_w_down,
              final_norm):
    mem_n = rms_norm(mem, mem_norm)
    for layer in range(DEPTH):
        i = layer // 2
        if layer % 2 == 0:
            h = rms_norm(x, ev_mix_norm[i])
            delta = retention_fox_mixer(h, ev_w_in[i], ev_b_f[i], ev_w_out[i])
        else:
            h = rms_norm(x, od_mix_norm[i])
            delta = mamba2_mixer(h, od_w_in[i], od_conv_w[i], od_conv_b[i], od_dt_bias[i],
                                 od_a_log[i], od_d_skip[i], od_gn_w[i], od_w_out[i])
        x = x + delta.astype(x.dtype)
        h = rms_norm(x, xa_norm[layer])
        x = x + memory_cross_attention(h, mem_n, xa_wq[layer], xa_wk[layer], xa_wv[layer], xa_wo[layer]).astype(x.dtype)
        h = rms_norm(x, ffn_norm[layer])
        x = x + swiglu(h, ffn_w_gate[layer], ffn_w_up[layer], ffn_w_down[layer]).astype(x.dtype)
    return rms_norm(x, final_norm)
```

```python
import numpy as np
import ml_dtypes
import concourse.bass as bass
import concourse.mybir as mybir
from concourse.bass_utils import run_bass_kernel_spmd

F32 = mybir.dt.float32
BF16 = mybir.dt.bfloat16
U8 = mybir.dt.uint8
AF = mybir.ActivationFunctionType
ALU = mybir.AluOpType

NCORE = 8
D = 4096
KD = D // 128
EPS = 1e-6
NSLOT = 8
KC = 8
DFF = 11008
XAW = 512
W0C = 1794
W1C = 2320
ARENA = 204 * 1024
RG = [list(range(NCORE))]


class Trk:
    __slots__ = ("w", "r", "multi", "ws")

    def __init__(self, multi=False):
        self.w = None
        self.r = {}
        self.multi = multi
        self.ws = {}


class Op:
    __slots__ = ("fn", "waits", "inc", "dma")

    def __init__(self, fn, waits, dma=None):
        self.fn = fn
        self.waits = waits
        self.inc = False
        self.dma = dma


class Prog:
    ENGS = ("pe", "act", "dve", "pool", "sp")

    def __init__(self, nc):
        self.nc = nc
        self.ops = {e: [] for e in self.ENGS}
        self.seen = {e: {} for e in self.ENGS}
        self.ndma = {"sp": 0, "pool": 0}
        self.ncc = 0
        self.eng_obj = {"pe": nc.tensor, "act": nc.scalar, "dve": nc.vector,
                        "pool": nc.gpsimd, "sp": nc.sync}

    def _deps(self, eng, reads, writes):
        deps = {}

        def add(c, i):
            if deps.get(c, -1) < i:
                deps[c] = i
        for t in reads:
            if t.multi:
                for c, i in t.ws.items():
                    add(c, i)
            if t.w is not None:
                add(*t.w)
        for t in writes:
            if t.multi:
                continue
            if t.w is not None:
                add(*t.w)
            for c, i in t.r.items():
                add(c, i)
        waits = []
        seen = self.seen[eng]
        for c, i in deps.items():
            if c == "pe" and eng == "pe":
                continue
            if seen.get(c, -1) >= i:
                continue
            seen[c] = i
            waits.append((c, i))
            if not isinstance(c, tuple):
                self.ops[c][i].inc = True
        return waits

    def op(self, eng, fn, reads=(), writes=()):
        waits = self._deps(eng, reads, writes)
        idx = len(self.ops[eng])
        self.ops[eng].append(Op(fn, waits))
        for t in reads:
            t.r[eng] = idx
        for t in writes:
            t.w = (eng, idx)
            t.r = {}
            t.ws = {}

    def dma(self, q, fn, reads=(), writes=()):
        n = self.ndma[q]
        self.ndma[q] = n + 1
        chan = (q, n % NSLOT)
        k = n // NSLOT
        waits = self._deps(q, reads, writes)
        if k > 0:
            seen = self.seen[q]
            if seen.get(chan, -1) < k - 1:
                seen[chan] = k - 1
                waits.append((chan, k - 1))
        self.ops[q].append(Op(fn, waits, dma=(chan, 16)))
        for t in reads:
            t.r[chan] = k
        for t in writes:
            if t.multi:
                t.ws[chan] = k
            else:
                t.w = (chan, k)
                t.r = {}

    def coll(self, fn, reads=(), writes=()):
        chan = ("cc", self.ncc)
        self.ncc += 1
        waits = self._deps("pool", reads, writes)
        self.ops["pool"].append(Op(fn, waits, dma=(chan, 1)))
        for t in reads:
            t.r[chan] = 0
        for t in writes:
            t.w = (chan, 0)
            t.r = {}
            t.ws = {}

    def drain(self, q):
        n = self.ndma[q]
        waits = []
        seen = self.seen[q]
        for i in range(max(0, n - NSLOT), n):
            chan = (q, i % NSLOT)
            k = i // NSLOT
            if seen.get(chan, -1) < k:
                seen[chan] = k
                waits.append((chan, k))
        self.ops[q].append(Op(None, waits))

    def emit(self, marks):
        nc = self.nc
        import contextlib
        with contextlib.ExitStack() as st:
            esem = {e: st.enter_context(nc.semaphore("s_" + e)) for e in self.ENGS}
            dsem = {}
            for q in self.ndma:
                for s in range(NSLOT):
                    dsem[(q, s)] = st.enter_context(nc.semaphore("d_%s%d" % (q, s)))
            for i in range(self.ncc):
                dsem[("cc", i)] = st.enter_context(nc.semaphore("cc%d" % i))
            cum = {}
            for e in self.ENGS:
                c = 0
                arr = []
                for o in self.ops[e]:
                    if o.inc and o.dma is None:
                        c += 1
                    arr.append(c)
                cum[e] = arr

            def resolve(c, i):
                if isinstance(c, tuple):
                    if c[0] == "cc":
                        return dsem[c], 1
                    return dsem[c], 16 * (i + 1)
                return esem[c], cum[c][i]

            block = st.enter_context(nc.Block())

            def run(e):
                eo = self.eng_obj[e]
                for o in self.ops[e]:
                    for (c, i) in o.waits:
                        s, v = resolve(c, i)
                        eo.wait_ge(s, v)
                    if o.fn is None:
                        continue
                    ins = o.fn()
                    if o.dma is not None:
                        ins.then_inc(dsem[o.dma[0]], o.dma[1])
                    elif o.inc:
                        ins.then_inc(esem[e], 1)

            @block.tensor
            def _(e):
                run("pe")

            @block.scalar
            def _(e):
                run("act")

            @block.vector
            def _(e):
                run("dve")

            @block.gpsimd
            def _(e):
                run("pool")

            @block.sync
            def _(e):
                run("sp")


class Buf:
    __slots__ = ("t", "k")

    def __init__(self, t, k=None):
        self.t = t
        self.k = k or Trk()


class Ring:
    def __init__(self, bufs):
        self.bufs = bufs
        self.i = 0

    def get(self):
        b = self.bufs[self.i % len(self.bufs)]
        self.i += 1
        return b


class Cx:
    def __init__(self, nc):
        self.nc = nc
        self.P = Prog(nc)
        self.allbanks = [Buf(nc.alloc_psum_tensor("bank%d" % i, [128, 512], F32)) for i in range(8)]
        self.banks = Ring(self.allbanks)
        self.arena = nc.alloc_sbuf_tensor("arena", [128, ARENA], U8)
        self.off = 0
        self.mk = {e: Buf(nc.alloc_sbuf_tensor("mk_" + e, [128, 8], F32)) for e in ("act", "dve", "pool")}
        self.nin = 0

    def sb(self, shape, dt, name=None):
        esz = 4 if dt == F32 else 2
        n = 1
        for s in shape[1:]:
            n *= s
        nbytes = (n * esz + 63) // 64 * 64
        assert self.off + nbytes <= ARENA, ("SBUF arena overflow", name, self.off, nbytes)
        v = self.arena[0:shape[0], self.off:self.off + n * esz].bitcast(dt)
        self.off += nbytes
        if len(shape) == 3:
            v = v.rearrange("p (a b) -> p a b", b=shape[2])
        return Buf(v)

    def ring(self, n, shape, dt, name=None):
        return Ring([self.sb(shape, dt, name) for _ in range(n)])

    def dram_in(self, name, shape, dt):
        return self.nc.dram_tensor(name, list(shape), dt, kind="ExternalInput").ap()

    def dram_out(self, name, shape, dt):
        return self.nc.dram_tensor(name, list(shape), dt, kind="ExternalOutput").ap()

    def dram(self, name, shape, dt):
        return self.nc.dram_tensor(name, list(shape), dt, kind="Internal").ap()

    def load(self, buf, src, q="sp"):
        nc = self.nc
        if q == "sp":
            self.P.dma("sp", lambda: nc.sync.dma_start(out=buf.t, in_=src), writes=[buf.k])
        else:
            self.P.dma("pool", lambda: nc.gpsimd.dma_start(out=buf.t, in_=src), writes=[buf.k])
        return buf

    def phase_barrier(self):
        nc, P = self.nc, self.P
        P.drain("sp")
        P.drain("pool")
        allt = [b.k for b in self.allbanks] + [b.k for b in self.mk.values()]
        t_sp, t_pe = Trk(), Trk()
        bk = self.allbanks[0]
        P.op("pe", lambda: nc.tensor.matmul(bk.t[0:8, 0:8], self.mk["dve"].t[:, 0:8], self.mk["dve"].t[:, 0:8],
                                             start=True, stop=True), reads=[self.mk["dve"].k], writes=[bk.k, t_pe])
        for e in ("act", "dve", "pool"):
            m = self.mk[e]
            eo = {"act": nc.scalar, "dve": nc.vector, "pool": nc.gpsimd}[e]
            if e == "act":
                P.op(e, lambda m=m: nc.scalar.copy(m.t[:, 0:4], m.t[:, 4:8]), writes=[m.k])
            else:
                P.op(e, lambda m=m, eo=eo: eo.memset(m.t[:, 0:4], 0.0), writes=[m.k])
        P.op("sp", lambda: nc.sync.nop(), writes=[t_sp])
        every = [self.mk["act"].k, self.mk["dve"].k, self.mk["pool"].k, t_sp, t_pe]
        P.op("pe", lambda: nc.tensor.matmul(bk.t[0:8, 0:8], self.mk["dve"].t[:, 0:8], self.mk["dve"].t[:, 0:8],
                                             start=True, stop=True), reads=every, writes=[])
        P.op("act", lambda: nc.scalar.copy(self.mk["act"].t[:, 0:4], self.mk["act"].t[:, 4:8]), reads=every)
        P.op("dve", lambda: nc.vector.memset(self.mk["dve"].t[:, 0:4], 0.0), reads=every)
        P.op("pool", lambda: nc.gpsimd.memset(self.mk["pool"].t[:, 0:4], 0.0), reads=every)
        P.op("sp", lambda: nc.sync.nop(), reads=every)
        self.off = 0
        self.banks = Ring(self.allbanks)


def rstd_from_ss(cx, st):
    nc, P = cx.nc, cx.P
    P.op("act", lambda: nc.scalar.activation(out=st.t[:, 1:2], in_=st.t[:, 0:1], func=AF.Ln, bias=EPS),
         reads=[st.k], writes=[st.k])
    P.op("act", lambda: nc.scalar.activation(out=st.t[:, 2:3], in_=st.t[:, 1:2], func=AF.Exp, scale=-0.5),
         reads=[st.k], writes=[st.k])


def norm_rows(cx, res, src_tiles, gbc, hT):
    nc, P = cx.nc, cx.P
    for tt, src in enumerate(src_tiles):
        xt = res["xt"].get()
        xn = res["xn"].get()
        st = res["st"].get()
        cx.load(xt, src)
        P.op("dve", lambda st=st: nc.vector.memset(st.t[:, :], 0.0), writes=[st.k])
        P.op("act", lambda xt=xt, xn=xn, st=st: nc.scalar.activation(
            out=xn.t[:, :], in_=xt.t[:, :], func=AF.Square, scale=float(D) ** -0.5,
            accum_out=st.t[:, 0:1]), reads=[xt.k, st.k], writes=[xn.k, st.k])
        rstd_from_ss(cx, st)
        P.op("dve", lambda xt=xt, xn=xn, st=st: nc.vector.scalar_tensor_tensor(
            out=xn.t[:, :], in0=xt.t[:, :], scalar=st.t[:, 2:3], in1=gbc.t[:, :],
            op0=ALU.mult, op1=ALU.mult), reads=[xt.k, st.k, gbc.k], writes=[xn.k])
        transpose_rows(cx, res, xn, KD, hT, tt)


def transpose_rows(cx, res, xn, kch, hT, tt, k0=0):
    nc, P = cx.nc, cx.P
    ident = res["ident"]
    for g0 in range(0, kch, 8):
        ng = min(8, kch - g0)
        bank = cx.banks.get()
        pv = bank.t[:, :].bitcast(BF16)
        for i in range(ng):
            kc = g0 + i
            P.op("pe", lambda pv=pv, xn=xn, i=i, kc=kc: nc.tensor.transpose(
                pv[:, i * 128:(i + 1) * 128], xn.t[:, kc * 128:(kc + 1) * 128], ident.t[:, :]),
                reads=[xn.k, ident.k], writes=[bank.k])
        src = pv[:, 0:ng * 128].rearrange("p (c t) -> p c t", t=128)
        dst = hT.t[:, k0 + g0:k0 + g0 + ng, tt * 128:(tt + 1) * 128]
        if (g0 // 8) % 2 == 0:
            P.op("act", lambda src=src, dst=dst: nc.scalar.copy(dst, src), reads=[bank.k], writes=[hT.k])
        else:
            P.op("dve", lambda src=src, dst=dst: nc.vector.tensor_copy(dst, src), reads=[bank.k], writes=[hT.k])


def load_w(cx, wt, wdram, wk, k0, nk, col0, ncols, dst_col=0):
    nc, P = cx.nc, cx.P
    src = wdram[k0 * 128:(k0 + nk) * 128, col0:col0 + ncols].rearrange("(c p) n -> p c n", p=128)
    P.dma("sp", lambda: nc.sync.dma_start(out=wt.t[:, 0:nk, dst_col:dst_col + ncols], in_=src),
          reads=[wk], writes=[wt.k])


def gemm(cx, res, mode, A, kch, ntok, ncols, wload, a_off=0):
    nc, P = cx.nc, cx.P
    nb = ntok // 128 if mode == "T" else (ncols + 127) // 128
    banks = [cx.banks.get() for _ in range(nb)]
    for k0 in range(0, kch, KC):
        nk = min(KC, kch - k0)
        wt = res["wt"].get()
        wload(wt, k0, nk)
        for b in range(nb):
            for k in range(nk):
                kk = k0 + k
                if mode == "T":
                    lhsT = A.t[:, a_off + kk, b * 128:(b + 1) * 128]
                    rhs = wt.t[:, k, 0:ncols]
                    out = banks[b].t[:, 0:ncols]
                else:
                    m = min(128, ncols - b * 128)
                    lhsT = wt.t[:, k, b * 128:b * 128 + m]
                    rhs = A.t[:, a_off + kk, 0:ntok]
                    out = banks[b].t[0:m, 0:ntok]
                P.op("pe", lambda out=out, lhsT=lhsT, rhs=rhs, s=(kk == 0), e=(kk == kch - 1):
                     nc.tensor.matmul(out, lhsT, rhs, start=s, stop=e),
                     reads=[A.k, wt.k], writes=[banks[b].k])
    return banks


def resid_out(cx, res, banks, x_src, ksrc, x_dst, kdst, row0, col0, ncols):
    nc, P = cx.nc, cx.P
    for tt, bank in enumerate(banks):
        xb = res["xblk"].get()
        r0 = row0 + tt * 128
        src = x_src[r0:r0 + 128, col0:col0 + ncols]
        dst = x_dst[r0:r0 + 128, col0:col0 + ncols]
        P.dma("sp", lambda xb=xb, src=src: nc.sync.dma_start(out=xb.t[:, 0:ncols], in_=src),
              reads=[ksrc], writes=[xb.k])
        P.op("dve", lambda xb=xb, bank=bank: nc.vector.tensor_tensor(
            out=xb.t[:, 0:ncols], in0=bank.t[:, 0:ncols], in1=xb.t[:, 0:ncols], op=ALU.add),
            reads=[bank.k, xb.k], writes=[xb.k])
        P.dma("pool", lambda xb=xb, dst=dst: nc.gpsimd.dma_start(out=dst, in_=xb.t[:, 0:ncols]),
              reads=[xb.k], writes=[kdst])


def common_res(cx, K):
    res = {}
    res["ident"] = cx.load(cx.sb([128, 128], BF16), K["ident"])
    res["xt"] = cx.ring(1, [128, D], F32, "xt")
    res["xn"] = cx.ring(2, [128, D], BF16, "xn")
    res["st"] = cx.ring(4, [128, 4], F32, "st")
    res["wt"] = cx.ring(4, [128, KC, 512], BF16, "wt")
    res["xblk"] = cx.ring(4, [128, 512], F32, "xblk")
    res["xt2"] = cx.ring(2, [128, 2048], F32, "xt2")
    return res


def wprep(cx, stg, src, dst, kdst):
    nc, P = cx.nc, cx.P
    R, N = src.shape
    for r0 in range(0, R, 128):
        nr = min(128, R - r0)
        for c0 in range(0, N, 4096):
            ncl = min(4096, N - c0)
            b = stg.get()
            s_ap = src[r0:r0 + nr, c0:c0 + ncl]
            d_ap = dst[r0:r0 + nr, c0:c0 + ncl]
            P.dma("pool", lambda b=b, s_ap=s_ap, nr=nr, ncl=ncl: nc.gpsimd.dma_start(out=b.t[0:nr, 0:ncl], in_=s_ap),
                  writes=[b.k])
            P.dma("sp", lambda b=b, d_ap=d_ap, nr=nr, ncl=ncl: nc.sync.dma_start(out=d_ap, in_=b.t[0:nr, 0:ncl]),
                  reads=[b.k], writes=[kdst])


def allgather(cx, src, ksrc, dst, kdst):
    nc, P = cx.nc, cx.P
    P.coll(lambda: nc.gpsimd.collective_compute("AllGather", ALU.bypass, replica_groups=RG, ins=[src], outs=[dst]),
           reads=[ksrc], writes=[kdst])


def stage_norm_gather(cx, K, x_src, kx, gvec, hs, khs, hg, khg, TOK, G, gather=True):
    nc, P = cx.nc, cx.P
    res = common_res(cx, K)
    gbc = cx.sb([128, D], F32)
    cx.load(gbc, gvec.partition_broadcast(128))
    hTr = cx.ring(2, [128, KD, G], BF16)
    for g in range(TOK // G):
        hT = hTr.get()
        tiles = [x_src[g * G + tt * 128:g * G + (tt + 1) * 128, :] for tt in range(G // 128)]
        norm_rows_k(cx, res, tiles, kx, gbc, hT)
        dst = hs[:, g * G:(g + 1) * G].rearrange("(c p) t -> p c t", p=128)
        P.dma("sp", lambda hT=hT, dst=dst: nc.sync.dma_start(out=dst, in_=hT.t[:, :, :]), reads=[hT.k], writes=[khs])
    if gather:
        allgather(cx, hs, khs, hg, khg)


def norm_rows_k(cx, res, tiles, kx, gbc, hT):
    nc, P = cx.nc, cx.P
    for tt, src in enumerate(tiles):
        xt = res["xt"].get()
        xn = res["xn"].get()
        st = res["st"].get()
        P.dma("sp", lambda xt=xt, src=src: nc.sync.dma_start(out=xt.t[:, :], in_=src), reads=[kx], writes=[xt.k])
        P.op("dve", lambda st=st: nc.vector.memset(st.t[:, :], 0.0), writes=[st.k])
        P.op("act", lambda xt=xt, xn=xn, st=st: nc.scalar.activation(
            out=xn.t[:, :], in_=xt.t[:, :], func=AF.Square, scale=float(D) ** -0.5,
            accum_out=st.t[:, 0:1]), reads=[xt.k, st.k], writes=[xn.k, st.k])
        rstd_from_ss(cx, st)
        P.op("dve", lambda xt=xt, xn=xn, st=st: nc.vector.scalar_tensor_tensor(
            out=xn.t[:, :], in0=xt.t[:, :], scalar=st.t[:, 2:3], in1=gbc.t[:, :],
            op0=ALU.mult, op1=ALU.mult), reads=[xt.k, st.k, gbc.k], writes=[xn.k])
        transpose_rows(cx, res, xn, KD, hT, tt)


def seg_map(S, TOK, G, r, gg, CW):
    if CW is None:
        CW = TOK

    def gpos(l):
        return (l // CW) * (NCORE * CW) + r * CW + (l % CW)
    L = min(G, CW)
    segs = [(ls, gpos(gg * G + ls), L) for ls in range(0, G, L)]
    return segs, (lambda tt: gpos(gg * G + tt * 128))


def stage_a0(cx, K, S, TOK, G, hg, khg, w0b, kw0, dd, CW=None):
    nc, P = cx.nc, cx.P
    res = common_res(cx, K)
    hTr = cx.ring(2, [128, KD, G], BF16)
    csr = cx.ring(2, [128, G], F32)
    snr = cx.ring(2, [128, G], F32)
    tmp = cx.ring(4, [128, G], F32)
    ob = cx.ring(4, [128, G], BF16)
    ot = cx.ring(3, [128, 512], BF16)
    of = cx.ring(2, [128, G], F32)
    NTT = G // 128
    kd = dd["k"]
    for r in range(NCORE):
        for gg in range(TOK // G):
            segs, gp = seg_map(S, TOK, G, r, gg, CW)
            hT = hTr.get()
            src = hg[r * D:(r + 1) * D, gg * G:(gg + 1) * G].rearrange("(c p) t -> p c t", p=128)
            P.dma("sp", lambda hT=hT, src=src: nc.sync.dma_start(out=hT.t[:, :, :], in_=src), reads=[khg], writes=[hT.k])
            cs, sn = csr.get(), snr.get()
            for (ls, gs, L) in segs:
                P.dma("sp", lambda cs=cs, ls=ls, gs=gs, L=L: nc.sync.dma_start(out=cs.t[:, ls:ls + L], in_=K["cosT"][:, gs:gs + L]), writes=[cs.k])
                P.dma("sp", lambda sn=sn, ls=ls, gs=gs, L=L: nc.sync.dma_start(out=sn.t[:, ls:ls + L], in_=K["sinT"][:, gs:gs + L]), writes=[sn.k])
            bk = gemm(cx, res, "F", hT, KD, G, 512, lambda wt, k0, nk: load_w(cx, wt, w0b, kw0, k0, nk, 0, 512))
            for qi, (dst, sc) in enumerate(((dd["rqT"], 1.0), (dd["rkT"], 1.0 / 16))):
                b1, b2 = bk[2 * qi], bk[2 * qi + 1]
                t1, t2, t3, t4 = tmp.get(), tmp.get(), tmp.get(), tmp.get()
                for (tb, bb, tab) in ((t1, b1, cs), (t2, b2, sn), (t3, b1, sn), (t4, b2, cs)):
                    P.op("dve", lambda tb=tb, bb=bb, tab=tab, sc=sc: nc.vector.scalar_tensor_tensor(
                        out=tb.t[:, :], in0=bb.t[:, 0:G], scalar=sc, in1=tab.t[:, :], op0=ALU.mult, op1=ALU.mult),
                        reads=[bb.k, tab.k], writes=[tb.k])
                o1, o2 = ob.get(), ob.get()
                P.op("pool", lambda o1=o1, t1=t1, t2=t2: nc.gpsimd.tensor_tensor(
                    out=o1.t[:, :], in0=t1.t[:, :], in1=t2.t[:, :], op=ALU.subtract), reads=[t1.k, t2.k], writes=[o1.k])
                P.op("pool", lambda o2=o2, t3=t3, t4=t4: nc.gpsimd.tensor_tensor(
                    out=o2.t[:, :], in0=t3.t[:, :], in1=t4.t[:, :], op=ALU.add), reads=[t3.k, t4.k], writes=[o2.k])
                for hh, oo in ((0, o1), (1, o2)):
                    for (ls, gs, L) in segs:
                        d_ap = dst[hh * 128:(hh + 1) * 128, gs:gs + L]
                        P.dma("pool", lambda oo=oo, d_ap=d_ap, ls=ls, L=L: nc.gpsimd.dma_start(out=d_ap, in_=oo.t[:, ls:ls + L]), reads=[oo.k], writes=[kd])
            bk = gemm(cx, res, "T", hT, KD, G, 512, lambda wt, k0, nk: load_w(cx, wt, w0b, kw0, k0, nk, 512, 512))
            for tt in range(NTT):
                o = ot.get()
                P.op("act", lambda o=o, b=bk[tt]: nc.scalar.copy(o.t[:, :], b.t[:, :]), reads=[bk[tt].k], writes=[o.k])
                rows = slice(gp(tt), gp(tt) + 128)
                P.dma("pool", lambda o=o, rows=rows: nc.gpsimd.dma_start(out=dd["rv"][rows, :], in_=o.t[:, 0:256]), reads=[o.k], writes=[kd])
                P.dma("pool", lambda o=o, rows=rows: nc.gpsimd.dma_start(out=dd["rg"][rows, :], in_=o.t[:, 256:512]), reads=[o.k], writes=[kd])
            bk = gemm(cx, res, "F", hT, KD, G, 512, lambda wt, k0, nk: load_w(cx, wt, w0b, kw0, k0, nk, 1024, 512))
            for i in range(4):
                o = ob.get()
                P.op("act", lambda o=o, b=bk[i]: nc.scalar.copy(o.t[:, :], b.t[:, 0:G]), reads=[bk[i].k], writes=[o.k])
                for (ls, gs, L) in segs:
                    dst = (dd["fqT"] if i < 2 else dd["fkT"])[i % 2, :, gs:gs + L]
                    P.dma("pool", lambda o=o, dst=dst, ls=ls, L=L: nc.gpsimd.dma_start(out=dst, in_=o.t[:, ls:ls + L]), reads=[o.k], writes=[kd])
            bk = gemm(cx, res, "T", hT, KD, G, 256, lambda wt, k0, nk: load_w(cx, wt, w0b, kw0, k0, nk, 1536, 256))
            for tt in range(NTT):
                o = ot.get()
                P.op("act", lambda o=o, b=bk[tt]: nc.scalar.copy(o.t[:, 0:256], b.t[:, 0:256]), reads=[bk[tt].k], writes=[o.k])
                rows = slice(gp(tt), gp(tt) + 128)
                for hd in range(2):
                    P.dma("pool", lambda o=o, rows=rows, hd=hd: nc.gpsimd.dma_start(
                        out=dd["fv"][hd, rows, :], in_=o.t[:, hd * 128:(hd + 1) * 128]), reads=[o.k], writes=[kd])
            bk = gemm(cx, res, "F", hT, KD, G, 2, lambda wt, k0, nk: load_w(cx, wt, w0b, kw0, k0, nk, 1792, 2))
            o = of.get()
            P.op("act", lambda o=o, b=bk[0]: nc.scalar.copy(o.t[0:2, :], b.t[0:2, 0:G]), reads=[bk[0].k], writes=[o.k])
            for (ls, gs, L) in segs:
                P.dma("pool", lambda o=o, ls=ls, gs=gs, L=L: nc.gpsimd.dma_start(out=dd["ffT"][:, gs:gs + L], in_=o.t[0:2, ls:ls + L]), reads=[o.k], writes=[kd])


def stage_m0(cx, K, S, dd, mix, kmix):
    nc, P = cx.nc, cx.P
    kd = dd["k"]
    ident = cx.load(cx.sb([128, 128], BF16), K["ident"])
    identf = cx.load(cx.sb([128, 128], F32), K["identf"])
    decT = cx.load(cx.sb([128, 128], F32), K["ret_decT"])
    qdec = cx.load(cx.sb([128, 128], F32), K["ret_qdec"])
    rsc = cx.load(cx.sb([128, 2], F32), K["ret_sc"])
    stf = cx.sb([128, 512], F32)
    stb = cx.sb([128, 512], BF16)
    P.op("dve", lambda: nc.vector.memset(stf.t[:, :], 0.0), writes=[stf.k])
    P.op("dve", lambda: nc.vector.memset(stb.t[:, :], 0.0), writes=[stb.k])
    qTr = cx.ring(2, [128, 2, 512], BF16)
    kTr = cx.ring(2, [128, 2, 512], BF16)
    vr = cx.ring(2, [128, 4, 256], BF16)
    gr = cx.ring(2, [128, 4, 256], BF16)
    sgr = cx.ring(2, [128, 4, 256], F32)
    kdr = cx.ring(2, [128, 256], BF16)
    pmr = cx.ring(2, [128, 128], BF16)
    qdr = cx.ring(2, [128, 2, 128], BF16)
    jk = cx.ring(2, [128, 256], F32)
    str_ = cx.ring(4, [128, 4], F32)
    osr = cx.ring(2, [128, 4, 256], BF16)
    for sc in range(S // 512):
        t0 = sc * 512
        qT, kT, v, g, sg = qTr.get(), kTr.get(), vr.get(), gr.get(), sgr.get()
        P.dma("sp", lambda qT=qT, t0=t0: nc.sync.dma_start(
            out=qT.t[:, :, :], in_=dd["rqT"][:, t0:t0 + 512].rearrange("(c p) t -> p c t", p=128)), reads=[kd], writes=[qT.k])
        P.dma("sp", lambda kT=kT, t0=t0: nc.sync.dma_start(
            out=kT.t[:, :, :], in_=dd["rkT"][:, t0:t0 + 512].rearrange("(c p) t -> p c t", p=128)), reads=[kd], writes=[kT.k])
        P.dma("sp", lambda v=v, t0=t0: nc.sync.dma_start(
            out=v.t[:, :, :], in_=dd["rv"][t0:t0 + 512, :].rearrange("(c p) e -> p c e", p=128)), reads=[kd], writes=[v.k])
        P.dma("sp", lambda g=g, t0=t0: nc.sync.dma_start(
            out=g.t[:, :, :], in_=dd["rg"][t0:t0 + 512, :].rearrange("(c p) e -> p c e", p=128)), reads=[kd], writes=[g.k])
        P.op("act", lambda sg=sg, g=g: nc.scalar.activation(out=sg.t[:, :, :], in_=g.t[:, :, :], func=AF.Silu),
             reads=[g.k], writes=[sg.k])
        os_ = osr.get()
        for c in range(4):
            cols = slice(c * 128, (c + 1) * 128)
            bt = cx.banks.get()
            pv = bt.t[:, :].bitcast(BF16)
            for dc in range(2):
                P.op("pe", lambda pv=pv, kT=kT, dc=dc, cols=cols: nc.tensor.transpose(
                    pv[:, dc * 128:(dc + 1) * 128], kT.t[:, dc, cols], ident.t[:, :]), reads=[kT.k, ident.k], writes=[bt.k])
            kdec = kdr.get()
            P.op("act", lambda kdec=kdec, pv=pv: nc.scalar.activation(
                out=kdec.t[:, :], in_=pv[:, 0:256], func=AF.Copy, scale=rsc.t[:, 0:1]), reads=[bt.k, rsc.k], writes=[kdec.k])
            bs = cx.banks.get()
            for dc in range(2):
                P.op("pe", lambda bs=bs, kT=kT, qT=qT, dc=dc, cols=cols: nc.tensor.matmul(
                    bs.t[:, 0:128], kT.t[:, dc, cols], qT.t[:, dc, cols], start=(dc == 0), stop=(dc == 1)),
                    reads=[kT.k, qT.k], writes=[bs.k])
            pm = pmr.get()
            P.op("dve", lambda pm=pm, bs=bs: nc.vector.tensor_tensor(
                out=pm.t[:, :], in0=bs.t[:, 0:128], in1=decT.t[:, :], op=ALU.mult), reads=[bs.k, decT.k], writes=[pm.k])
            qd = qdr.get()
            for dc in range(2):
                P.op("pool", lambda qd=qd, qT=qT, dc=dc, cols=cols: nc.gpsimd.tensor_tensor(
                    out=qd.t[:, dc, :], in0=qT.t[:, dc, cols], in1=qdec.t[:, :], op=ALU.mult), reads=[qT.k, qdec.k], writes=[qd.k])
            bo = cx.banks.get()
            P.op("pe", lambda bo=bo, pm=pm, v=v, c=c: nc.tensor.matmul(
                bo.t[:, 0:256], pm.t[:, :], v.t[:, c, :], start=True, stop=False), reads=[pm.k, v.k], writes=[bo.k])
            for dc in range(2):
                P.op("pe", lambda bo=bo, qd=qd, dc=dc: nc.tensor.matmul(
                    bo.t[:, 0:256], qd.t[:, dc, :], stb.t[:, dc * 256:(dc + 1) * 256], start=False, stop=(dc == 1)),
                    reads=[qd.k, stb.k], writes=[bo.k])
            bu = cx.banks.get()
            for dc in range(2):
                P.op("pe", lambda bu=bu, kdec=kdec, v=v, c=c, dc=dc: nc.tensor.matmul(
                    bu.t[:, dc * 256:(dc + 1) * 256], kdec.t[:, dc * 128:(dc + 1) * 128], v.t[:, c, :], start=True, stop=True),
                    reads=[kdec.k, v.k], writes=[bu.k])
            P.op("dve", lambda bu=bu: nc.vector.scalar_tensor_tensor(
                out=stf.t[:, :], in0=stf.t[:, :], scalar=rsc.t[:, 1:2], in1=bu.t[:, :], op0=ALU.mult, op1=ALU.add),
                reads=[stf.k, rsc.k, bu.k], writes=[stf.k])
            P.op("pool", lambda: nc.gpsimd.tensor_copy(stb.t[:, :], stf.t[:, :]), reads=[stf.k], writes=[stb.k])
            st = str_.get()
            j = jk.get()
            P.op("dve", lambda st=st: nc.vector.memset(st.t[:, :], 0.0), writes=[st.k])
            P.op("act", lambda st=st, j=j, bo=bo: nc.scalar.activation(
                out=j.t[:, :], in_=bo.t[:, 0:256], func=AF.Square, scale=1.0 / 16, accum_out=st.t[:, 0:1]),
                reads=[bo.k, st.k], writes=[j.k, st.k])
            rstd_from_ss(cx, st)
            P.op("dve", lambda os_=os_, bo=bo, st=st, sg=sg, c=c: nc.vector.scalar_tensor_tensor(
                out=os_.t[:, c, :], in0=bo.t[:, 0:256], scalar=st.t[:, 2:3], in1=sg.t[:, c, :], op0=ALU.mult, op1=ALU.mult),
                reads=[bo.k, st.k, sg.k], writes=[os_.k])
        P.dma("sp", lambda os_=os_, t0=t0: nc.sync.dma_start(
            out=mix[t0:t0 + 512, 0:256].rearrange("(c p) e -> p c e", p=128), in_=os_.t[:, :, :]), reads=[os_.k], writes=[kmix])

    cx.P.drain("sp")
    NT = S // 128
    NQ = S // 512
    triu = cx.load(cx.sb([128, 128], F32), K["triu"])
    sltri = cx.load(cx.sb([128, 128], F32), K["sltri"])
    trineg = cx.load(cx.sb([128, 128], F32), K["trineg"])
    bfr = cx.load(cx.sb([128, 2], F32), K["fox_b"])
    negb = cx.sb([128, 2], F32)
    P.op("dve", lambda: nc.vector.tensor_scalar(negb.t[:, :], bfr.t[:, :], -1.0, 0.0, ALU.mult, ALU.add), reads=[bfr.k], writes=[negb.k])
    LF = cx.sb([128, 128], F32)
    LTs = cx.sb([128, 128], F32)
    Ws = cx.sb([128, 128], F32)
    offs = cx.sb([128, 1], F32)
    cn = cx.sb([128, 128], F32)
    cnT = cx.sb([128, 128], F32)
    kTb = cx.sb([128, S], BF16)
    vaug = cx.sb([128, NT, 132], BF16)
    qbr = cx.ring(2, [128, 512], BF16)
    cnqr = cx.ring(2, [128, 512], F32)
    tmpr = cx.ring(3, [128, 512], F32)
    pr = cx.ring(3, [128, 512], BF16)
    rdr = cx.ring(4, [128, 1], F32)
    fos = cx.ring(2, [128, 4, 128], BF16)
    accb = [cx.allbanks[i] for i in range(4)]
    sring = Ring([cx.allbanks[i] for i in range(4, 8)])
    scale = 128.0 ** -0.5
    kcn = Trk(multi=True)
    for hd in range(2):
        P.dma("sp", lambda hd=hd: nc.sync.dma_start(out=LF.t[0:NT, :], in_=dd["ffT"][hd, :].rearrange("(j p) -> j p", p=128)),
              reads=[kd], writes=[LF.k])
        P.op("act", lambda hd=hd: nc.scalar.activation(out=LF.t[0:NT, :], in_=LF.t[0:NT, :], func=AF.Exp, scale=-1.0,
                                                        bias=negb.t[0:NT, hd:hd + 1]), reads=[LF.k, negb.k], writes=[LF.k])
        P.op("act", lambda: nc.scalar.activation(out=LF.t[0:NT, :], in_=LF.t[0:NT, :], func=AF.Ln, bias=1.0),
             reads=[LF.k], writes=[LF.k])
        b = sring.get()
        P.op("pe", lambda b=b: nc.tensor.transpose(b.t[:, 0:NT], LF.t[0:NT, :], identf.t[0:NT, 0:NT]),
             reads=[LF.k, identf.k], writes=[b.k])
        P.op("dve", lambda b=b: nc.vector.tensor_copy(LTs.t[:, 0:NT], b.t[:, 0:NT]), reads=[b.k], writes=[LTs.k])
        b2 = sring.get()
        P.op("pe", lambda b2=b2: nc.tensor.matmul(b2.t[0:NT, 0:128], LTs.t[:, 0:NT], triu.t[:, :], start=True, stop=True),
             reads=[LTs.k, triu.k], writes=[b2.k])
        P.op("dve", lambda b2=b2: nc.vector.tensor_copy(Ws.t[0:NT, :], b2.t[0:NT, 0:128]), reads=[b2.k], writes=[Ws.k])
        b3 = sring.get()
        P.op("pe", lambda b3=b3: nc.tensor.matmul(b3.t[0:NT, 0:1], sltri.t[0:NT, 0:NT], Ws.t[0:NT, 127:128], start=True, stop=True),
             reads=[Ws.k, sltri.k], writes=[b3.k])
        P.op("dve", lambda b3=b3: nc.vector.tensor_copy(offs.t[0:NT, :], b3.t[0:NT, 0:1]), reads=[b3.k], writes=[offs.k])
        P.op("dve", lambda: nc.vector.tensor_scalar(cn.t[0:NT, :], Ws.t[0:NT, :], offs.t[0:NT, 0:1], 0.0, ALU.add, ALU.add),
             reads=[Ws.k, offs.k], writes=[cn.k])
        P.dma("sp", lambda hd=hd: nc.sync.dma_start(out=dd["cn"][hd, :].rearrange("(j p) -> j p", p=128), in_=cn.t[0:NT, :]),
              reads=[cn.k], writes=[kcn])
        b4 = sring.get()
        P.op("pe", lambda b4=b4: nc.tensor.transpose(b4.t[:, 0:NT], cn.t[0:NT, :], identf.t[0:NT, 0:NT]),
             reads=[cn.k, identf.k], writes=[b4.k])
        P.op("dve", lambda b4=b4: nc.vector.tensor_copy(cnT.t[:, 0:NT], b4.t[:, 0:NT]), reads=[b4.k], writes=[cnT.k])
        P.dma("sp", lambda hd=hd: nc.sync.dma_start(out=kTb.t[:, :], in_=dd["fkT"][hd, :, :]), reads=[kd], writes=[kTb.k])
        P.op("pool", lambda: nc.gpsimd.memset(vaug.t[:, :, 128:132], 1.0), writes=[vaug.k])
        for j0 in range(0, NT, 16):
            j1 = min(NT, j0 + 16)
            P.dma("sp", lambda hd=hd, j0=j0, j1=j1: nc.sync.dma_start(
                out=vaug.t[:, j0:j1, 0:128], in_=dd["fv"][hd, j0 * 128:j1 * 128, :].rearrange("(j p) e -> p j e", p=128)),
                reads=[kd], writes=[vaug.k])
        for qb in range(NQ):
            qblk, cnq = qbr.get(), cnqr.get()
            P.dma("sp", lambda qblk=qblk, hd=hd, qb=qb: nc.sync.dma_start(out=qblk.t[:, :], in_=dd["fqT"][hd, :, qb * 512:(qb + 1) * 512]),
                  reads=[kd], writes=[qblk.k])
            P.dma("sp", lambda cnq=cnq, hd=hd, qb=qb: nc.sync.dma_start(
                out=cnq.t[:, :], in_=dd["cn"][hd, qb * 512:(qb + 1) * 512].partition_broadcast(128)), reads=[kcn], writes=[cnq.k])
            nj = 4 * qb + 4
            for j in range(nj):
                r = j - 4 * qb
                c0 = max(r, 0) * 128
                bs = sring.get()
                P.op("pe", lambda bs=bs, j=j, c0=c0, qblk=qblk: nc.tensor.matmul(
                    bs.t[:, c0:512], kTb.t[:, j * 128:(j + 1) * 128], qblk.t[:, c0:512], start=True, stop=True),
                    reads=[kTb.k, qblk.k], writes=[bs.k])
                tm = tmpr.get()
                P.op("dve", lambda tm=tm, bs=bs, cnq=cnq, c0=c0: nc.vector.scalar_tensor_tensor(
                    out=tm.t[:, c0:512], in0=bs.t[:, c0:512], scalar=scale, in1=cnq.t[:, c0:512], op0=ALU.mult, op1=ALU.subtract),
                    reads=[bs.k, cnq.k], writes=[tm.k])
                if r >= 0:
                    P.op("pool", lambda tm=tm, c0=c0: nc.gpsimd.tensor_tensor(
                        out=tm.t[:, c0:c0 + 128], in0=tm.t[:, c0:c0 + 128], in1=trineg.t[:, :], op=ALU.add),
                        reads=[tm.k, trineg.k], writes=[tm.k])
                p = pr.get()
                P.op("act", lambda p=p, tm=tm, j=j, c0=c0: nc.scalar.activation(
                    out=p.t[:, c0:512], in_=tm.t[:, c0:512], func=AF.Exp, bias=cnT.t[:, j:j + 1], scale=1.0),
                    reads=[tm.k, cnT.k], writes=[p.k])
                for r2 in range(max(r, 0), 4):
                    P.op("pe", lambda p=p, r2=r2, j=j, s_=(j == 0), e_=(j == 4 * qb + r2): nc.tensor.matmul(
                        accb[r2].t[:, 0:129], p.t[:, r2 * 128:(r2 + 1) * 128], vaug.t[:, j, 0:129], start=s_, stop=e_),
                        reads=[p.k, vaug.k], writes=[accb[r2].k])
            fo = fos.get()
            for r2 in range(4):
                rd = rdr.get()
                P.op("dve", lambda rd=rd, r2=r2: nc.vector.reciprocal(rd.t[:, :], accb[r2].t[:, 128:129]), reads=[accb[r2].k], writes=[rd.k])
                P.op("act", lambda fo=fo, rd=rd, r2=r2: nc.scalar.activation(
                    out=fo.t[:, r2, :], in_=accb[r2].t[:, 0:128], func=AF.Copy, scale=rd.t[:, 0:1]), reads=[accb[r2].k, rd.k], writes=[fo.k])
            P.dma("sp", lambda fo=fo, qb=qb, hd=hd: nc.sync.dma_start(
                out=mix[qb * 512:(qb + 1) * 512, 256 + hd * 128:256 + (hd + 1) * 128].rearrange("(c p) e -> p c e", p=128),
                in_=fo.t[:, :, :]), reads=[fo.k], writes=[kmix])


def stage_a1(cx, K, S, TOK, G, hg, khg, w1b, kw1, dd, CW=None):
    nc, P = cx.nc, cx.P
    res = common_res(cx, K)
    hTr = cx.ring(2, [128, KD, G], BF16)
    ob = cx.ring(4, [128, G], BF16)
    ot = cx.ring(3, [128, 512], BF16)
    of = cx.ring(2, [128, G], F32)
    NTT = G // 128
    kd = dd["k"]
    for r in range(NCORE):
        for gg in range(TOK // G):
            segs, gp = seg_map(S, TOK, G, r, gg, CW)
            hT = hTr.get()
            src = hg[r * D:(r + 1) * D, gg * G:(gg + 1) * G].rearrange("(c p) t -> p c t", p=128)
            P.dma("sp", lambda hT=hT, src=src: nc.sync.dma_start(out=hT.t[:, :, :], in_=src), reads=[khg], writes=[hT.k])
            for cb in range(2):
                bk = gemm(cx, res, "T", hT, KD, G, 512, lambda wt, k0, nk, cb=cb: load_w(cx, wt, w1b, kw1, k0, nk, cb * 512, 512))
                for tt in range(NTT):
                    o = ot.get()
                    P.op("act", lambda o=o, b=bk[tt]: nc.scalar.copy(o.t[:, :], b.t[:, :]), reads=[bk[tt].k], writes=[o.k])
                    rows = slice(gp(tt), gp(tt) + 128)
                    P.dma("pool", lambda o=o, rows=rows, cb=cb: nc.gpsimd.dma_start(out=dd["z"][rows, cb * 512:(cb + 1) * 512], in_=o.t[:, :]),
                          reads=[o.k], writes=[kd])
            for cb in range(3):
                ncl = 512 if cb < 2 else 256
                bk = gemm(cx, res, "F", hT, KD, G, ncl, lambda wt, k0, nk, cb=cb, ncl=ncl: load_w(cx, wt, w1b, kw1, k0, nk, 1024 + cb * 512, ncl))
                for i in range(ncl // 128):
                    o = ob.get()
                    eng = "act" if i % 2 == 0 else "dve"
                    if eng == "act":
                        P.op("act", lambda o=o, b=bk[i]: nc.scalar.copy(o.t[:, :], b.t[:, 0:G]), reads=[bk[i].k], writes=[o.k])
                    else:
                        P.op("dve", lambda o=o, b=bk[i]: nc.vector.tensor_copy(o.t[:, :], b.t[:, 0:G]), reads=[bk[i].k], writes=[o.k])
                    ch = cb * 4 + i
                    for (ls, gs, L) in segs:
                        P.dma("pool", lambda o=o, ch=ch, ls=ls, gs=gs, L=L: nc.gpsimd.dma_start(
                            out=dd["xbc"][ch * 128:(ch + 1) * 128, gs:gs + L], in_=o.t[:, ls:ls + L]), reads=[o.k], writes=[kd])
            bk = gemm(cx, res, "F", hT, KD, G, 16, lambda wt, k0, nk: load_w(cx, wt, w1b, kw1, k0, nk, 2304, 16))
            o = of.get()
            P.op("act", lambda o=o, b=bk[0]: nc.scalar.copy(o.t[0:16, :], b.t[0:16, 0:G]), reads=[bk[0].k], writes=[o.k])
            for (ls, gs, L) in segs:
                P.dma("pool", lambda o=o, ls=ls, gs=gs, L=L: nc.gpsimd.dma_start(out=dd["dtT"][:, gs:gs + L], in_=o.t[0:16, ls:ls + L]), reads=[o.k], writes=[kd])


def stage_m1(cx, K, S, dd, mix, kmix):
    nc, P = cx.nc, cx.P
    kd = dd["k"]
    ident = cx.load(cx.sb([128, 128], BF16), K["ident"])
    identf = cx.load(cx.sb([128, 128], F32), K["identf"])
    triu = cx.load(cx.sb([128, 128], F32), K["triu"])
    onesf = cx.load(cx.sb([128, 128], F32), K["onesf"])
    selm = cx.load(cx.sb([16, 16, 128], F32), K["selmat"])
    cw = cx.load(cx.sb([128, 10, 4], F32), K["m_convw"])
    cbias = cx.load(cx.sb([128, 10], F32), K["m_convb"])
    dtb = cx.load(cx.sb([16, 1], F32), K["m_dtb"])
    abc = cx.load(cx.sb([128, 16], F32), K["m_alog"])
    dsk = cx.load(cx.sb([128, 16], F32), K["m_dskip"])
    gnw = cx.load(cx.sb([128, 1024], F32), K["m_gnw"].partition_broadcast(128))
    trim = cx.load(cx.sb([128, 128], F32), K["tri01"])
    P.op("act", lambda: nc.scalar.activation(out=abc.t[:, :], in_=abc.t[:, :], func=AF.Exp), reads=[abc.k], writes=[abc.k])
    P.op("dve", lambda: nc.vector.tensor_scalar(abc.t[:, :], abc.t[:, :], -1.0, 0.0, ALU.mult, ALU.add), reads=[abc.k], writes=[abc.k])
    stf = cx.sb([128, 1024], F32)
    stb = cx.sb([128, 1024], BF16)
    P.op("dve", lambda: nc.vector.memset(stf.t[:, :], 0.0), writes=[stf.k])
    P.op("dve", lambda: nc.vector.memset(stb.t[:, :], 0.0), writes=[stb.k])
    prer = cx.ring(2, [128, 10, 516], BF16)
    accr = cx.ring(2, [128, 512], F32)
    xcr = cx.ring(2, [128, 10, 512], BF16)
    dtr = cx.ring(2, [16, 512], F32)
    zr = cx.ring(2, [128, 4, 1024], BF16)
    xtr = cx.ring(2, [128, 1024], BF16)
    btr = cx.ring(2, [128, 128], BF16)
    smr = cx.ring(12, [128, 16], F32)
    acT = cx.ring(2, [16, 128], F32)
    cbr = cx.ring(2, [128, 128], F32)
    decr = cx.ring(2, [128, 4, 128], F32)
    Mr = cx.ring(2, [128, 16, 128], BF16)
    xdr = cx.ring(2, [128, 1024], BF16)
    xwr = cx.ring(2, [128, 1024], BF16)
    f1 = cx.ring(2, [128, 1024], F32)
    f2 = cx.ring(2, [128, 1024], F32)
    f3 = cx.ring(2, [128, 1024], F32)
    str_ = cx.ring(4, [128, 4], F32)
    osr = cx.ring(2, [128, 4, 1024], BF16)

    def v3(ap):
        return ap.rearrange("p (h e) -> p h e", e=64)

    def bc3(ap16):
        return ap16.unsqueeze(2).to_broadcast([128, 16, 64])

    for sc in range(S // 512):
        t0 = sc * 512
        pre = prer.get()
        if sc == 0:
            P.op("pool", lambda pre=pre: nc.gpsimd.memset(pre.t[:, :, 0:4], 0.0), writes=[pre.k])
            P.dma("sp", lambda pre=pre: nc.sync.dma_start(out=pre.t[:, :, 4:516], in_=dd["xbc"][:, 0:512].rearrange("(c p) t -> p c t", p=128)),
                  reads=[kd], writes=[pre.k])
        else:
            P.dma("sp", lambda pre=pre, t0=t0: nc.sync.dma_start(
                out=pre.t[:, :, 0:516], in_=dd["xbc"][:, t0 - 4:t0 + 512].rearrange("(c p) t -> p c t", p=128)), reads=[kd], writes=[pre.k])
        dtt = dtr.get()
        P.dma("sp", lambda dtt=dtt, t0=t0: nc.sync.dma_start(out=dtt.t[:, :], in_=dd["dtT"][:, t0:t0 + 512]), reads=[kd], writes=[dtt.k])
        z = zr.get()
        P.dma("sp", lambda z=z, t0=t0: nc.sync.dma_start(out=z.t[:, :, :], in_=dd["z"][t0:t0 + 512, :].rearrange("(c p) e -> p c e", p=128)),
              reads=[kd], writes=[z.k])
        xc = xcr.get()
        for ch in range(10):
            eng = "dve"
            eo = nc.vector
            acc = accr.get()
            P.op(eng, lambda eo=eo, acc=acc, pre=pre, ch=ch: eo.tensor_scalar(
                acc.t[:, :], pre.t[:, ch, 1:513], cw.t[:, ch, 0:1], 0.0, ALU.mult, ALU.add), reads=[pre.k, cw.k], writes=[acc.k])
            for k in range(1, 4):
                P.op(eng, lambda eo=eo, acc=acc, pre=pre, ch=ch, k=k: eo.scalar_tensor_tensor(
                    out=acc.t[:, :], in0=pre.t[:, ch, 1 + k:513 + k], scalar=cw.t[:, ch, k:k + 1], in1=acc.t[:, :],
                    op0=ALU.mult, op1=ALU.add), reads=[pre.k, cw.k, acc.k], writes=[acc.k])
            P.op("act", lambda acc=acc, xc=xc, ch=ch: nc.scalar.activation(
                out=xc.t[:, ch, :], in_=acc.t[:, :], func=AF.Silu, bias=cbias.t[:, ch:ch + 1]), reads=[acc.k, cbias.k], writes=[xc.k])
        P.op("act", lambda dtt=dtt: nc.scalar.activation(out=dtt.t[:, :], in_=dtt.t[:, :], func=AF.Exp, bias=dtb.t[:, 0:1]),
             reads=[dtt.k, dtb.k], writes=[dtt.k])
        P.op("act", lambda dtt=dtt: nc.scalar.activation(out=dtt.t[:, :], in_=dtt.t[:, :], func=AF.Ln, bias=1.0),
             reads=[dtt.k], writes=[dtt.k])
        os_ = osr.get()
        for c in range(4):
            cols = slice(c * 128, (c + 1) * 128)
            bx = cx.banks.get()
            pvx = bx.t[:, :].bitcast(BF16)
            for i in range(8):
                P.op("pe", lambda pvx=pvx, xc=xc, i=i, cols=cols: nc.tensor.transpose(
                    pvx[:, i * 128:(i + 1) * 128], xc.t[:, i, cols], ident.t[:, :]), reads=[xc.k, ident.k], writes=[bx.k])
            xt = xtr.get()
            P.op("act", lambda xt=xt, pvx=pvx: nc.scalar.copy(xt.t[:, :], pvx[:, 0:1024]), reads=[bx.k], writes=[xt.k])
            bb = cx.banks.get()
            pvb = bb.t[:, :].bitcast(BF16)
            P.op("pe", lambda pvb=pvb, xc=xc, cols=cols: nc.tensor.transpose(pvb[:, 0:128], xc.t[:, 8, cols], ident.t[:, :]),
                 reads=[xc.k, ident.k], writes=[bb.k])
            bt = btr.get()
            P.op("dve", lambda bt=bt, pvb=pvb: nc.vector.tensor_copy(bt.t[:, :], pvb[:, 0:128]), reads=[bb.k], writes=[bt.k])
            bd = cx.banks.get()
            P.op("pe", lambda bd=bd, dtt=dtt, cols=cols: nc.tensor.transpose(bd.t[:, 0:16], dtt.t[0:16, cols], identf.t[0:16, 0:16]),
                 reads=[dtt.k, identf.k], writes=[bd.k])
            dts, da = smr.get(), smr.get()
            P.op("dve", lambda dts=dts, bd=bd: nc.vector.tensor_copy(dts.t[:, :], bd.t[:, 0:16]), reads=[bd.k], writes=[dts.k])
            P.op("dve", lambda da=da, dts=dts: nc.vector.tensor_tensor(out=da.t[:, :], in0=dts.t[:, :], in1=abc.t[:, :], op=ALU.mult),
                 reads=[dts.k, abc.k], writes=[da.k])
            ba = cx.banks.get()
            P.op("pe", lambda ba=ba, da=da: nc.tensor.matmul(ba.t[:, 0:16], triu.t[:, :], da.t[:, :], start=True, stop=True),
                 reads=[triu.k, da.k], writes=[ba.k])
            P.op("pe", lambda ba=ba, da=da: nc.tensor.matmul(ba.t[:, 16:32], onesf.t[:, :], da.t[:, :], start=True, stop=True),
                 reads=[onesf.k, da.k], writes=[ba.k])
            P.op("pe", lambda ba=ba, da=da: nc.tensor.matmul(ba.t[0:16, 64:192], da.t[:, :], triu.t[:, :], start=True, stop=True),
                 reads=[triu.k, da.k], writes=[ba.k])
            nacs, eacs, tail, eL = smr.get(), smr.get(), smr.get(), smr.get()
            P.op("dve", lambda nacs=nacs, ba=ba: nc.vector.tensor_scalar(nacs.t[:, :], ba.t[:, 0:16], -1.0, 0.0, ALU.mult, ALU.add),
                 reads=[ba.k], writes=[nacs.k])
            P.op("act", lambda eacs=eacs, ba=ba: nc.scalar.activation(out=eacs.t[:, :], in_=ba.t[:, 0:16], func=AF.Exp),
                 reads=[ba.k], writes=[eacs.k])
            P.op("act", lambda eL=eL, ba=ba: nc.scalar.activation(out=eL.t[:, :], in_=ba.t[:, 16:32], func=AF.Exp),
                 reads=[ba.k], writes=[eL.k])
            P.op("dve", lambda tail=tail, ba=ba, nacs=nacs: nc.vector.tensor_tensor(out=tail.t[:, :], in0=ba.t[:, 16:32], in1=nacs.t[:, :], op=ALU.add),
                 reads=[ba.k, nacs.k], writes=[tail.k])
            P.op("act", lambda tail=tail: nc.scalar.activation(out=tail.t[:, :], in_=tail.t[:, :], func=AF.Exp), reads=[tail.k], writes=[tail.k])
            at = acT.get()
            P.op("dve", lambda at=at, ba=ba: nc.vector.tensor_copy(at.t[:, :], ba.t[0:16, 64:192]), reads=[ba.k], writes=[at.k])
            bcb = cx.banks.get()
            P.op("pe", lambda bcb=bcb, xc=xc, cols=cols: nc.tensor.matmul(bcb.t[:, 0:128], xc.t[:, 8, cols], xc.t[:, 9, cols], start=True, stop=True),
                 reads=[xc.k], writes=[bcb.k])
            cbm = cbr.get()
            P.op("dve", lambda cbm=cbm, bcb=bcb: nc.vector.tensor_tensor(out=cbm.t[:, :], in0=bcb.t[:, 0:128], in1=trim.t[:, :], op=ALU.mult),
                 reads=[bcb.k, trim.k], writes=[cbm.k])
            xd, xw = xdr.get(), xwr.get()
            P.op("dve", lambda xd=xd, xt=xt, dts=dts: nc.vector.tensor_tensor(out=v3(xd.t[:, :]), in0=v3(xt.t[:, :]), in1=bc3(dts.t[:, :]), op=ALU.mult),
                 reads=[xt.k, dts.k], writes=[xd.k])
            P.op("pool", lambda xw=xw, xd=xd, tail=tail: nc.gpsimd.tensor_tensor(out=v3(xw.t[:, :]), in0=v3(xd.t[:, :]), in1=bc3(tail.t[:, :]), op=ALU.mult),
                 reads=[xd.k, tail.k], writes=[xw.k])
            M = Mr.get()
            by = [cx.banks.get(), cx.banks.get()]
            for hg_ in range(4):
                bdec = cx.banks.get()
                for hh in range(4):
                    h = hg_ * 4 + hh
                    P.op("pe", lambda bdec=bdec, at=at, h=h, hh=hh: nc.tensor.matmul(
                        bdec.t[:, hh * 128:(hh + 1) * 128], selm.t[:, h, :], at.t[:, :], start=True, stop=True),
                        reads=[selm.k, at.k], writes=[bdec.k])
                dec = decr.get()
                for hh in range(4):
                    h = hg_ * 4 + hh
                    P.op("act", lambda dec=dec, bdec=bdec, h=h, hh=hh, nacs=nacs: nc.scalar.activation(
                        out=dec.t[:, hh, :], in_=bdec.t[:, hh * 128:(hh + 1) * 128], func=AF.Exp, bias=nacs.t[:, h:h + 1]),
                        reads=[bdec.k, nacs.k], writes=[dec.k])
                for hh in range(4):
                    h = hg_ * 4 + hh
                    P.op("dve", lambda dec=dec, M=M, cbm=cbm, h=h, hh=hh: nc.vector.scalar_tensor_tensor(
                        out=M.t[:, h, :], in0=dec.t[:, hh, :], scalar=1.0, in1=cbm.t[:, :], op0=ALU.min, op1=ALU.mult),
                        reads=[dec.k, cbm.k], writes=[M.k])
                for hh in range(4):
                    h = hg_ * 4 + hh
                    P.op("pe", lambda M=M, xd=xd, h=h, byb=by[h // 8]: nc.tensor.matmul(
                        byb.t[:, (h % 8) * 64:(h % 8 + 1) * 64], M.t[:, h, :], xd.t[:, h * 64:(h + 1) * 64], start=True, stop=True),
                        reads=[M.k, xd.k], writes=[by[h // 8].k])
            bo = [cx.banks.get(), cx.banks.get()]
            for hb in range(2):
                P.op("pe", lambda hb=hb, xc=xc, cols=cols, bob=bo[hb]: nc.tensor.matmul(
                    bob.t[:, :], xc.t[:, 9, cols], stb.t[:, hb * 512:(hb + 1) * 512], start=True, stop=True),
                    reads=[xc.k, stb.k], writes=[bo[hb].k])
            y1, y2, y3 = f1.get(), f2.get(), f3.get()
            for hb in range(2):
                hs = slice(hb * 512, (hb + 1) * 512)
                P.op("dve", lambda y1=y1, hb=hb, hs=hs, eacs=eacs, bob=bo[hb]: nc.vector.tensor_tensor(
                    out=v3(y1.t[:, hs]), in0=v3(bob.t[:, :]), in1=eacs.t[:, hb * 8:(hb + 1) * 8].unsqueeze(2).to_broadcast([128, 8, 64]), op=ALU.mult),
                    reads=[bo[hb].k, eacs.k], writes=[y1.k])
                P.op("dve", lambda y1=y1, hs=hs, byb=by[hb]: nc.vector.tensor_tensor(
                    out=y1.t[:, hs], in0=byb.t[:, :], in1=y1.t[:, hs], op=ALU.add), reads=[by[hb].k, y1.k], writes=[y1.k])
            bsu = [cx.banks.get(), cx.banks.get()]
            for hb in range(2):
                P.op("pe", lambda hb=hb, bt=bt, xw=xw, bsb=bsu[hb]: nc.tensor.matmul(
                    bsb.t[:, :], bt.t[:, :], xw.t[:, hb * 512:(hb + 1) * 512], start=True, stop=True),
                    reads=[bt.k, xw.k], writes=[bsu[hb].k])
            P.op("pool", lambda eL=eL: nc.gpsimd.tensor_tensor(out=v3(stf.t[:, :]), in0=v3(stf.t[:, :]), in1=bc3(eL.t[:, :]), op=ALU.mult),
                 reads=[stf.k, eL.k], writes=[stf.k])
            for hb in range(2):
                hs = slice(hb * 512, (hb + 1) * 512)
                P.op("dve", lambda hs=hs, bsb=bsu[hb]: nc.vector.tensor_tensor(out=stf.t[:, hs], in0=bsb.t[:, :], in1=stf.t[:, hs], op=ALU.add),
                     reads=[bsu[hb].k, stf.k], writes=[stf.k])
            P.op("act", lambda: nc.scalar.copy(stb.t[:, :], stf.t[:, :]), reads=[stf.k], writes=[stb.k])
            P.op("pool", lambda y2=y2, xt=xt: nc.gpsimd.tensor_tensor(out=v3(y2.t[:, :]), in0=v3(xt.t[:, :]), in1=bc3(dsk.t[:, :]), op=ALU.mult),
                 reads=[xt.k, dsk.k], writes=[y2.k])
            P.op("pool", lambda y1=y1, y2=y2: nc.gpsimd.tensor_tensor(out=y1.t[:, :], in0=y1.t[:, :], in1=y2.t[:, :], op=ALU.add),
                 reads=[y1.k, y2.k], writes=[y1.k])
            P.op("act", lambda y3=y3, z=z, c=c: nc.scalar.activation(out=y3.t[:, :], in_=z.t[:, c, :], func=AF.Silu), reads=[z.k], writes=[y3.k])
            P.op("dve", lambda y1=y1, y3=y3: nc.vector.tensor_tensor(out=y1.t[:, :], in0=y1.t[:, :], in1=y3.t[:, :], op=ALU.mult),
                 reads=[y1.k, y3.k], writes=[y1.k])
            st = str_.get()
            P.op("dve", lambda st=st: nc.vector.memset(st.t[:, :], 0.0), writes=[st.k])
            P.op("act", lambda st=st, y1=y1, y2=y2: nc.scalar.activation(
                out=y2.t[:, :], in_=y1.t[:, :], func=AF.Square, scale=1.0 / 32, accum_out=st.t[:, 0:1]),
                reads=[y1.k, st.k], writes=[y2.k, st.k])
            rstd_from_ss(cx, st)
            P.op("dve", lambda os_=os_, y1=y1, st=st, c=c: nc.vector.scalar_tensor_tensor(
                out=os_.t[:, c, :], in0=y1.t[:, :], scalar=st.t[:, 2:3], in1=gnw.t[:, :], op0=ALU.mult, op1=ALU.mult),
                reads=[y1.k, st.k, gnw.k], writes=[os_.k])
        P.dma("sp", lambda os_=os_, t0=t0: nc.sync.dma_start(
            out=mix[t0:t0 + 512, :].rearrange("(c p) e -> p c e", p=128), in_=os_.t[:, :, :]), reads=[os_.k], writes=[kmix])


def stage_c(cx, K, TOK, G, layer, mg, kmg, x_in, kxin, x_outd, kxo, W, final):
    nc, P = cx.nc, cx.P
    NG = TOK // G
    NTT = G // 128
    res = common_res(cx, K)
    ones = cx.load(cx.sb([128, 128], BF16), K["ones"])
    gbc = cx.sb([128, D], F32)
    hT = cx.sb([128, KD, G], BF16)
    big = cx.sb([128, 44, G], BF16)
    kT = cx.sb([128, 4, 256], BF16)
    vv = cx.sb([128, 2, 512], BF16)
    qT = cx.sb([128, 4, G], BF16)
    oT = cx.sb([128, 4, G], BF16)
    pT = cx.ring(2, [128, 2, G], BF16)
    tmpf = cx.ring(2, [128, G], F32)
    x_a, x_b, x_c = K["x_a"], K["x_b"], K["x_c%d" % layer]
    ka, kb, kc_ = Trk(multi=True), Trk(multi=True), kxo
    wq, kwq = W["wq%d" % layer]
    wk, kwk = W["wk%d" % layer]
    wv, kwv = W["wv%d" % layer]
    wo, kwo = W["wo%d" % layer]
    wg, kwg = W["wg%d" % layer]
    wu, kwu = W["wu%d" % layer]
    wd, kwd = W["wd%d" % layer]

    def load_g(vec):
        P.dma("sp", lambda: nc.sync.dma_start(out=gbc.t[:, :], in_=vec.partition_broadcast(128)), writes=[gbc.k])

    load_g(K["mem_norm"])
    norm_rows(cx, res, [K["mem"][i * 128:(i + 1) * 128, :] for i in range(2)], gbc, hT)
    bk = gemm(cx, res, "F", hT, KD, 256, 512, lambda wt, k0, nk: load_w(cx, wt, wk, kwk, k0, nk, 0, 512))
    for h in range(4):
        P.op("act", lambda h=h, b=bk[h]: nc.scalar.copy(kT.t[:, h, :], b.t[:, 0:256]), reads=[bk[h].k], writes=[kT.k])
    bv = gemm(cx, res, "T", hT, KD, 256, 512, lambda wt, k0, nk: load_w(cx, wt, wv, kwv, k0, nk, 0, 512))
    for mt in range(2):
        P.op("dve", lambda mt=mt, b=bv[mt]: nc.vector.tensor_copy(vv.t[:, mt, :], b.t[:, :]), reads=[bv[mt].k], writes=[vv.k])

    for tt in range(TOK // 128):
        xt = res["xt"].get()
        rows = slice(tt * 128, (tt + 1) * 128)
        P.dma("sp", lambda xt=xt, rows=rows: nc.sync.dma_start(out=xt.t[:, :], in_=x_in[rows, :]), reads=[kxin], writes=[xt.k])
        for hh in range(2):
            xd_ = res["xt2"].get()
            cs_ = slice(hh * 2048, (hh + 1) * 2048)
            P.dma("sp", lambda xd_=xd_, rows=rows, cs_=cs_: nc.sync.dma_start(out=xd_.t[:, :], in_=mg[rows, cs_]), reads=[kmg], writes=[xd_.k])
            P.op("pool", lambda xt=xt, xd_=xd_, cs_=cs_: nc.gpsimd.tensor_tensor(out=xt.t[:, cs_], in0=xt.t[:, cs_], in1=xd_.t[:, :], op=ALU.add),
                 reads=[xt.k, xd_.k], writes=[xt.k])
        P.dma("sp", lambda xt=xt, rows=rows: nc.sync.dma_start(out=x_a[rows, :], in_=xt.t[:, :]), reads=[xt.k], writes=[ka])

    load_g(K["xa_norm"][layer, :])
    scale = 128.0 ** -0.5
    for g in range(NG):
        r0 = g * G
        norm_rows_k(cx, res, [x_a[r0 + tt * 128:r0 + (tt + 1) * 128, :] for tt in range(NTT)], ka, gbc, hT)
        bq = gemm(cx, res, "F", hT, KD, G, 512, lambda wt, k0, nk: load_w(cx, wt, wq, kwq, k0, nk, 0, 512))
        for h in range(4):
            P.op("act", lambda h=h, b=bq[h]: nc.scalar.copy(qT.t[:, h, :], b.t[:, 0:G]), reads=[bq[h].k], writes=[qT.k])
        for h in range(4):
            p = pT.get()
            for mt in range(2):
                bs = cx.banks.get()
                P.op("pe", lambda bs=bs, h=h, mt=mt: nc.tensor.matmul(
                    bs.t[:, 0:G], kT.t[:, h, mt * 128:(mt + 1) * 128], qT.t[:, h, :], start=True, stop=True),
                    reads=[kT.k, qT.k], writes=[bs.k])
                P.op("act", lambda bs=bs, p=p, mt=mt: nc.scalar.activation(
                    out=p.t[:, mt, :], in_=bs.t[:, 0:G], func=AF.Exp, scale=scale), reads=[bs.k], writes=[p.k])
            bo = cx.banks.get()
            bd = cx.banks.get()
            for mt in range(2):
                P.op("pe", lambda bo=bo, h=h, mt=mt, p=p: nc.tensor.matmul(
                    bo.t[:, 0:G], vv.t[:, mt, h * 128:(h + 1) * 128], p.t[:, mt, :], start=(mt == 0), stop=(mt == 1)),
                    reads=[vv.k, p.k], writes=[bo.k])
            for mt in range(2):
                P.op("pe", lambda bd=bd, mt=mt, p=p: nc.tensor.matmul(
                    bd.t[:, 0:G], ones.t[:, :], p.t[:, mt, :], start=(mt == 0), stop=(mt == 1)),
                    reads=[ones.k, p.k], writes=[bd.k])
            tf = tmpf.get()
            P.op("dve", lambda bd=bd, tf=tf: nc.vector.reciprocal(tf.t[:, :], bd.t[:, 0:G]), reads=[bd.k], writes=[tf.k])
            P.op("dve", lambda bo=bo, tf=tf, h=h: nc.vector.tensor_tensor(
                out=oT.t[:, h, :], in0=bo.t[:, 0:G], in1=tf.t[:, :], op=ALU.mult), reads=[bo.k, tf.k], writes=[oT.k])
        for cb in range(D // 512):
            banks = gemm(cx, res, "T", oT, 4, G, 512,
                         lambda wt, k0, nk, cb=cb: load_w(cx, wt, wo, kwo, k0, nk, cb * 512, 512))
            resid_out(cx, res, banks, x_a, ka, x_b, kb, r0, cb * 512, 512)

    load_g(K["ffn_norm"][layer, :])
    NFB = DFF // 256
    for g in range(NG):
        r0 = g * G
        norm_rows_k(cx, res, [x_b[r0 + tt * 128:r0 + (tt + 1) * 128, :] for tt in range(NTT)], kb, gbc, hT)
        for hf, (fb0, fb1) in enumerate(((0, 22), (22, NFB))):
            for fb in range(fb0, fb1):
                def wl(wt, k0, nk, fb=fb):
                    load_w(cx, wt, wg, kwg, k0, nk, fb * 256, 256, 0)
                    load_w(cx, wt, wu, kwu, k0, nk, fb * 256, 256, 256)
                banks = gemm(cx, res, "F", hT, KD, G, 512, wl)
                for s in range(2):
                    tf = tmpf.get()
                    bg, bu = banks[s], banks[2 + s]
                    P.op("act", lambda tf=tf, bg=bg: nc.scalar.activation(out=tf.t[:, :], in_=bg.t[:, 0:G], func=AF.Silu),
                         reads=[bg.k], writes=[tf.k])
                    P.op("dve", lambda tf=tf, bu=bu, c=(fb - fb0) * 2 + s: nc.vector.tensor_tensor(
                        out=big.t[:, c, :], in0=bu.t[:, 0:G], in1=tf.t[:, :], op=ALU.mult),
                        reads=[bu.k, tf.k], writes=[big.k])
            nch = (fb1 - fb0) * 2
            for cb in range(D // 512):
                banks = gemm(cx, res, "T", big, nch, G, 512,
                             lambda wt, k0, nk, cb=cb, fb0=fb0: load_w(cx, wt, wd, kwd, fb0 * 2 + k0, nk, cb * 512, 512))
                if hf == 0:
                    resid_out(cx, res, banks, x_b, kb, x_c, kc_, r0, cb * 512, 512)
                else:
                    resid_out(cx, res, banks, x_c, kc_, x_c, kc_, r0, cb * 512, 512)

    if final:
        load_g(K["final_norm"])
        for tt in range(TOK // 128):
            xt = res["xt"].get()
            xn = res["xn"].get()
            st = res["st"].get()
            src = x_c[tt * 128:(tt + 1) * 128, :]
            dst = x_outd[tt * 128:(tt + 1) * 128, :]
            P.dma("sp", lambda xt=xt, src=src: nc.sync.dma_start(out=xt.t[:, :], in_=src), reads=[kc_], writes=[xt.k])
            P.op("dve", lambda st=st: nc.vector.memset(st.t[:, :], 0.0), writes=[st.k])
            P.op("act", lambda xt=xt, xn=xn, st=st: nc.scalar.activation(
                out=xn.t[:, :], in_=xt.t[:, :], func=AF.Square, scale=float(D) ** -0.5, accum_out=st.t[:, 0:1]),
                reads=[xt.k, st.k], writes=[xn.k, st.k])
            rstd_from_ss(cx, st)
            P.op("dve", lambda xt=xt, st=st: nc.vector.scalar_tensor_tensor(
                out=xt.t[:, :], in0=xt.t[:, :], scalar=st.t[:, 2:3], in1=gbc.t[:, :], op0=ALU.mult, op1=ALU.mult),
                reads=[xt.k, st.k, gbc.k], writes=[xt.k])
            P.dma("sp", lambda xt=xt, dst=dst: nc.sync.dma_start(out=dst, in_=xt.t[:, :]), reads=[xt.k])


def stage_wout(cx, K, S, mix, kmix, mw, woutb, kwout, partial, delta, kdelta, rs=True, nring=2):
    nc, P = cx.nc, cx.P
    res = common_res(cx, K)
    mk = mw // 128
    mTr = cx.ring(nring, [128, mk, 512], BF16)
    ev = cx.ring(4, [128, 512], F32)
    kpart = Trk(multi=True)
    n = 0
    for g in range(S // 512):
        m = mTr.get()
        for tt in range(4):
            rows = slice(g * 512 + tt * 128, g * 512 + (tt + 1) * 128)
            pw = min(mw, 4096)
            for pc in range(mw // pw):
                xn = res["xn"].get()
                P.dma("sp", lambda xn=xn, rows=rows, pc=pc, pw=pw: nc.sync.dma_start(out=xn.t[:, 0:pw], in_=mix[rows, pc * pw:(pc + 1) * pw]),
                      reads=[kmix], writes=[xn.k])
                transpose_rows(cx, res, xn, pw // 128, m, tt, k0=pc * (pw // 128))
        for cb in range(D // 512):
            banks = gemm(cx, res, "T", m, mk, 512, 512,
                         lambda wt, k0, nk, cb=cb: load_w(cx, wt, woutb, kwout, k0, nk, cb * 512, 512))
            for tt in range(4):
                e = ev.get()
                n += 1
                if n % 2 == 0:
                    P.op("act", lambda e=e, b=banks[tt]: nc.scalar.copy(e.t[:, :], b.t[:, :]), reads=[banks[tt].k], writes=[e.k])
                else:
                    P.op("dve", lambda e=e, b=banks[tt]: nc.vector.tensor_copy(e.t[:, :], b.t[:, :]), reads=[banks[tt].k], writes=[e.k])
                dst = partial[g * 512 + tt * 128:g * 512 + (tt + 1) * 128, cb * 512:(cb + 1) * 512]
                P.dma("pool", lambda e=e, dst=dst: nc.gpsimd.dma_start(out=dst, in_=e.t[:, :]), reads=[e.k], writes=[kpart])
    if rs:
        CH = min(S, 2048)
        CWl = CH // NCORE
        for q in range(S // CH):
            src = partial[q * CH:(q + 1) * CH, :]
            dst = delta[q * CWl:(q + 1) * CWl, :]
            P.coll(lambda src=src, dst=dst: nc.gpsimd.collective_compute("ReduceScatter", ALU.add, replica_groups=RG, ins=[src], outs=[dst]),
                   reads=[kpart], writes=[kdelta])
    return kpart


def build_full(S, upto=99):
    TOK = S // NCORE
    G = min(512, TOK)
    nc = bass.Bass("TRN2", target_bir_lowering=False)
    cx = Cx(nc)
    P = cx.P
    K = {}
    for name, shape, dt in (
            ("x", [TOK, D], F32), ("w0", [D, W0C], F32), ("w1", [D, W1C], F32),
            ("wout0", [512, D], F32), ("wout1", [1024, D], F32),
            ("wq_s", [2, 512, XAW], F32), ("wk_s", [2, 512, XAW], F32), ("wv_s", [2, 512, XAW], F32),
            ("wo_s", [2, 64, D], F32), ("wg_s", [2, 512, DFF], F32), ("wu_s", [2, 512, DFF], F32),
            ("wd_s", [2, 1376, D], F32),
            ("mem", [256, D], F32), ("mem_norm", [D], F32), ("ev_norm", [D], F32), ("od_norm", [D], F32),
            ("xa_norm", [2, D], F32), ("ffn_norm", [2, D], F32), ("final_norm", [D], F32),
            ("ident", [128, 128], BF16), ("identf", [128, 128], F32), ("ones", [128, 128], BF16),
            ("onesf", [128, 128], F32), ("triu", [128, 128], F32), ("sltri", [128, 128], F32),
            ("trineg", [128, 128], F32), ("tri01", [128, 128], F32), ("cosT", [128, S], F32), ("sinT", [128, S], F32),
            ("selmat", [16, 16, 128], F32),
            ("ret_decT", [128, 128], F32), ("ret_qdec", [128, 128], F32), ("ret_sc", [128, 2], F32),
            ("fox_b", [128, 2], F32), ("m_convw", [128, 10, 4], F32), ("m_convb", [128, 10], F32),
            ("m_dtb", [16, 1], F32), ("m_alog", [128, 16], F32), ("m_dskip", [128, 16], F32), ("m_gnw", [1024], F32)):
        K[name] = cx.dram_in(name, shape, dt)
    x_out = cx.dram_out("x_out", [TOK, D], F32)
    K["x_a"] = cx.dram("x_a", [TOK, D], F32)
    K["x_b"] = cx.dram("x_b", [TOK, D], F32)
    K["x_c0"] = cx.dram("x_c0", [TOK, D], F32)
    K["x_c1"] = cx.dram("x_c1", [TOK, D], F32)
    kxc0, kxc1 = Trk(multi=True), Trk(multi=True)

    stg = cx.ring(3, [128, 4096], BF16)
    W = {}

    def prep(name, src, gather):
        R, N = src.shape
        sh = cx.dram(name + "_b", [R, N], BF16)
        ksh = Trk(multi=True)
        wprep(cx, stg, src, sh, ksh)
        if gather:
            full = cx.dram(name + "_g", [R * NCORE, N], BF16)
            kf = Trk()
            allgather(cx, sh, ksh, full, kf)
            W[name] = (full, kf)
        else:
            W[name] = (sh, ksh)
    prep("w0", K["w0"], False)
    prep("wout0", K["wout0"], False)
    for l in range(2):
        for nm in ("wq", "wk", "wv", "wo", "wg", "wu", "wd"):
            prep("%s%d" % (nm, l), K[nm + "_s"][l], True)
        if l == 0:
            prep("w1", K["w1"], False)
            prep("wout1", K["wout1"], False)
    if upto < 1:
        P.emit({})
        return nc
    hs = cx.dram("hs", [D, TOK], BF16)
    hg = cx.dram("hg", [NCORE * D, TOK], BF16)
    partial = cx.dram("partial", [S, D], F32)
    delta = cx.dram("delta", [TOK, D], F32)
    d0 = {"k": Trk(multi=True),
          "rqT": cx.dram("rqT", [256, S], BF16), "rkT": cx.dram("rkT", [256, S], BF16),
          "rv": cx.dram("rv", [S, 256], BF16), "rg": cx.dram("rg", [S, 256], BF16),
          "fqT": cx.dram("fqT", [2, 128, S], BF16), "fkT": cx.dram("fkT", [2, 128, S], BF16),
          "fv": cx.dram("fv", [2, S, 128], BF16), "ffT": cx.dram("ffT", [2, S], F32),
          "cn": cx.dram("cn", [2, S], F32)}
    d1 = {"k": Trk(multi=True),
          "z": cx.dram("z", [S, 1024], BF16), "xbc": cx.dram("xbc", [1280, S], BF16),
          "dtT": cx.dram("dtT", [16, S], F32)}
    mix0 = cx.dram("mix0", [S, 512], BF16)
    mix1 = cx.dram("mix1", [S, 1024], BF16)

    def run_layer(layer, x_src, kx, x_dst, kxd, final):
        khs, khg, kmix, kdelta = Trk(multi=True), Trk(), Trk(multi=True), Trk()
        cx.phase_barrier()
        stage_norm_gather(cx, K, x_src, kx, K["ev_norm"] if layer == 0 else K["od_norm"], hs, khs, hg, khg, TOK, G)
        cx.phase_barrier()
        if layer == 0:
            stage_a0(cx, K, S, TOK, G, hg, khg, W["w0"][0], W["w0"][1], d0, CW=min(S, 2048) // NCORE)
            cx.phase_barrier()
            stage_m0(cx, K, S, d0, mix0, kmix)
            cx.phase_barrier()
            stage_wout(cx, K, S, mix0, kmix, 512, W["wout0"][0], W["wout0"][1], partial, delta, kdelta)
        else:
            stage_a1(cx, K, S, TOK, G, hg, khg, W["w1"][0], W["w1"][1], d1, CW=min(S, 2048) // NCORE)
            cx.phase_barrier()
            stage_m1(cx, K, S, d1, mix1, kmix)
            cx.phase_barrier()
            stage_wout(cx, K, S, mix1, kmix, 1024, W["wout1"][0], W["wout1"][1], partial, delta, kdelta)
        cx.phase_barrier()
        stage_c(cx, K, TOK, G, layer, delta, kdelta, x_src, kx, x_dst, kxd, W, final)

    if upto == 3:
        run_layer(1, K["x"], Trk(), x_out, kxc1, True)
        cx.phase_barrier()
        for nm, ap in (("z", d1["z"]), ("xbc", d1["xbc"]), ("dtT", d1["dtT"]), ("mix1", mix1)):
            o = cx.dram_out("dbg_" + nm, list(ap.shape), ap.dtype)
            P.dma("sp", lambda o=o, ap=ap: nc.sync.dma_start(out=o, in_=ap))
        P.drain("sp")
        P.emit({})
        return nc
    run_layer(0, K["x"], Trk(), K["x_c0"], kxc0, False)
    if upto >= 2:
        run_layer(1, K["x_c0"], kxc0, x_out, kxc1, True)
    else:
        cx.phase_barrier()
        t = cx.sb([128, D], F32)
        for tt in range(TOK // 128):
            P.dma("sp", lambda tt=tt: nc.sync.dma_start(out=t.t[:, :], in_=K["x_c0"][tt * 128:(tt + 1) * 128, :]), reads=[kxc0], writes=[t.k])
            P.dma("sp", lambda tt=tt: nc.sync.dma_start(out=x_out[tt * 128:(tt + 1) * 128, :], in_=t.t[:, :]), reads=[t.k])
    P.drain("sp")
    P.drain("pool")
    P.emit({})
    return nc


def host_consts(S):
    bf = ml_dtypes.bfloat16
    c = {}
    c["ident"] = np.eye(128, dtype=np.float32).astype(bf)
    c["identf"] = np.eye(128, dtype=np.float32)
    c["ones"] = np.ones((128, 128), np.float32).astype(bf)
    c["onesf"] = np.ones((128, 128), np.float32)
    i = np.arange(128)
    c["triu"] = (i[:, None] <= i[None, :]).astype(np.float32)
    c["tri01"] = c["triu"].copy()
    c["sltri"] = (i[:, None] < i[None, :]).astype(np.float32)
    c["trineg"] = np.where(i[None, :] >= i[:, None], 0.0, -30000.0).astype(np.float32)
    inv = (10000.0 ** (-np.arange(128, dtype=np.float32) / 128)).astype(np.float32)
    ang = (np.arange(S, dtype=np.float32)[None, :] * inv[:, None]).astype(np.float32)
    c["cosT"] = np.cos(ang).astype(np.float32)
    c["sinT"] = np.sin(ang).astype(np.float32)
    sel = np.zeros((16, 16, 128), np.float32)
    for h in range(16):
        sel[h, h, :] = 1.0
    c["selmat"] = sel
    return c


def ret_consts(j):
    g = np.float32(1.0) - np.exp2(np.float32(-5.0 - j)).astype(np.float32)
    lg = np.log1p(-np.exp2(np.float32(-5.0 - j))).astype(np.float32)
    i = np.arange(128, dtype=np.float32)
    diff = i[None, :] - i[:, None]
    decT = np.where(diff >= 0, np.exp(lg * np.maximum(diff, 0)), 0.0).astype(np.float32)
    qdec = np.broadcast_to(np.exp(lg * (i + 1.0))[None, :], (128, 128)).astype(np.float32).copy()
    sc = np.stack([np.exp(lg * (127.0 - i)), np.full(128, np.exp(lg * 128.0))], 1).astype(np.float32)
    return decT, qdec, sc


def kernel_fused(**inp):
    x = np.asarray(inp["x"], np.float32)
    S = x.shape[1]
    TOK = S // NCORE
    f = lambda a: np.ascontiguousarray(np.asarray(a, np.float32))
    ev_w_in = np.asarray(inp["ev_w_in"][0])
    od_w_in = np.asarray(inp["od_w_in"][0])
    ev_w_out = np.asarray(inp["ev_w_out"][0])
    od_w_out = np.asarray(inp["od_w_out"][0])
    conv_w = np.asarray(inp["od_conv_w"][0])
    conv_b = np.asarray(inp["od_conv_b"][0])
    consts = host_consts(S)
    CW = min(S, 2048) // NCORE
    l = np.arange(TOK)
    gidx = [(l // CW) * (NCORE * CW) + j * CW + (l % CW) for j in range(NCORE)]
    maps = []
    for j in range(NCORE):
        m = dict(consts)
        m["x"] = f(x[0, gidx[j]])
        cols = np.concatenate([
            np.arange(j * 256, (j + 1) * 256), 2048 + np.arange(j * 256, (j + 1) * 256),
            4096 + np.arange(j * 256, (j + 1) * 256), 6144 + np.arange(j * 256, (j + 1) * 256),
            8192 + np.arange(j * 256, (j + 1) * 256), 10240 + np.arange(j * 256, (j + 1) * 256),
            12288 + np.arange(j * 256, (j + 1) * 256), 14336 + np.arange(2 * j, 2 * j + 2)])
        m["w0"] = f(ev_w_in[:, cols])
        cols1 = np.concatenate([
            np.arange(j * 1024, (j + 1) * 1024), 8192 + np.arange(j * 1024, (j + 1) * 1024),
            16384 + np.arange(j * 128, (j + 1) * 128), 16384 + 1024 + np.arange(j * 128, (j + 1) * 128),
            18432 + np.arange(j * 16, (j + 1) * 16)])
        m["w1"] = f(od_w_in[:, cols1])
        rows0 = np.concatenate([np.arange(j * 256, (j + 1) * 256), 2048 + np.arange(j * 256, (j + 1) * 256)])
        m["wout0"] = f(ev_w_out[rows0])
        m["wout1"] = f(od_w_out[j * 1024:(j + 1) * 1024])
        for nm, key, R in (("wq_s", "xa_wq", 512), ("wk_s", "xa_wk", 512), ("wv_s", "xa_wv", 512), ("wo_s", "xa_wo", 64),
                           ("wg_s", "ffn_w_gate", 512), ("wu_s", "ffn_w_up", 512), ("wd_s", "ffn_w_down", 1376)):
            m[nm] = f(np.asarray(inp[key])[:, j * R:(j + 1) * R])
        m["mem"] = f(inp["mem"][0])
        m["mem_norm"] = f(inp["mem_norm"])
        m["ev_norm"] = f(inp["ev_mix_norm"][0])
        m["od_norm"] = f(inp["od_mix_norm"][0])
        m["xa_norm"] = f(inp["xa_norm"])
        m["ffn_norm"] = f(inp["ffn_norm"])
        m["final_norm"] = f(inp["final_norm"])
        decT, qdec, sc = ret_consts(j)
        m["ret_decT"], m["ret_qdec"], m["ret_sc"] = decT, qdec, sc
        bfv = np.asarray(inp["ev_b_f"][0], np.float32)[2 * j:2 * j + 2]
        m["fox_b"] = f(np.broadcast_to(bfv[None, :], (128, 2)))
        ch = np.concatenate([j * 1024 + np.arange(1024), 8192 + j * 128 + np.arange(128), 8192 + 1024 + j * 128 + np.arange(128)])
        cwj = conv_w[:, ch]
        m["m_convw"] = f(cwj.T.reshape(10, 128, 4).transpose(1, 0, 2))
        m["m_convb"] = f(conv_b[ch].reshape(10, 128).T)
        hsl = slice(j * 16, (j + 1) * 16)
        m["m_dtb"] = f(np.asarray(inp["od_dt_bias"][0])[hsl].reshape(16, 1))
        m["m_alog"] = f(np.broadcast_to(np.asarray(inp["od_a_log"][0])[hsl][None, :], (128, 16)))
        m["m_dskip"] = f(np.broadcast_to(np.asarray(inp["od_d_skip"][0])[hsl][None, :], (128, 16)))
        m["m_gnw"] = f(np.asarray(inp["od_gn_w"][0])[j * 1024:(j + 1) * 1024])
        maps.append(m)
    nc = build_full(S)
    res = run_bass_kernel_spmd(nc, maps, core_ids=list(range(NCORE)))
    out = np.empty((S, D), np.float32)
    for j in range(NCORE):
        out[gidx[j]] = np.asarray(res.results[j]["x_out"], np.float32)
    global LAST_RES
    LAST_RES = res
    return out[None]


CONST_SHAPES = {
    "ident": ([128, 128], BF16), "identf": ([128, 128], F32), "ones": ([128, 128], BF16),
    "onesf": ([128, 128], F32), "triu": ([128, 128], F32), "sltri": ([128, 128], F32),
    "trineg": ([128, 128], F32), "tri01": ([128, 128], F32), "selmat": ([16, 16, 128], F32),
    "ret_decT": ([128, 128], F32), "ret_qdec": ([128, 128], F32), "ret_sc": ([128, 2], F32),
    "fox_b": ([128, 2], F32), "m_convw": ([128, 10, 4], F32), "m_convb": ([128, 10], F32),
    "m_dtb": ([16, 1], F32), "m_alog": ([128, 16], F32), "m_dskip": ([128, 16], F32), "m_gnw": ([1024], F32),
    "mem": ([256, D], F32), "mem_norm": ([D], F32), "xa_norm": ([2, D], F32), "ffn_norm": ([2, D], F32),
    "final_norm": ([D], F32), "g": ([D], F32),
}


def _declare(cx, names, extra):
    K = {}
    for n in names:
        shape, dt = CONST_SHAPES[n]
        K[n] = cx.dram_in(n, shape, dt)
    for n, (shape, dt) in extra.items():
        K[n] = cx.dram_in(n, shape, dt)
    return K


def build_p1(TOK, G):
    nc = bass.Bass("TRN2", target_bir_lowering=False)
    cx = Cx(nc)
    K = _declare(cx, ["ident", "g"], {"x": ([TOK, D], F32)})
    hs = cx.dram_out("hs", [D, TOK], BF16)
    stage_norm_gather(cx, K, K["x"], Trk(), K["g"], hs, Trk(multi=True), None, None, TOK, G, gather=False)
    cx.P.drain("sp")
    cx.P.drain("pool")
    cx.P.emit({})
    return nc


def build_p2(S, TOK, G, layer):
    nc = bass.Bass("TRN2", target_bir_lowering=False)
    cx = Cx(nc)
    if layer == 0:
        K = _declare(cx, ["ident", "identf", "triu", "sltri", "trineg", "ret_decT", "ret_qdec", "ret_sc", "fox_b"],
                     {"hg": ([NCORE * D, TOK], BF16), "w": ([D, W0C], F32), "cosT": ([128, S], F32), "sinT": ([128, S], F32)})
        wc, mw = W0C, 512
    else:
        K = _declare(cx, ["ident", "identf", "triu", "onesf", "selmat", "tri01", "m_convw", "m_convb", "m_dtb", "m_alog",
                          "m_dskip", "m_gnw"], {"hg": ([NCORE * D, TOK], BF16), "w": ([D, W1C], F32)})
        wc, mw = W1C, 1024
    mix = cx.dram_out("mix", [S, mw], BF16)
    stg = cx.ring(3, [128, 4096], BF16)
    wb = cx.dram("wb", [D, wc], BF16)
    kwb = Trk(multi=True)
    wprep(cx, stg, K["w"], wb, kwb)
    kmix = Trk(multi=True)
    cx.phase_barrier()
    if layer == 0:
        d0 = {"k": Trk(multi=True),
              "rqT": cx.dram("rqT", [256, S], BF16), "rkT": cx.dram("rkT", [256, S], BF16),
              "rv": cx.dram("rv", [S, 256], BF16), "rg": cx.dram("rg", [S, 256], BF16),
              "fqT": cx.dram("fqT", [2, 128, S], BF16), "fkT": cx.dram("fkT", [2, 128, S], BF16),
              "fv": cx.dram("fv", [2, S, 128], BF16), "ffT": cx.dram("ffT", [2, S], F32),
              "cn": cx.dram("cn", [2, S], F32)}
        stage_a0(cx, K, S, TOK, G, K["hg"], Trk(), wb, kwb, d0)
        cx.phase_barrier()
        stage_m0(cx, K, S, d0, mix, kmix)
    else:
        d1 = {"k": Trk(multi=True),
              "z": cx.dram("z", [S, 1024], BF16), "xbc": cx.dram("xbc", [1280, S], BF16),
              "dtT": cx.dram("dtT", [16, S], F32)}
        stage_a1(cx, K, S, TOK, G, K["hg"], Trk(), wb, kwb, d1)
        cx.phase_barrier()
        stage_m1(cx, K, S, d1, mix, kmix)
    cx.P.drain("sp")
    cx.P.drain("pool")
    cx.P.emit({})
    return nc


def build_p3(TOK, G, layer):
    final = layer == 1
    mwt = 4096 if layer == 0 else 8192
    nc = bass.Bass("TRN2", target_bir_lowering=False)
    cx = Cx(nc)
    names = ["ident", "ones", "mem", "mem_norm", "xa_norm", "ffn_norm"] + (["final_norm"] if final else ["g"])
    K = _declare(cx, names, {"x_in": ([TOK, D], F32), "mixed": ([TOK, mwt], BF16), "wout": ([mwt, D], F32),
                             "wq": ([D, XAW], F32), "wk": ([D, XAW], F32), "wv": ([D, XAW], F32), "wo": ([XAW, D], F32),
                             "wg": ([D, DFF], F32), "wu": ([D, DFF], F32), "wd": ([DFF, D], F32)})
    x_out = cx.dram_out("x_out", [TOK, D], F32)
    K["x_a"] = cx.dram("x_a", [TOK, D], F32)
    K["x_b"] = cx.dram("x_b", [TOK, D], F32)
    if final:
        K["x_c1"] = cx.dram("x_c1", [TOK, D], F32)
    else:
        K["x_c0"] = x_out
        hs = cx.dram_out("hs", [D, TOK], BF16)
    stg = cx.ring(3, [128, 4096], BF16)
    W = {}
    for nm in ("wout", "wq", "wk", "wv", "wo", "wg", "wu", "wd"):
        R, N = K[nm].shape
        b = cx.dram(nm + "_b", [R, N], BF16)
        kb = Trk(multi=True)
        wprep(cx, stg, K[nm], b, kb)
        W["%s%d" % (nm, layer)] = (b, kb)
    cx.phase_barrier()
    delta = cx.dram("delta", [TOK, D], F32)
    kdelta = stage_wout(cx, K, TOK, K["mixed"], Trk(), mwt, W["wout%d" % layer][0], W["wout%d" % layer][1],
                        delta, None, None, rs=False, nring=(2 if layer == 0 else 1))
    cx.phase_barrier()
    kxo = Trk(multi=True)
    stage_c(cx, K, TOK, G, layer, delta, kdelta, K["x_in"], Trk(), x_out, kxo, W, final)
    if not final:
        cx.phase_barrier()
        stage_norm_gather(cx, K, x_out, kxo, K["g"], hs, Trk(multi=True), None, None, TOK, G, gather=False)
    cx.P.drain("sp")
    cx.P.drain("pool")
    cx.P.emit({})
    return nc


def kernel_multi(**inp):
    bf = ml_dtypes.bfloat16
    x = np.asarray(inp["x"], np.float32)
    S = x.shape[1]
    TOK = S // NCORE
    G = min(512, TOK)
    cores = list(range(NCORE))
    f = lambda a: np.ascontiguousarray(np.asarray(a, np.float32))
    consts = host_consts(S)
    ev_w_in = np.asarray(inp["ev_w_in"][0])
    od_w_in = np.asarray(inp["od_w_in"][0])
    conv_w = np.asarray(inp["od_conv_w"][0])
    conv_b = np.asarray(inp["od_conv_b"][0])

    def pick(names):
        return {n: consts[n] for n in names}

    nc = build_p1(TOK, G)
    maps = [dict(pick(["ident"]), g=f(inp["ev_mix_norm"][0]), x=f(x[0, j * TOK:(j + 1) * TOK])) for j in cores]
    r = run_bass_kernel_spmd(nc, maps, core_ids=cores)
    hg = np.ascontiguousarray(np.concatenate([np.asarray(r.results[j]["hs"]) for j in cores], 0))
    nc = build_p2(S, TOK, G, 0)
    maps = []
    for j in cores:
        m = pick(["ident", "identf", "triu", "sltri", "trineg", "cosT", "sinT"])
        m["hg"] = hg
        cols = np.concatenate([
            np.arange(j * 256, (j + 1) * 256), 2048 + np.arange(j * 256, (j + 1) * 256),
            4096 + np.arange(j * 256, (j + 1) * 256), 6144 + np.arange(j * 256, (j + 1) * 256),
            8192 + np.arange(j * 256, (j + 1) * 256), 10240 + np.arange(j * 256, (j + 1) * 256),
            12288 + np.arange(j * 256, (j + 1) * 256), 14336 + np.arange(2 * j, 2 * j + 2)])
        m["w"] = f(ev_w_in[:, cols])
        m["ret_decT"], m["ret_qdec"], m["ret_sc"] = ret_consts(j)
        bfv = np.asarray(inp["ev_b_f"][0], np.float32)[2 * j:2 * j + 2]
        m["fox_b"] = f(np.broadcast_to(bfv[None, :], (128, 2)))
        maps.append(m)
    r = run_bass_kernel_spmd(nc, maps, core_ids=cores)
    mixes = [np.asarray(r.results[j]["mix"]) for j in cores]
    rows0 = np.concatenate([np.concatenate([np.arange(j * 256, (j + 1) * 256), 2048 + np.arange(j * 256, (j + 1) * 256)])
                            for j in cores])
    wout0 = f(np.asarray(inp["ev_w_out"][0])[rows0])

    def c_maps(layer, x_shards, mixes, wout, extra):
        maps = []
        for i in cores:
            m = pick(["ident", "ones"])
            m["x_in"] = x_shards[i]
            m["mixed"] = np.ascontiguousarray(np.concatenate([mixes[j][i * TOK:(i + 1) * TOK] for j in cores], 1))
            m["wout"] = wout
            for nm, key in (("wq", "xa_wq"), ("wk", "xa_wk"), ("wv", "xa_wv"), ("wo", "xa_wo"),
                            ("wg", "ffn_w_gate"), ("wu", "ffn_w_up"), ("wd", "ffn_w_down")):
                m[nm] = f(np.asarray(inp[key])[layer])
            m["mem"] = f(inp["mem"][0])
            m["mem_norm"] = f(inp["mem_norm"])
            m["xa_norm"] = f(inp["xa_norm"])
            m["ffn_norm"] = f(inp["ffn_norm"])
            m.update(extra)
            maps.append(m)
        return maps
    nc = build_p3(TOK, G, 0)
    maps = c_maps(0, [f(x[0, j * TOK:(j + 1) * TOK]) for j in cores], mixes, wout0, {"g": f(inp["od_mix_norm"][0])})
    r = run_bass_kernel_spmd(nc, maps, core_ids=cores)
    x1 = [np.asarray(r.results[j]["x_out"], np.float32) for j in cores]
    hg = np.ascontiguousarray(np.concatenate([np.asarray(r.results[j]["hs"]) for j in cores], 0))
    nc = build_p2(S, TOK, G, 1)
    maps = []
    for j in cores:
        m = pick(["ident", "identf", "triu", "onesf", "selmat", "tri01"])
        m["hg"] = hg
        cols1 = np.concatenate([
            np.arange(j * 1024, (j + 1) * 1024), 8192 + np.arange(j * 1024, (j + 1) * 1024),
            16384 + np.arange(j * 128, (j + 1) * 128), 16384 + 1024 + np.arange(j * 128, (j + 1) * 128),
            18432 + np.arange(j * 16, (j + 1) * 16)])
        m["w"] = f(od_w_in[:, cols1])
        ch = np.concatenate([j * 1024 + np.arange(1024), 8192 + j * 128 + np.arange(128), 8192 + 1024 + j * 128 + np.arange(128)])
        m["m_convw"] = f(conv_w[:, ch].T.reshape(10, 128, 4).transpose(1, 0, 2))
        m["m_convb"] = f(conv_b[ch].reshape(10, 128).T)
        hsl = slice(j * 16, (j + 1) * 16)
        m["m_dtb"] = f(np.asarray(inp["od_dt_bias"][0])[hsl].reshape(16, 1))
        m["m_alog"] = f(np.broadcast_to(np.asarray(inp["od_a_log"][0])[hsl][None, :], (128, 16)))
        m["m_dskip"] = f(np.broadcast_to(np.asarray(inp["od_d_skip"][0])[hsl][None, :], (128, 16)))
        m["m_gnw"] = f(np.asarray(inp["od_gn_w"][0])[j * 1024:(j + 1) * 1024])
        maps.append(m)
    r = run_bass_kernel_spmd(nc, maps, core_ids=cores)
    mixes = [np.asarray(r.results[j]["mix"]) for j in cores]
    nc = build_p3(TOK, G, 1)
    maps = c_maps(1, x1, mixes, f(inp["od_w_out"][0]), {"final_norm": f(inp["final_norm"])})
    r = run_bass_kernel_spmd(nc, maps, core_ids=cores)
    out = np.concatenate([np.asarray(r.results[j]["x_out"], np.float32) for j in cores], 0)
    return out[None]


def kernel(**inp):
    return kernel_multi(**inp)
```

```python
import numpy as np
import ml_dtypes
import concourse.bass as bass
import concourse.mybir as mybir
from concourse.bass_utils import run_bass_kernel_spmd

F32 = mybir.dt.float32
BF16 = mybir.dt.bfloat16
U8 = mybir.dt.uint8
AF = mybir.ActivationFunctionType
ALU = mybir.AluOpType

NCORE = 8
D = 4096
KD = D // 128
EPS = 1e-6
NSLOT = 8
KC = 8
DFF = 11008
XAW = 512
W0C = 1794
W1C = 2320
ARENA = 204 * 1024
RG = [list(range(NCORE))]


class Trk:
    __slots__ = ("w", "r", "multi", "ws")

    def __init__(self, multi=False):
        self.w = None
        self.r = {}
        self.multi = multi
        self.ws = {}


class Op:
    __slots__ = ("fn", "waits", "inc", "dma")

    def __init__(self, fn, waits, dma=None):
        self.fn = fn
        self.waits = waits
        self.inc = False
        self.dma = dma


class Prog:
    ENGS = ("pe", "act", "dve", "pool", "sp")

    def __init__(self, nc):
        self.nc = nc
        self.ops = {e: [] for e in self.ENGS}
        self.seen = {e: {} for e in self.ENGS}
        self.ndma = {"sp": 0, "pool": 0}
        self.ncc = 0
        self.eng_obj = {"pe": nc.tensor, "act": nc.scalar, "dve": nc.vector,
                        "pool": nc.gpsimd, "sp": nc.sync}

    def _deps(self, eng, reads, writes):
        deps = {}

        def add(c, i):
            if deps.get(c, -1) < i:
                deps[c] = i
        for t in reads:
            if t.multi:
                for c, i in t.ws.items():
                    add(c, i)
            if t.w is not None:
                add(*t.w)
        for t in writes:
            if t.multi:
                continue
            if t.w is not None:
                add(*t.w)
            for c, i in t.r.items():
                add(c, i)
        waits = []
        seen = self.seen[eng]
        for c, i in deps.items():
            if c == "pe" and eng == "pe":
                continue
            if seen.get(c, -1) >= i:
                continue
            seen[c] = i
            waits.append((c, i))
            if not isinstance(c, tuple):
                self.ops[c][i].inc = True
        return waits

    def op(self, eng, fn, reads=(), writes=()):
        waits = self._deps(eng, reads, writes)
        idx = len(self.ops[eng])
        self.ops[eng].append(Op(fn, waits))
        for t in reads:
            t.r[eng] = idx
        for t in writes:
            t.w = (eng, idx)
            t.r = {}
            t.ws = {}

    def dma(self, q, fn, reads=(), writes=()):
        n = self.ndma[q]
        self.ndma[q] = n + 1
        chan = (q, n % NSLOT)
        k = n // NSLOT
        waits = self._deps(q, reads, writes)
        if k > 0:
            seen = self.seen[q]
            if seen.get(chan, -1) < k - 1:
                seen[chan] = k - 1
                waits.append((chan, k - 1))
        self.ops[q].append(Op(fn, waits, dma=(chan, 16)))
        for t in reads:
            t.r[chan] = k
        for t in writes:
            if t.multi:
                t.ws[chan] = k
            else:
                t.w = (chan, k)
                t.r = {}

    def coll(self, fn, reads=(), writes=()):
        chan = ("cc", self.ncc)
        self.ncc += 1
        waits = self._deps("pool", reads, writes)
        self.ops["pool"].append(Op(fn, waits, dma=(chan, 1)))
        for t in reads:
            t.r[chan] = 0
        for t in writes:
            t.w = (chan, 0)
            t.r = {}
            t.ws = {}

    def drain(self, q):
        n = self.ndma[q]
        waits = []
        seen = self.seen[q]
        for i in range(max(0, n - NSLOT), n):
            chan = (q, i % NSLOT)
            k = i // NSLOT
            if seen.get(chan, -1) < k:
                seen[chan] = k
                waits.append((chan, k))
        self.ops[q].append(Op(None, waits))

    def emit(self, marks):
        nc = self.nc
        import contextlib
        with contextlib.ExitStack() as st:
            esem = {e: st.enter_context(nc.semaphore("s_" + e)) for e in self.ENGS}
            dsem = {}
            for q in self.ndma:
                for s in range(NSLOT):
                    dsem[(q, s)] = st.enter_context(nc.semaphore("d_%s%d" % (q, s)))
            for i in range(self.ncc):
                dsem[("cc", i)] = st.enter_context(nc.semaphore("cc%d" % i))
            cum = {}
            for e in self.ENGS:
                c = 0
                arr = []
                for o in self.ops[e]:
                    if o.inc and o.dma is None:
                        c += 1
                    arr.append(c)
                cum[e] = arr

            def resolve(c, i):
                if isinstance(c, tuple):
                    if c[0] == "cc":
                        return dsem[c], 1
                    return dsem[c], 16 * (i + 1)
                return esem[c], cum[c][i]

            block = st.enter_context(nc.Block())

            def run(e):
                eo = self.eng_obj[e]
                for o in self.ops[e]:
                    for (c, i) in o.waits:
                        s, v = resolve(c, i)
                        eo.wait_ge(s, v)
                    if o.fn is None:
                        continue
                    ins = o.fn()
                    if o.dma is not None:
                        ins.then_inc(dsem[o.dma[0]], o.dma[1])
                    elif o.inc:
                        ins.then_inc(esem[e], 1)

            @block.tensor
            def _(e):
                run("pe")

            @block.scalar
            def _(e):
                run("act")

            @block.vector
            def _(e):
                run("dve")

            @block.gpsimd
            def _(e):
                run("pool")

            @block.sync
            def _(e):
                run("sp")


class Buf:
    __slots__ = ("t", "k")

    def __init__(self, t, k=None):
        self.t = t
        self.k = k or Trk()


class Ring:
    def __init__(self, bufs):
        self.bufs = bufs
        self.i = 0

    def get(self):
        b = self.bufs[self.i % len(self.bufs)]
        self.i += 1
        return b


class Cx:
    def __init__(self, nc):
        self.nc = nc
        self.P = Prog(nc)
        self.allbanks = [Buf(nc.alloc_psum_tensor("bank%d" % i, [128, 512], F32)) for i in range(8)]
        self.banks = Ring(self.allbanks)
        self.arena = nc.alloc_sbuf_tensor("arena", [128, ARENA], U8)
        self.off = 0
        self.mk = {e: Buf(nc.alloc_sbuf_tensor("mk_" + e, [128, 8], F32)) for e in ("act", "dve", "pool")}
        self.nin = 0

    def sb(self, shape, dt, name=None):
        esz = 4 if dt == F32 else 2
        n = 1
        for s in shape[1:]:
            n *= s
        nbytes = (n * esz + 63) // 64 * 64
        assert self.off + nbytes <= ARENA, ("SBUF arena overflow", name, self.off, nbytes)
        v = self.arena[0:shape[0], self.off:self.off + n * esz].bitcast(dt)
        self.off += nbytes
        if len(shape) == 3:
            v = v.rearrange("p (a b) -> p a b", b=shape[2])
        return Buf(v)

    def ring(self, n, shape, dt, name=None):
        return Ring([self.sb(shape, dt, name) for _ in range(n)])

    def dram_in(self, name, shape, dt):
        return self.nc.dram_tensor(name, list(shape), dt, kind="ExternalInput").ap()

    def dram_out(self, name, shape, dt):
        return self.nc.dram_tensor(name, list(shape), dt, kind="ExternalOutput").ap()

    def dram(self, name, shape, dt):
        return self.nc.dram_tensor(name, list(shape), dt, kind="Internal").ap()

    def load(self, buf, src, q="sp"):
        nc = self.nc
        if q == "sp":
            self.P.dma("sp", lambda: nc.sync.dma_start(out=buf.t, in_=src), writes=[buf.k])
        else:
            self.P.dma("pool", lambda: nc.gpsimd.dma_start(out=buf.t, in_=src), writes=[buf.k])
        return buf

    def phase_barrier(self):
        nc, P = self.nc, self.P
        P.drain("sp")
        P.drain("pool")
        allt = [b.k for b in self.allbanks] + [b.k for b in self.mk.values()]
        t_sp, t_pe = Trk(), Trk()
        bk = self.allbanks[0]
        P.op("pe", lambda: nc.tensor.matmul(bk.t[0:8, 0:8], self.mk["dve"].t[:, 0:8], self.mk["dve"].t[:, 0:8],
                                             start=True, stop=True), reads=[self.mk["dve"].k], writes=[bk.k, t_pe])
        for e in ("act", "dve", "pool"):
            m = self.mk[e]
            eo = {"act": nc.scalar, "dve": nc.vector, "pool": nc.gpsimd}[e]
            if e == "act":
                P.op(e, lambda m=m: nc.scalar.copy(m.t[:, 0:4], m.t[:, 4:8]), writes=[m.k])
            else:
                P.op(e, lambda m=m, eo=eo: eo.memset(m.t[:, 0:4], 0.0), writes=[m.k])
        P.op("sp", lambda: nc.sync.nop(), writes=[t_sp])
        every = [self.mk["act"].k, self.mk["dve"].k, self.mk["pool"].k, t_sp, t_pe]
        P.op("pe", lambda: nc.tensor.matmul(bk.t[0:8, 0:8], self.mk["dve"].t[:, 0:8], self.mk["dve"].t[:, 0:8],
                                             start=True, stop=True), reads=every, writes=[])
        P.op("act", lambda: nc.scalar.copy(self.mk["act"].t[:, 0:4], self.mk["act"].t[:, 4:8]), reads=every)
        P.op("dve", lambda: nc.vector.memset(self.mk["dve"].t[:, 0:4], 0.0), reads=every)
        P.op("pool", lambda: nc.gpsimd.memset(self.mk["pool"].t[:, 0:4], 0.0), reads=every)
        P.op("sp", lambda: nc.sync.nop(), reads=every)
        self.off = 0
        self.banks = Ring(self.allbanks)


def rstd_from_ss(cx, st):
    nc, P = cx.nc, cx.P
    P.op("act", lambda: nc.scalar.activation(out=st.t[:, 1:2], in_=st.t[:, 0:1], func=AF.Ln, bias=EPS),
         reads=[st.k], writes=[st.k])
    P.op("act", lambda: nc.scalar.activation(out=st.t[:, 2:3], in_=st.t[:, 1:2], func=AF.Exp, scale=-0.5),
         reads=[st.k], writes=[st.k])


def norm_rows(cx, res, src_tiles, gbc, hT):
    nc, P = cx.nc, cx.P
    for tt, src in enumerate(src_tiles):
        xt = res["xt"].get()
        xn = res["xn"].get()
        st = res["st"].get()
        cx.load(xt, src)
        P.op("dve", lambda st=st: nc.vector.memset(st.t[:, :], 0.0), writes=[st.k])
        P.op("act", lambda xt=xt, xn=xn, st=st: nc.scalar.activation(
            out=xn.t[:, :], in_=xt.t[:, :], func=AF.Square, scale=float(D) ** -0.5,
            accum_out=st.t[:, 0:1]), reads=[xt.k, st.k], writes=[xn.k, st.k])
        rstd_from_ss(cx, st)
        P.op("dve", lambda xt=xt, xn=xn, st=st: nc.vector.scalar_tensor_tensor(
            out=xn.t[:, :], in0=xt.t[:, :], scalar=st.t[:, 2:3], in1=gbc.t[:, :],
            op0=ALU.mult, op1=ALU.mult), reads=[xt.k, st.k, gbc.k], writes=[xn.k])
        transpose_rows(cx, res, xn, KD, hT, tt)


def transpose_rows(cx, res, xn, kch, hT, tt, k0=0):
    nc, P = cx.nc, cx.P
    ident = res["ident"]
    for g0 in range(0, kch, 8):
        ng = min(8, kch - g0)
        bank = cx.banks.get()
        pv = bank.t[:, :].bitcast(BF16)
        for i in range(ng):
            kc = g0 + i
            P.op("pe", lambda pv=pv, xn=xn, i=i, kc=kc: nc.tensor.transpose(
                pv[:, i * 128:(i + 1) * 128], xn.t[:, kc * 128:(kc + 1) * 128], ident.t[:, :]),
                reads=[xn.k, ident.k], writes=[bank.k])
        src = pv[:, 0:ng * 128].rearrange("p (c t) -> p c t", t=128)
        dst = hT.t[:, k0 + g0:k0 + g0 + ng, tt * 128:(tt + 1) * 128]
        if (g0 // 8) % 2 == 0:
            P.op("act", lambda src=src, dst=dst: nc.scalar.copy(dst, src), reads=[bank.k], writes=[hT.k])
        else:
            P.op("dve", lambda src=src, dst=dst: nc.vector.tensor_copy(dst, src), reads=[bank.k], writes=[hT.k])


def load_w(cx, wt, wdram, wk, k0, nk, col0, ncols, dst_col=0):
    nc, P = cx.nc, cx.P
    src = wdram[k0 * 128:(k0 + nk) * 128, col0:col0 + ncols].rearrange("(c p) n -> p c n", p=128)
    P.dma("sp", lambda: nc.sync.dma_start(out=wt.t[:, 0:nk, dst_col:dst_col + ncols], in_=src),
          reads=[wk], writes=[wt.k])


def gemm(cx, res, mode, A, kch, ntok, ncols, wload, a_off=0):
    nc, P = cx.nc, cx.P
    nb = ntok // 128 if mode == "T" else (ncols + 127) // 128
    banks = [cx.banks.get() for _ in range(nb)]
    for k0 in range(0, kch, KC):
        nk = min(KC, kch - k0)
        wt = res["wt"].get()
        wload(wt, k0, nk)
        for b in range(nb):
            for k in range(nk):
                kk = k0 + k
                if mode == "T":
                    lhsT = A.t[:, a_off + kk, b * 128:(b + 1) * 128]
                    rhs = wt.t[:, k, 0:ncols]
                    out = banks[b].t[:, 0:ncols]
                else:
                    m = min(128, ncols - b * 128)
                    lhsT = wt.t[:, k, b * 128:b * 128 + m]
                    rhs = A.t[:, a_off + kk, 0:ntok]
                    out = banks[b].t[0:m, 0:ntok]
                P.op("pe", lambda out=out, lhsT=lhsT, rhs=rhs, s=(kk == 0), e=(kk == kch - 1):
                     nc.tensor.matmul(out, lhsT, rhs, start=s, stop=e),
                     reads=[A.k, wt.k], writes=[banks[b].k])
    return banks


def resid_out(cx, res, banks, x_src, ksrc, x_dst, kdst, row0, col0, ncols):
    nc, P = cx.nc, cx.P
    for tt, bank in enumerate(banks):
        xb = res["xblk"].get()
        r0 = row0 + tt * 128
        src = x_src[r0:r0 + 128, col0:col0 + ncols]
        dst = x_dst[r0:r0 + 128, col0:col0 + ncols]
        P.dma("sp", lambda xb=xb, src=src: nc.sync.dma_start(out=xb.t[:, 0:ncols], in_=src),
              reads=[ksrc], writes=[xb.k])
        P.op("dve", lambda xb=xb, bank=bank: nc.vector.tensor_tensor(
            out=xb.t[:, 0:ncols], in0=bank.t[:, 0:ncols], in1=xb.t[:, 0:ncols], op=ALU.add),
            reads=[bank.k, xb.k], writes=[xb.k])
        P.dma("pool", lambda xb=xb, dst=dst: nc.gpsimd.dma_start(out=dst, in_=xb.t[:, 0:ncols]),
              reads=[xb.k], writes=[kdst])


def common_res(cx, K):
    res = {}
    res["ident"] = cx.load(cx.sb([128, 128], BF16), K["ident"])
    res["xt"] = cx.ring(1, [128, D], F32, "xt")
    res["xn"] = cx.ring(2, [128, D], BF16, "xn")
    res["st"] = cx.ring(4, [128, 4], F32, "st")
    res["wt"] = cx.ring(4, [128, KC, 512], BF16, "wt")
    res["xblk"] = cx.ring(4, [128, 512], F32, "xblk")
    res["xt2"] = cx.ring(2, [128, 2048], F32, "xt2")
    return res


def wprep(cx, stg, src, dst, kdst):
    nc, P = cx.nc, cx.P
    R, N = src.shape
    for r0 in range(0, R, 128):
        nr = min(128, R - r0)
        for c0 in range(0, N, 4096):
            ncl = min(4096, N - c0)
            b = stg.get()
            s_ap = src[r0:r0 + nr, c0:c0 + ncl]
            d_ap = dst[r0:r0 + nr, c0:c0 + ncl]
            P.dma("pool", lambda b=b, s_ap=s_ap, nr=nr, ncl=ncl: nc.gpsimd.dma_start(out=b.t[0:nr, 0:ncl], in_=s_ap),
                  writes=[b.k])
            P.dma("sp", lambda b=b, d_ap=d_ap, nr=nr, ncl=ncl: nc.sync.dma_start(out=d_ap, in_=b.t[0:nr, 0:ncl]),
                  reads=[b.k], writes=[kdst])


def allgather(cx, src, ksrc, dst, kdst):
    nc, P = cx.nc, cx.P
    P.coll(lambda: nc.gpsimd.collective_compute("AllGather", ALU.bypass, replica_groups=RG, ins=[src], outs=[dst]),
           reads=[ksrc], writes=[kdst])


def stage_norm_gather(cx, K, x_src, kx, gvec, hs, khs, hg, khg, TOK, G, gather=True):
    nc, P = cx.nc, cx.P
    res = common_res(cx, K)
    gbc = cx.sb([128, D], F32)
    cx.load(gbc, gvec.partition_broadcast(128))
    hTr = cx.ring(2, [128, KD, G], BF16)
    for g in range(TOK // G):
        hT = hTr.get()
        tiles = [x_src[g * G + tt * 128:g * G + (tt + 1) * 128, :] for tt in range(G // 128)]
        norm_rows_k(cx, res, tiles, kx, gbc, hT)
        dst = hs[:, g * G:(g + 1) * G].rearrange("(c p) t -> p c t", p=128)
        P.dma("sp", lambda hT=hT, dst=dst: nc.sync.dma_start(out=dst, in_=hT.t[:, :, :]), reads=[hT.k], writes=[khs])
    if gather:
        allgather(cx, hs, khs, hg, khg)


def norm_rows_k(cx, res, tiles, kx, gbc, hT):
    nc, P = cx.nc, cx.P
    for tt, src in enumerate(tiles):
        xt = res["xt"].get()
        xn = res["xn"].get()
        st = res["st"].get()
        P.dma("sp", lambda xt=xt, src=src: nc.sync.dma_start(out=xt.t[:, :], in_=src), reads=[kx], writes=[xt.k])
        P.op("dve", lambda st=st: nc.vector.memset(st.t[:, :], 0.0), writes=[st.k])
        P.op("act", lambda xt=xt, xn=xn, st=st: nc.scalar.activation(
            out=xn.t[:, :], in_=xt.t[:, :], func=AF.Square, scale=float(D) ** -0.5,
            accum_out=st.t[:, 0:1]), reads=[xt.k, st.k], writes=[xn.k, st.k])
        rstd_from_ss(cx, st)
        P.op("dve", lambda xt=xt, xn=xn, st=st: nc.vector.scalar_tensor_tensor(
            out=xn.t[:, :], in0=xt.t[:, :], scalar=st.t[:, 2:3], in1=gbc.t[:, :],
            op0=ALU.mult, op1=ALU.mult), reads=[xt.k, st.k, gbc.k], writes=[xn.k])
        transpose_rows(cx, res, xn, KD, hT, tt)


def seg_map(S, TOK, G, r, gg, CW):
    if CW is None:
        CW = TOK

    def gpos(l):
        return (l // CW) * (NCORE * CW) + r * CW + (l % CW)
    L = min(G, CW)
    segs = [(ls, gpos(gg * G + ls), L) for ls in range(0, G, L)]
    return segs, (lambda tt: gpos(gg * G + tt * 128))


def stage_a0(cx, K, S, TOK, G, hg, khg, w0b, kw0, dd, CW=None):
    nc, P = cx.nc, cx.P
    res = common_res(cx, K)
    hTr = cx.ring(2, [128, KD, G], BF16)
    csr = cx.ring(2, [128, G], F32)
    snr = cx.ring(2, [128, G], F32)
    tmp = cx.ring(4, [128, G], F32)
    ob = cx.ring(4, [128, G], BF16)
    ot = cx.ring(3, [128, 512], BF16)
    of = cx.ring(2, [128, G], F32)
    NTT = G // 128
    kd = dd["k"]
    for r in range(NCORE):
        for gg in range(TOK // G):
            segs, gp = seg_map(S, TOK, G, r, gg, CW)
            hT = hTr.get()
            src = hg[r * D:(r + 1) * D, gg * G:(gg + 1) * G].rearrange("(c p) t -> p c t", p=128)
            P.dma("sp", lambda hT=hT, src=src: nc.sync.dma_start(out=hT.t[:, :, :], in_=src), reads=[khg], writes=[hT.k])
            cs, sn = csr.get(), snr.get()
            for (ls, gs, L) in segs:
                P.dma("sp", lambda cs=cs, ls=ls, gs=gs, L=L: nc.sync.dma_start(out=cs.t[:, ls:ls + L], in_=K["cosT"][:, gs:gs + L]), writes=[cs.k])
                P.dma("sp", lambda sn=sn, ls=ls, gs=gs, L=L: nc.sync.dma_start(out=sn.t[:, ls:ls + L], in_=K["sinT"][:, gs:gs + L]), writes=[sn.k])
            bk = gemm(cx, res, "F", hT, KD, G, 512, lambda wt, k0, nk: load_w(cx, wt, w0b, kw0, k0, nk, 0, 512))
            for qi, (dst, sc) in enumerate(((dd["rqT"], 1.0), (dd["rkT"], 1.0 / 16))):
                b1, b2 = bk[2 * qi], bk[2 * qi + 1]
                t1, t2, t3, t4 = tmp.get(), tmp.get(), tmp.get(), tmp.get()
                for (tb, bb, tab) in ((t1, b1, cs), (t2, b2, sn), (t3, b1, sn), (t4, b2, cs)):
                    P.op("dve", lambda tb=tb, bb=bb, tab=tab, sc=sc: nc.vector.scalar_tensor_tensor(
                        out=tb.t[:, :], in0=bb.t[:, 0:G], scalar=sc, in1=tab.t[:, :], op0=ALU.mult, op1=ALU.mult),
                        reads=[bb.k, tab.k], writes=[tb.k])
                o1, o2 = ob.get(), ob.get()
                P.op("pool", lambda o1=o1, t1=t1, t2=t2: nc.gpsimd.tensor_tensor(
                    out=o1.t[:, :], in0=t1.t[:, :], in1=t2.t[:, :], op=ALU.subtract), reads=[t1.k, t2.k], writes=[o1.k])
                P.op("pool", lambda o2=o2, t3=t3, t4=t4: nc.gpsimd.tensor_tensor(
                    out=o2.t[:, :], in0=t3.t[:, :], in1=t4.t[:, :], op=ALU.add), reads=[t3.k, t4.k], writes=[o2.k])
                for hh, oo in ((0, o1), (1, o2)):
                    for (ls, gs, L) in segs:
                        d_ap = dst[hh * 128:(hh + 1) * 128, gs:gs + L]
                        P.dma("pool", lambda oo=oo, d_ap=d_ap, ls=ls, L=L: nc.gpsimd.dma_start(out=d_ap, in_=oo.t[:, ls:ls + L]), reads=[oo.k], writes=[kd])
            bk = gemm(cx, res, "T", hT, KD, G, 512, lambda wt, k0, nk: load_w(cx, wt, w0b, kw0, k0, nk, 512, 512))
            for tt in range(NTT):
                o = ot.get()
                P.op("act", lambda o=o, b=bk[tt]: nc.scalar.copy(o.t[:, :], b.t[:, :]), reads=[bk[tt].k], writes=[o.k])
                rows = slice(gp(tt), gp(tt) + 128)
                P.dma("pool", lambda o=o, rows=rows: nc.gpsimd.dma_start(out=dd["rv"][rows, :], in_=o.t[:, 0:256]), reads=[o.k], writes=[kd])
                P.dma("pool", lambda o=o, rows=rows: nc.gpsimd.dma_start(out=dd["rg"][rows, :], in_=o.t[:, 256:512]), reads=[o.k], writes=[kd])
            bk = gemm(cx, res, "F", hT, KD, G, 512, lambda wt, k0, nk: load_w(cx, wt, w0b, kw0, k0, nk, 1024, 512))
            for i in range(4):
                o = ob.get()
                P.op("act", lambda o=o, b=bk[i]: nc.scalar.copy(o.t[:, :], b.t[:, 0:G]), reads=[bk[i].k], writes=[o.k])
                for (ls, gs, L) in segs:
                    dst = (dd["fqT"] if i < 2 else dd["fkT"])[i % 2, :, gs:gs + L]
                    P.dma("pool", lambda o=o, dst=dst, ls=ls, L=L: nc.gpsimd.dma_start(out=dst, in_=o.t[:, ls:ls + L]), reads=[o.k], writes=[kd])
            bk = gemm(cx, res, "T", hT, KD, G, 256, lambda wt, k0, nk: load_w(cx, wt, w0b, kw0, k0, nk, 1536, 256))
            for tt in range(NTT):
                o = ot.get()
                P.op("act", lambda o=o, b=bk[tt]: nc.scalar.copy(o.t[:, 0:256], b.t[:, 0:256]), reads=[bk[tt].k], writes=[o.k])
                rows = slice(gp(tt), gp(tt) + 128)
                for hd in range(2):
                    P.dma("pool", lambda o=o, rows=rows, hd=hd: nc.gpsimd.dma_start(
                        out=dd["fv"][hd, rows, :], in_=o.t[:, hd * 128:(hd + 1) * 128]), reads=[o.k], writes=[kd])
            bk = gemm(cx, res, "F", hT, KD, G, 2, lambda wt, k0, nk: load_w(cx, wt, w0b, kw0, k0, nk, 1792, 2))
            o = of.get()
            P.op("act", lambda o=o, b=bk[0]: nc.scalar.copy(o.t[0:2, :], b.t[0:2, 0:G]), reads=[bk[0].k], writes=[o.k])
            for (ls, gs, L) in segs:
                P.dma("pool", lambda o=o, ls=ls, gs=gs, L=L: nc.gpsimd.dma_start(out=dd["ffT"][:, gs:gs + L], in_=o.t[0:2, ls:ls + L]), reads=[o.k], writes=[kd])


def stage_m0(cx, K, S, dd, mix, kmix):
    nc, P = cx.nc, cx.P
    kd = dd["k"]
    ident = cx.load(cx.sb([128, 128], BF16), K["ident"])
    identf = cx.load(cx.sb([128, 128], F32), K["identf"])
    decT = cx.load(cx.sb([128, 128], F32), K["ret_decT"])
    qdec = cx.load(cx.sb([128, 128], F32), K["ret_qdec"])
    rsc = cx.load(cx.sb([128, 2], F32), K["ret_sc"])
    stf = cx.sb([128, 512], F32)
    stb = cx.sb([128, 512], BF16)
    P.op("dve", lambda: nc.vector.memset(stf.t[:, :], 0.0), writes=[stf.k])
    P.op("dve", lambda: nc.vector.memset(stb.t[:, :], 0.0), writes=[stb.k])
    qTr = cx.ring(2, [128, 2, 512], BF16)
    kTr = cx.ring(2, [128, 2, 512], BF16)
    vr = cx.ring(2, [128, 4, 256], BF16)
    gr = cx.ring(2, [128, 4, 256], BF16)
    sgr = cx.ring(2, [128, 4, 256], F32)
    kdr = cx.ring(2, [128, 256], BF16)
    pmr = cx.ring(2, [128, 128], BF16)
    qdr = cx.ring(2, [128, 2, 128], BF16)
    jk = cx.ring(2, [128, 256], F32)
    str_ = cx.ring(4, [128, 4], F32)
    osr = cx.ring(2, [128, 4, 256], BF16)
    for sc in range(S // 512):
        t0 = sc * 512
        qT, kT, v, g, sg = qTr.get(), kTr.get(), vr.get(), gr.get(), sgr.get()
        P.dma("sp", lambda qT=qT, t0=t0: nc.sync.dma_start(
            out=qT.t[:, :, :], in_=dd["rqT"][:, t0:t0 + 512].rearrange("(c p) t -> p c t", p=128)), reads=[kd], writes=[qT.k])
        P.dma("sp", lambda kT=kT, t0=t0: nc.sync.dma_start(
            out=kT.t[:, :, :], in_=dd["rkT"][:, t0:t0 + 512].rearrange("(c p) t -> p c t", p=128)), reads=[kd], writes=[kT.k])
        P.dma("sp", lambda v=v, t0=t0: nc.sync.dma_start(
            out=v.t[:, :, :], in_=dd["rv"][t0:t0 + 512, :].rearrange("(c p) e -> p c e", p=128)), reads=[kd], writes=[v.k])
        P.dma("sp", lambda g=g, t0=t0: nc.sync.dma_start(
            out=g.t[:, :, :], in_=dd["rg"][t0:t0 + 512, :].rearrange("(c p) e -> p c e", p=128)), reads=[kd], writes=[g.k])
        P.op("act", lambda sg=sg, g=g: nc.scalar.activation(out=sg.t[:, :, :], in_=g.t[:, :, :], func=AF.Silu),
             reads=[g.k], writes=[sg.k])
        os_ = osr.get()
        for c in range(4):
            cols = slice(c * 128, (c + 1) * 128)
            bt = cx.banks.get()
            pv = bt.t[:, :].bitcast(BF16)
            for dc in range(2):
                P.op("pe", lambda pv=pv, kT=kT, dc=dc, cols=cols: nc.tensor.transpose(
                    pv[:, dc * 128:(dc + 1) * 128], kT.t[:, dc, cols], ident.t[:, :]), reads=[kT.k, ident.k], writes=[bt.k])
            kdec = kdr.get()
            P.op("act", lambda kdec=kdec, pv=pv: nc.scalar.activation(
                out=kdec.t[:, :], in_=pv[:, 0:256], func=AF.Copy, scale=rsc.t[:, 0:1]), reads=[bt.k, rsc.k], writes=[kdec.k])
            bs = cx.banks.get()
            for dc in range(2):
                P.op("pe", lambda bs=bs, kT=kT, qT=qT, dc=dc, cols=cols: nc.tensor.matmul(
                    bs.t[:, 0:128], kT.t[:, dc, cols], qT.t[:, dc, cols], start=(dc == 0), stop=(dc == 1)),
                    reads=[kT.k, qT.k], writes=[bs.k])
            pm = pmr.get()
            P.op("dve", lambda pm=pm, bs=bs: nc.vector.tensor_tensor(
                out=pm.t[:, :], in0=bs.t[:, 0:128], in1=decT.t[:, :], op=ALU.mult), reads=[bs.k, decT.k], writes=[pm.k])
            qd = qdr.get()
            for dc in range(2):
                P.op("pool", lambda qd=qd, qT=qT, dc=dc, cols=cols: nc.gpsimd.tensor_tensor(
                    out=qd.t[:, dc, :], in0=qT.t[:, dc, cols], in1=qdec.t[:, :], op=ALU.mult), reads=[qT.k, qdec.k], writes=[qd.k])
            bo = cx.banks.get()
            P.op("pe", lambda bo=bo, pm=pm, v=v, c=c: nc.tensor.matmul(
                bo.t[:, 0:256], pm.t[:, :], v.t[:, c, :], start=True, stop=False), reads=[pm.k, v.k], writes=[bo.k])
            for dc in range(2):
                P.op("pe", lambda bo=bo, qd=qd, dc=dc: nc.tensor.matmul(
                    bo.t[:, 0:256], qd.t[:, dc, :], stb.t[:, dc * 256:(dc + 1) * 256], start=False, stop=(dc == 1)),
                    reads=[qd.k, stb.k], writes=[bo.k])
            bu = cx.banks.get()
            for dc in range(2):
                P.op("pe", lambda bu=bu, kdec=kdec, v=v, c=c, dc=dc: nc.tensor.matmul(
                    bu.t[:, dc * 256:(dc + 1) * 256], kdec.t[:, dc * 128:(dc + 1) * 128], v.t[:, c, :], start=True, stop=True),
                    reads=[kdec.k, v.k], writes=[bu.k])
            P.op("dve", lambda bu=bu: nc.vector.scalar_tensor_tensor(
                out=stf.t[:, :], in0=stf.t[:, :], scalar=rsc.t[:, 1:2], in1=bu.t[:, :], op0=ALU.mult, op1=ALU.add),
                reads=[stf.k, rsc.k, bu.k], writes=[stf.k])
            P.op("pool", lambda: nc.gpsimd.tensor_copy(stb.t[:, :], stf.t[:, :]), reads=[stf.k], writes=[stb.k])
            st = str_.get()
            j = jk.get()
            P.op("dve", lambda st=st: nc.vector.memset(st.t[:, :], 0.0), writes=[st.k])
            P.op("act", lambda st=st, j=j, bo=bo: nc.scalar.activation(
                out=j.t[:, :], in_=bo.t[:, 0:256], func=AF.Square, scale=1.0 / 16, accum_out=st.t[:, 0:1]),
                reads=[bo.k, st.k], writes=[j.k, st.k])
            rstd_from_ss(cx, st)
            P.op("dve", lambda os_=os_, bo=bo, st=st, sg=sg, c=c: nc.vector.scalar_tensor_tensor(
                out=os_.t[:, c, :], in0=bo.t[:, 0:256], scalar=st.t[:, 2:3], in1=sg.t[:, c, :], op0=ALU.mult, op1=ALU.mult),
                reads=[bo.k, st.k, sg.k], writes=[os_.k])
        P.dma("sp", lambda os_=os_, t0=t0: nc.sync.dma_start(
            out=mix[t0:t0 + 512, 0:256].rearrange("(c p) e -> p c e", p=128), in_=os_.t[:, :, :]), reads=[os_.k], writes=[kmix])

    cx.P.drain("sp")
    NT = S // 128
    NQ = S // 512
    triu = cx.load(cx.sb([128, 128], F32), K["triu"])
    sltri = cx.load(cx.sb([128, 128], F32), K["sltri"])
    trineg = cx.load(cx.sb([128, 128], F32), K["trineg"])
    bfr = cx.load(cx.sb([128, 2], F32), K["fox_b"])
    negb = cx.sb([128, 2], F32)
    P.op("dve", lambda: nc.vector.tensor_scalar(negb.t[:, :], bfr.t[:, :], -1.0, 0.0, ALU.mult, ALU.add), reads=[bfr.k], writes=[negb.k])
    LF = cx.sb([128, 128], F32)
    LTs = cx.sb([128, 128], F32)
    Ws = cx.sb([128, 128], F32)
    offs = cx.sb([128, 1], F32)
    cn = cx.sb([128, 128], F32)
    cnT = cx.sb([128, 128], F32)
    kTb = cx.sb([128, S], BF16)
    vaug = cx.sb([128, NT, 132], BF16)
    qbr = cx.ring(2, [128, 512], BF16)
    cnqr = cx.ring(2, [128, 512], F32)
    tmpr = cx.ring(4, [128, 512], F32)
    pr = cx.ring(5, [128, 512], BF16)
    rdr = cx.ring(4, [128, 1], F32)
    fos = cx.ring(2, [128, 4, 128], BF16)
    accb = [cx.allbanks[i] for i in range(4)]
    sring = Ring([cx.allbanks[i] for i in range(4, 8)])
    scale = 128.0 ** -0.5
    kcn = Trk(multi=True)
    for hd in range(2):
        P.dma("sp", lambda hd=hd: nc.sync.dma_start(out=LF.t[0:NT, :], in_=dd["ffT"][hd, :].rearrange("(j p) -> j p", p=128)),
              reads=[kd], writes=[LF.k])
        P.op("act", lambda hd=hd: nc.scalar.activation(out=LF.t[0:NT, :], in_=LF.t[0:NT, :], func=AF.Exp, scale=-1.0,
                                                        bias=negb.t[0:NT, hd:hd + 1]), reads=[LF.k, negb.k], writes=[LF.k])
        P.op("act", lambda: nc.scalar.activation(out=LF.t[0:NT, :], in_=LF.t[0:NT, :], func=AF.Ln, bias=1.0),
             reads=[LF.k], writes=[LF.k])
        b = sring.get()
        P.op("pe", lambda b=b: nc.tensor.transpose(b.t[:, 0:NT], LF.t[0:NT, :], identf.t[0:NT, 0:NT]),
             reads=[LF.k, identf.k], writes=[b.k])
        P.op("dve", lambda b=b: nc.vector.tensor_copy(LTs.t[:, 0:NT], b.t[:, 0:NT]), reads=[b.k], writes=[LTs.k])
        b2 = sring.get()
        P.op("pe", lambda b2=b2: nc.tensor.matmul(b2.t[0:NT, 0:128], LTs.t[:, 0:NT], triu.t[:, :], start=True, stop=True),
             reads=[LTs.k, triu.k], writes=[b2.k])
        P.op("dve", lambda b2=b2: nc.vector.tensor_copy(Ws.t[0:NT, :], b2.t[0:NT, 0:128]), reads=[b2.k], writes=[Ws.k])
        b3 = sring.get()
        P.op("pe", lambda b3=b3: nc.tensor.matmul(b3.t[0:NT, 0:1], sltri.t[0:NT, 0:NT], Ws.t[0:NT, 127:128], start=True, stop=True),
             reads=[Ws.k, sltri.k], writes=[b3.k])
        P.op("dve", lambda b3=b3: nc.vector.tensor_copy(offs.t[0:NT, :], b3.t[0:NT, 0:1]), reads=[b3.k], writes=[offs.k])
        P.op("dve", lambda: nc.vector.tensor_scalar(cn.t[0:NT, :], Ws.t[0:NT, :], offs.t[0:NT, 0:1], 0.0, ALU.add, ALU.add),
             reads=[Ws.k, offs.k], writes=[cn.k])
        P.dma("sp", lambda hd=hd: nc.sync.dma_start(out=dd["cn"][hd, :].rearrange("(j p) -> j p", p=128), in_=cn.t[0:NT, :]),
              reads=[cn.k], writes=[kcn])
        b4 = sring.get()
        P.op("pe", lambda b4=b4: nc.tensor.transpose(b4.t[:, 0:NT], cn.t[0:NT, :], identf.t[0:NT, 0:NT]),
             reads=[cn.k, identf.k], writes=[b4.k])
        P.op("dve", lambda b4=b4: nc.vector.tensor_copy(cnT.t[:, 0:NT], b4.t[:, 0:NT]), reads=[b4.k], writes=[cnT.k])
        P.dma("sp", lambda hd=hd: nc.sync.dma_start(out=kTb.t[:, :], in_=dd["fkT"][hd, :, :]), reads=[kd], writes=[kTb.k])
        P.op("pool", lambda: nc.gpsimd.memset(vaug.t[:, :, 128:132], 1.0), writes=[vaug.k])
        for j0 in range(0, NT, 16):
            j1 = min(NT, j0 + 16)
            P.dma("sp", lambda hd=hd, j0=j0, j1=j1: nc.sync.dma_start(
                out=vaug.t[:, j0:j1, 0:128], in_=dd["fv"][hd, j0 * 128:j1 * 128, :].rearrange("(j p) e -> p j e", p=128)),
                reads=[kd], writes=[vaug.k])
        for qb in range(NQ):
            qblk, cnq = qbr.get(), cnqr.get()
            P.dma("sp", lambda qblk=qblk, hd=hd, qb=qb: nc.sync.dma_start(out=qblk.t[:, :], in_=dd["fqT"][hd, :, qb * 512:(qb + 1) * 512]),
                  reads=[kd], writes=[qblk.k])
            P.dma("sp", lambda cnq=cnq, hd=hd, qb=qb: nc.sync.dma_start(
                out=cnq.t[:, :], in_=dd["cn"][hd, qb * 512:(qb + 1) * 512].partition_broadcast(128)), reads=[kcn], writes=[cnq.k])
            nj = 4 * qb + 4
            pend = {}
            SK = 2
            for step in range(nj + SK):
                if step < nj:
                    j = step
                    r = j - 4 * qb
                    c0 = max(r, 0) * 128
                    bs = sring.get()
                    P.op("pe", lambda bs=bs, j=j, c0=c0, qblk=qblk: nc.tensor.matmul(
                        bs.t[:, c0:512], kTb.t[:, j * 128:(j + 1) * 128], qblk.t[:, c0:512], start=True, stop=True),
                        reads=[kTb.k, qblk.k], writes=[bs.k])
                    tm = tmpr.get()
                    P.op("dve", lambda tm=tm, bs=bs, cnq=cnq, c0=c0: nc.vector.scalar_tensor_tensor(
                        out=tm.t[:, c0:512], in0=bs.t[:, c0:512], scalar=scale, in1=cnq.t[:, c0:512], op0=ALU.mult, op1=ALU.subtract),
                        reads=[bs.k, cnq.k], writes=[tm.k])
                    if r >= 0:
                        P.op("pool", lambda tm=tm, c0=c0: nc.gpsimd.tensor_tensor(
                            out=tm.t[:, c0:c0 + 128], in0=tm.t[:, c0:c0 + 128], in1=trineg.t[:, :], op=ALU.add),
                            reads=[tm.k, trineg.k], writes=[tm.k])
                    p = pr.get()
                    P.op("act", lambda p=p, tm=tm, j=j, c0=c0: nc.scalar.activation(
                        out=p.t[:, c0:512], in_=tm.t[:, c0:512], func=AF.Exp, bias=cnT.t[:, j:j + 1], scale=1.0),
                        reads=[tm.k, cnT.k], writes=[p.k])
                    pend[j] = (p, r)
                if step >= SK:
                    j = step - SK
                    p, r = pend.pop(j)
                    for r2 in range(max(r, 0), 4):
                        P.op("pe", lambda p=p, r2=r2, j=j, s_=(j == 0), e_=(j == 4 * qb + r2): nc.tensor.matmul(
                            accb[r2].t[:, 0:129], p.t[:, r2 * 128:(r2 + 1) * 128], vaug.t[:, j, 0:129], start=s_, stop=e_),
                            reads=[p.k, vaug.k], writes=[accb[r2].k])
            fo = fos.get()
            for r2 in range(4):
                rd = rdr.get()
                P.op("dve", lambda rd=rd, r2=r2: nc.vector.reciprocal(rd.t[:, :], accb[r2].t[:, 128:129]), reads=[accb[r2].k], writes=[rd.k])
                P.op("act", lambda fo=fo, rd=rd, r2=r2: nc.scalar.activation(
                    out=fo.t[:, r2, :], in_=accb[r2].t[:, 0:128], func=AF.Copy, scale=rd.t[:, 0:1]), reads=[accb[r2].k, rd.k], writes=[fo.k])
            P.dma("sp", lambda fo=fo, qb=qb, hd=hd: nc.sync.dma_start(
                out=mix[qb * 512:(qb + 1) * 512, 256 + hd * 128:256 + (hd + 1) * 128].rearrange("(c p) e -> p c e", p=128),
                in_=fo.t[:, :, :]), reads=[fo.k], writes=[kmix])


def stage_a1(cx, K, S, TOK, G, hg, khg, w1b, kw1, dd, CW=None):
    nc, P = cx.nc, cx.P
    res = common_res(cx, K)
    hTr = cx.ring(2, [128, KD, G], BF16)
    ob = cx.ring(4, [128, G], BF16)
    ot = cx.ring(3, [128, 512], BF16)
    of = cx.ring(2, [128, G], F32)
    NTT = G // 128
    kd = dd["k"]
    for r in range(NCORE):
        for gg in range(TOK // G):
            segs, gp = seg_map(S, TOK, G, r, gg, CW)
            hT = hTr.get()
            src = hg[r * D:(r + 1) * D, gg * G:(gg + 1) * G].rearrange("(c p) t -> p c t", p=128)
            P.dma("sp", lambda hT=hT, src=src: nc.sync.dma_start(out=hT.t[:, :, :], in_=src), reads=[khg], writes=[hT.k])
            for cb in range(2):
                bk = gemm(cx, res, "T", hT, KD, G, 512, lambda wt, k0, nk, cb=cb: load_w(cx, wt, w1b, kw1, k0, nk, cb * 512, 512))
                for tt in range(NTT):
                    o = ot.get()
                    P.op("act", lambda o=o, b=bk[tt]: nc.scalar.copy(o.t[:, :], b.t[:, :]), reads=[bk[tt].k], writes=[o.k])
                    rows = slice(gp(tt), gp(tt) + 128)
                    P.dma("pool", lambda o=o, rows=rows, cb=cb: nc.gpsimd.dma_start(out=dd["z"][rows, cb * 512:(cb + 1) * 512], in_=o.t[:, :]),
                          reads=[o.k], writes=[kd])
            for cb in range(3):
                ncl = 512 if cb < 2 else 256
                bk = gemm(cx, res, "F", hT, KD, G, ncl, lambda wt, k0, nk, cb=cb, ncl=ncl: load_w(cx, wt, w1b, kw1, k0, nk, 1024 + cb * 512, ncl))
                for i in range(ncl // 128):
                    o = ob.get()
                    eng = "act" if i % 2 == 0 else "dve"
                    if eng == "act":
                        P.op("act", lambda o=o, b=bk[i]: nc.scalar.copy(o.t[:, :], b.t[:, 0:G]), reads=[bk[i].k], writes=[o.k])
                    else:
                        P.op("dve", lambda o=o, b=bk[i]: nc.vector.tensor_copy(o.t[:, :], b.t[:, 0:G]), reads=[bk[i].k], writes=[o.k])
                    ch = cb * 4 + i
                    for (ls, gs, L) in segs:
                        P.dma("pool", lambda o=o, ch=ch, ls=ls, gs=gs, L=L: nc.gpsimd.dma_start(
                            out=dd["xbc"][ch * 128:(ch + 1) * 128, gs:gs + L], in_=o.t[:, ls:ls + L]), reads=[o.k], writes=[kd])
            bk = gemm(cx, res, "F", hT, KD, G, 16, lambda wt, k0, nk: load_w(cx, wt, w1b, kw1, k0, nk, 2304, 16))
            o = of.get()
            P.op("act", lambda o=o, b=bk[0]: nc.scalar.copy(o.t[0:16, :], b.t[0:16, 0:G]), reads=[bk[0].k], writes=[o.k])
            for (ls, gs, L) in segs:
                P.dma("pool", lambda o=o, ls=ls, gs=gs, L=L: nc.gpsimd.dma_start(out=dd["dtT"][:, gs:gs + L], in_=o.t[0:16, ls:ls + L]), reads=[o.k], writes=[kd])


def stage_m1(cx, K, S, dd, mix, kmix):
    nc, P = cx.nc, cx.P
    kd = dd["k"]
    ident = cx.load(cx.sb([128, 128], BF16), K["ident"])
    identf = cx.load(cx.sb([128, 128], F32), K["identf"])
    triu = cx.load(cx.sb([128, 128], F32), K["triu"])
    onesf = cx.load(cx.sb([128, 128], F32), K["onesf"])
    selm = cx.load(cx.sb([16, 16, 128], F32), K["selmat"])
    cw = cx.load(cx.sb([128, 10, 4], F32), K["m_convw"])
    cbias = cx.load(cx.sb([128, 10], F32), K["m_convb"])
    dtb = cx.load(cx.sb([16, 1], F32), K["m_dtb"])
    abc = cx.load(cx.sb([128, 16], F32), K["m_alog"])
    dsk = cx.load(cx.sb([128, 16], F32), K["m_dskip"])
    gnw = cx.load(cx.sb([128, 1024], F32), K["m_gnw"].partition_broadcast(128))
    trim = cx.load(cx.sb([128, 128], F32), K["tri01"])
    P.op("act", lambda: nc.scalar.activation(out=abc.t[:, :], in_=abc.t[:, :], func=AF.Exp), reads=[abc.k], writes=[abc.k])
    P.op("dve", lambda: nc.vector.tensor_scalar(abc.t[:, :], abc.t[:, :], -1.0, 0.0, ALU.mult, ALU.add), reads=[abc.k], writes=[abc.k])
    stf = cx.sb([128, 1024], F32)
    stb = cx.sb([128, 1024], BF16)
    P.op("dve", lambda: nc.vector.memset(stf.t[:, :], 0.0), writes=[stf.k])
    P.op("dve", lambda: nc.vector.memset(stb.t[:, :], 0.0), writes=[stb.k])
    prer = cx.ring(2, [128, 10, 516], BF16)
    accr = cx.ring(2, [128, 512], F32)
    xcr = cx.ring(2, [128, 10, 512], BF16)
    dtr = cx.ring(2, [16, 512], F32)
    zr = cx.ring(2, [128, 4, 1024], BF16)
    xtr = cx.ring(2, [128, 1024], BF16)
    btr = cx.ring(2, [128, 128], BF16)
    smr = cx.ring(12, [128, 16], F32)
    acT = cx.ring(2, [16, 128], F32)
    cbr = cx.ring(2, [128, 128], F32)
    decr = cx.ring(2, [128, 4, 128], F32)
    Mr = cx.ring(2, [128, 16, 128], BF16)
    xdr = cx.ring(2, [128, 1024], BF16)
    xwr = cx.ring(2, [128, 1024], BF16)
    f1 = cx.ring(2, [128, 1024], F32)
    f2 = cx.ring(2, [128, 1024], F32)
    f3 = cx.ring(2, [128, 1024], F32)
    str_ = cx.ring(4, [128, 4], F32)
    osr = cx.ring(2, [128, 4, 1024], BF16)

    def v3(ap):
        return ap.rearrange("p (h e) -> p h e", e=64)

    def bc3(ap16):
        return ap16.unsqueeze(2).to_broadcast([128, 16, 64])

    for sc in range(S // 512):
        t0 = sc * 512
        pre = prer.get()
        if sc == 0:
            P.op("pool", lambda pre=pre: nc.gpsimd.memset(pre.t[:, :, 0:4], 0.0), writes=[pre.k])
            P.dma("sp", lambda pre=pre: nc.sync.dma_start(out=pre.t[:, :, 4:516], in_=dd["xbc"][:, 0:512].rearrange("(c p) t -> p c t", p=128)),
                  reads=[kd], writes=[pre.k])
        else:
            P.dma("sp", lambda pre=pre, t0=t0: nc.sync.dma_start(
                out=pre.t[:, :, 0:516], in_=dd["xbc"][:, t0 - 4:t0 + 512].rearrange("(c p) t -> p c t", p=128)), reads=[kd], writes=[pre.k])
        dtt = dtr.get()
        P.dma("sp", lambda dtt=dtt, t0=t0: nc.sync.dma_start(out=dtt.t[:, :], in_=dd["dtT"][:, t0:t0 + 512]), reads=[kd], writes=[dtt.k])
        z = zr.get()
        P.dma("sp", lambda z=z, t0=t0: nc.sync.dma_start(out=z.t[:, :, :], in_=dd["z"][t0:t0 + 512, :].rearrange("(c p) e -> p c e", p=128)),
              reads=[kd], writes=[z.k])
        xc = xcr.get()
        for ch in range(10):
            eng = "dve"
            eo = nc.vector
            acc = accr.get()
            P.op(eng, lambda eo=eo, acc=acc, pre=pre, ch=ch: eo.tensor_scalar(
                acc.t[:, :], pre.t[:, ch, 1:513], cw.t[:, ch, 0:1], 0.0, ALU.mult, ALU.add), reads=[pre.k, cw.k], writes=[acc.k])
            for k in range(1, 4):
                P.op(eng, lambda eo=eo, acc=acc, pre=pre, ch=ch, k=k: eo.scalar_tensor_tensor(
                    out=acc.t[:, :], in0=pre.t[:, ch, 1 + k:513 + k], scalar=cw.t[:, ch, k:k + 1], in1=acc.t[:, :],
                    op0=ALU.mult, op1=ALU.add), reads=[pre.k, cw.k, acc.k], writes=[acc.k])
            P.op("act", lambda acc=acc, xc=xc, ch=ch: nc.scalar.activation(
                out=xc.t[:, ch, :], in_=acc.t[:, :], func=AF.Silu, bias=cbias.t[:, ch:ch + 1]), reads=[acc.k, cbias.k], writes=[xc.k])
        P.op("act", lambda dtt=dtt: nc.scalar.activation(out=dtt.t[:, :], in_=dtt.t[:, :], func=AF.Exp, bias=dtb.t[:, 0:1]),
             reads=[dtt.k, dtb.k], writes=[dtt.k])
        P.op("act", lambda dtt=dtt: nc.scalar.activation(out=dtt.t[:, :], in_=dtt.t[:, :], func=AF.Ln, bias=1.0),
             reads=[dtt.k], writes=[dtt.k])
        os_ = osr.get()
        for c in range(4):
            cols = slice(c * 128, (c + 1) * 128)
            bx = cx.banks.get()
            pvx = bx.t[:, :].bitcast(BF16)
            for i in range(8):
                P.op("pe", lambda pvx=pvx, xc=xc, i=i, cols=cols: nc.tensor.transpose(
                    pvx[:, i * 128:(i + 1) * 128], xc.t[:, i, cols], ident.t[:, :]), reads=[xc.k, ident.k], writes=[bx.k])
            xt = xtr.get()
            P.op("act", lambda xt=xt, pvx=pvx: nc.scalar.copy(xt.t[:, :], pvx[:, 0:1024]), reads=[bx.k], writes=[xt.k])
            bb = cx.banks.get()
            pvb = bb.t[:, :].bitcast(BF16)
            P.op("pe", lambda pvb=pvb, xc=xc, cols=cols: nc.tensor.transpose(pvb[:, 0:128], xc.t[:, 8, cols], ident.t[:, :]),
                 reads=[xc.k, ident.k], writes=[bb.k])
            bt = btr.get()
            P.op("dve", lambda bt=bt, pvb=pvb: nc.vector.tensor_copy(bt.t[:, :], pvb[:, 0:128]), reads=[bb.k], writes=[bt.k])
            bd = cx.banks.get()
            P.op("pe", lambda bd=bd, dtt=dtt, cols=cols: nc.tensor.transpose(bd.t[:, 0:16], dtt.t[0:16, cols], identf.t[0:16, 0:16]),
                 reads=[dtt.k, identf.k], writes=[bd.k])
            dts, da = smr.get(), smr.get()
            P.op("dve", lambda dts=dts, bd=bd: nc.vector.tensor_copy(dts.t[:, :], bd.t[:, 0:16]), reads=[bd.k], writes=[dts.k])
            P.op("dve", lambda da=da, dts=dts: nc.vector.tensor_tensor(out=da.t[:, :], in0=dts.t[:, :], in1=abc.t[:, :], op=ALU.mult),
                 reads=[dts.k, abc.k], writes=[da.k])
            ba = cx.banks.get()
            P.op("pe", lambda ba=ba, da=da: nc.tensor.matmul(ba.t[:, 0:16], triu.t[:, :], da.t[:, :], start=True, stop=True),
                 reads=[triu.k, da.k], writes=[ba.k])
            P.op("pe", lambda ba=ba, da=da: nc.tensor.matmul(ba.t[:, 16:32], onesf.t[:, :], da.t[:, :], start=True, stop=True),
                 reads=[onesf.k, da.k], writes=[ba.k])
            P.op("pe", lambda ba=ba, da=da: nc.tensor.matmul(ba.t[0:16, 64:192], da.t[:, :], triu.t[:, :], start=True, stop=True),
                 reads=[triu.k, da.k], writes=[ba.k])
            nacs, eacs, tail, eL = smr.get(), smr.get(), smr.get(), smr.get()
            P.op("dve", lambda nacs=nacs, ba=ba: nc.vector.tensor_scalar(nacs.t[:, :], ba.t[:, 0:16], -1.0, 0.0, ALU.mult, ALU.add),
                 reads=[ba.k], writes=[nacs.k])
            P.op("act", lambda eacs=eacs, ba=ba: nc.scalar.activation(out=eacs.t[:, :], in_=ba.t[:, 0:16], func=AF.Exp),
                 reads=[ba.k], writes=[eacs.k])
            P.op("act", lambda eL=eL, ba=ba: nc.scalar.activation(out=eL.t[:, :], in_=ba.t[:, 16:32], func=AF.Exp),
                 reads=[ba.k], writes=[eL.k])
            P.op("dve", lambda tail=tail, ba=ba, nacs=nacs: nc.vector.tensor_tensor(out=tail.t[:, :], in0=ba.t[:, 16:32], in1=nacs.t[:, :], op=ALU.add),
                 reads=[ba.k, nacs.k], writes=[tail.k])
            P.op("act", lambda tail=tail: nc.scalar.activation(out=tail.t[:, :], in_=tail.t[:, :], func=AF.Exp), reads=[tail.k], writes=[tail.k])
            at = acT.get()
            P.op("dve", lambda at=at, ba=ba: nc.vector.tensor_copy(at.t[:, :], ba.t[0:16, 64:192]), reads=[ba.k], writes=[at.k])
            bcb = cx.banks.get()
            P.op("pe", lambda bcb=bcb, xc=xc, cols=cols: nc.tensor.matmul(bcb.t[:, 0:128], xc.t[:, 8, cols], xc.t[:, 9, cols], start=True, stop=True),
                 reads=[xc.k], writes=[bcb.k])
            cbm = cbr.get()
            P.op("dve", lambda cbm=cbm, bcb=bcb: nc.vector.tensor_tensor(out=cbm.t[:, :], in0=bcb.t[:, 0:128], in1=trim.t[:, :], op=ALU.mult),
                 reads=[bcb.k, trim.k], writes=[cbm.k])
            xd, xw = xdr.get(), xwr.get()
            P.op("dve", lambda xd=xd, xt=xt, dts=dts: nc.vector.tensor_tensor(out=v3(xd.t[:, :]), in0=v3(xt.t[:, :]), in1=bc3(dts.t[:, :]), op=ALU.mult),
                 reads=[xt.k, dts.k], writes=[xd.k])
            P.op("pool", lambda xw=xw, xd=xd, tail=tail: nc.gpsimd.tensor_tensor(out=v3(xw.t[:, :]), in0=v3(xd.t[:, :]), in1=bc3(tail.t[:, :]), op=ALU.mult),
                 reads=[xd.k, tail.k], writes=[xw.k])
            M = Mr.get()
            by = [cx.banks.get(), cx.banks.get()]
            for hg_ in range(4):
                bdec = cx.banks.get()
                for hh in range(4):
                    h = hg_ * 4 + hh
                    P.op("pe", lambda bdec=bdec, at=at, h=h, hh=hh: nc.tensor.matmul(
                        bdec.t[:, hh * 128:(hh + 1) * 128], selm.t[:, h, :], at.t[:, :], start=True, stop=True),
                        reads=[selm.k, at.k], writes=[bdec.k])
                dec = decr.get()
                for hh in range(4):
                    h = hg_ * 4 + hh
                    P.op("act", lambda dec=dec, bdec=bdec, h=h, hh=hh, nacs=nacs: nc.scalar.activation(
                        out=dec.t[:, hh, :], in_=bdec.t[:, hh * 128:(hh + 1) * 128], func=AF.Exp, bias=nacs.t[:, h:h + 1]),
                        reads=[bdec.k, nacs.k], writes=[dec.k])
                for hh in range(4):
                    h = hg_ * 4 + hh
                    P.op("dve", lambda dec=dec, M=M, cbm=cbm, h=h, hh=hh: nc.vector.scalar_tensor_tensor(
                        out=M.t[:, h, :], in0=dec.t[:, hh, :], scalar=1.0, in1=cbm.t[:, :], op0=ALU.min, op1=ALU.mult),
                        reads=[dec.k, cbm.k], writes=[M.k])
                for hh in range(4):
                    h = hg_ * 4 + hh
                    P.op("pe", lambda M=M, xd=xd, h=h, byb=by[h // 8]: nc.tensor.matmul(
                        byb.t[:, (h % 8) * 64:(h % 8 + 1) * 64], M.t[:, h, :], xd.t[:, h * 64:(h + 1) * 64], start=True, stop=True),
                        reads=[M.k, xd.k], writes=[by[h // 8].k])
            bo = [cx.banks.get(), cx.banks.get()]
            for hb in range(2):
                P.op("pe", lambda hb=hb, xc=xc, cols=cols, bob=bo[hb]: nc.tensor.matmul(
                    bob.t[:, :], xc.t[:, 9, cols], stb.t[:, hb * 512:(hb + 1) * 512], start=True, stop=True),
                    reads=[xc.k, stb.k], writes=[bo[hb].k])
            y1, y2, y3 = f1.get(), f2.get(), f3.get()
            for hb in range(2):
                hs = slice(hb * 512, (hb + 1) * 512)
                P.op("dve", lambda y1=y1, hb=hb, hs=hs, eacs=eacs, bob=bo[hb]: nc.vector.tensor_tensor(
                    out=v3(y1.t[:, hs]), in0=v3(bob.t[:, :]), in1=eacs.t[:, hb * 8:(hb + 1) * 8].unsqueeze(2).to_broadcast([128, 8, 64]), op=ALU.mult),
                    reads=[bo[hb].k, eacs.k], writes=[y1.k])
                P.op("dve", lambda y1=y1, hs=hs, byb=by[hb]: nc.vector.tensor_tensor(
                    out=y1.t[:, hs], in0=byb.t[:, :], in1=y1.t[:, hs], op=ALU.add), reads=[by[hb].k, y1.k], writes=[y1.k])
            bsu = [cx.banks.get(), cx.banks.get()]
            for hb in range(2):
                P.op("pe", lambda hb=hb, bt=bt, xw=xw, bsb=bsu[hb]: nc.tensor.matmul(
                    bsb.t[:, :], bt.t[:, :], xw.t[:, hb * 512:(hb + 1) * 512], start=True, stop=True),
                    reads=[bt.k, xw.k], writes=[bsu[hb].k])
            P.op("pool", lambda eL=eL: nc.gpsimd.tensor_tensor(out=v3(stf.t[:, :]), in0=v3(stf.t[:, :]), in1=bc3(eL.t[:, :]), op=ALU.mult),
                 reads=[stf.k, eL.k], writes=[stf.k])
            for hb in range(2):
                hs = slice(hb * 512, (hb + 1) * 512)
                P.op("dve", lambda hs=hs, bsb=bsu[hb]: nc.vector.tensor_tensor(out=stf.t[:, hs], in0=bsb.t[:, :], in1=stf.t[:, hs], op=ALU.add),
                     reads=[bsu[hb].k, stf.k], writes=[stf.k])
            P.op("act", lambda: nc.scalar.copy(stb.t[:, :], stf.t[:, :]), reads=[stf.k], writes=[stb.k])
            P.op("pool", lambda y2=y2, xt=xt: nc.gpsimd.tensor_tensor(out=v3(y2.t[:, :]), in0=v3(xt.t[:, :]), in1=bc3(dsk.t[:, :]), op=ALU.mult),
                 reads=[xt.k, dsk.k], writes=[y2.k])
            P.op("pool", lambda y1=y1, y2=y2: nc.gpsimd.tensor_tensor(out=y1.t[:, :], in0=y1.t[:, :], in1=y2.t[:, :], op=ALU.add),
                 reads=[y1.k, y2.k], writes=[y1.k])
            P.op("act", lambda y3=y3, z=z, c=c: nc.scalar.activation(out=y3.t[:, :], in_=z.t[:, c, :], func=AF.Silu), reads=[z.k], writes=[y3.k])
            P.op("dve", lambda y1=y1, y3=y3: nc.vector.tensor_tensor(out=y1.t[:, :], in0=y1.t[:, :], in1=y3.t[:, :], op=ALU.mult),
                 reads=[y1.k, y3.k], writes=[y1.k])
            st = str_.get()
            P.op("dve", lambda st=st: nc.vector.memset(st.t[:, :], 0.0), writes=[st.k])
            P.op("act", lambda st=st, y1=y1, y2=y2: nc.scalar.activation(
                out=y2.t[:, :], in_=y1.t[:, :], func=AF.Square, scale=1.0 / 32, accum_out=st.t[:, 0:1]),
                reads=[y1.k, st.k], writes=[y2.k, st.k])
            rstd_from_ss(cx, st)
            P.op("dve", lambda os_=os_, y1=y1, st=st, c=c: nc.vector.scalar_tensor_tensor(
                out=os_.t[:, c, :], in0=y1.t[:, :], scalar=st.t[:, 2:3], in1=gnw.t[:, :], op0=ALU.mult, op1=ALU.mult),
                reads=[y1.k, st.k, gnw.k], writes=[os_.k])
        P.dma("sp", lambda os_=os_, t0=t0: nc.sync.dma_start(
            out=mix[t0:t0 + 512, :].rearrange("(c p) e -> p c e", p=128), in_=os_.t[:, :, :]), reads=[os_.k], writes=[kmix])


def stage_c(cx, K, TOK, G, layer, mg, kmg, x_in, kxin, x_outd, kxo, W, final):
    nc, P = cx.nc, cx.P
    NG = TOK // G
    NTT = G // 128
    res = common_res(cx, K)
    ones = cx.load(cx.sb([128, 128], BF16), K["ones"])
    gbc = cx.sb([128, D], F32)
    hT = cx.sb([128, KD, G], BF16)
    big = cx.sb([128, 44, G], BF16)
    kT = cx.sb([128, 4, 256], BF16)
    vv = cx.sb([128, 2, 512], BF16)
    qT = cx.sb([128, 4, G], BF16)
    oT = cx.sb([128, 4, G], BF16)
    pT = cx.ring(2, [128, 2, G], BF16)
    tmpf = cx.ring(2, [128, G], F32)
    x_a, x_b, x_c = K["x_a"], K["x_b"], K["x_c%d" % layer]
    ka, kb, kc_ = Trk(multi=True), Trk(multi=True), kxo
    wq, kwq = W["wq%d" % layer]
    wk, kwk = W["wk%d" % layer]
    wv, kwv = W["wv%d" % layer]
    wo, kwo = W["wo%d" % layer]
    wg, kwg = W["wg%d" % layer]
    wu, kwu = W["wu%d" % layer]
    wd, kwd = W["wd%d" % layer]

    def load_g(vec):
        P.dma("sp", lambda: nc.sync.dma_start(out=gbc.t[:, :], in_=vec.partition_broadcast(128)), writes=[gbc.k])

    load_g(K["mem_norm"])
    norm_rows(cx, res, [K["mem"][i * 128:(i + 1) * 128, :] for i in range(2)], gbc, hT)
    bk = gemm(cx, res, "F", hT, KD, 256, 512, lambda wt, k0, nk: load_w(cx, wt, wk, kwk, k0, nk, 0, 512))
    for h in range(4):
        P.op("act", lambda h=h, b=bk[h]: nc.scalar.copy(kT.t[:, h, :], b.t[:, 0:256]), reads=[bk[h].k], writes=[kT.k])
    bv = gemm(cx, res, "T", hT, KD, 256, 512, lambda wt, k0, nk: load_w(cx, wt, wv, kwv, k0, nk, 0, 512))
    for mt in range(2):
        P.op("dve", lambda mt=mt, b=bv[mt]: nc.vector.tensor_copy(vv.t[:, mt, :], b.t[:, :]), reads=[bv[mt].k], writes=[vv.k])

    for tt in range(TOK // 128):
        xt = res["xt"].get()
        rows = slice(tt * 128, (tt + 1) * 128)
        P.dma("sp", lambda xt=xt, rows=rows: nc.sync.dma_start(out=xt.t[:, :], in_=x_in[rows, :]), reads=[kxin], writes=[xt.k])
        for hh in range(2):
            xd_ = res["xt2"].get()
            cs_ = slice(hh * 2048, (hh + 1) * 2048)
            P.dma("sp", lambda xd_=xd_, rows=rows, cs_=cs_: nc.sync.dma_start(out=xd_.t[:, :], in_=mg[rows, cs_]), reads=[kmg], writes=[xd_.k])
            P.op("pool", lambda xt=xt, xd_=xd_, cs_=cs_: nc.gpsimd.tensor_tensor(out=xt.t[:, cs_], in0=xt.t[:, cs_], in1=xd_.t[:, :], op=ALU.add),
                 reads=[xt.k, xd_.k], writes=[xt.k])
        P.dma("sp", lambda xt=xt, rows=rows: nc.sync.dma_start(out=x_a[rows, :], in_=xt.t[:, :]), reads=[xt.k], writes=[ka])

    load_g(K["xa_norm"][layer, :])
    scale = 128.0 ** -0.5
    for g in range(NG):
        r0 = g * G
        norm_rows_k(cx, res, [x_a[r0 + tt * 128:r0 + (tt + 1) * 128, :] for tt in range(NTT)], ka, gbc, hT)
        bq = gemm(cx, res, "F", hT, KD, G, 512, lambda wt, k0, nk: load_w(cx, wt, wq, kwq, k0, nk, 0, 512))
        for h in range(4):
            P.op("act", lambda h=h, b=bq[h]: nc.scalar.copy(qT.t[:, h, :], b.t[:, 0:G]), reads=[bq[h].k], writes=[qT.k])
        for h in range(4):
            p = pT.get()
            for mt in range(2):
                bs = cx.banks.get()
                P.op("pe", lambda bs=bs, h=h, mt=mt: nc.tensor.matmul(
                    bs.t[:, 0:G], kT.t[:, h, mt * 128:(mt + 1) * 128], qT.t[:, h, :], start=True, stop=True),
                    reads=[kT.k, qT.k], writes=[bs.k])
                P.op("act", lambda bs=bs, p=p, mt=mt: nc.scalar.activation(
                    out=p.t[:, mt, :], in_=bs.t[:, 0:G], func=AF.Exp, scale=scale), reads=[bs.k], writes=[p.k])
            bo = cx.banks.get()
            bd = cx.banks.get()
            for mt in range(2):
                P.op("pe", lambda bo=bo, h=h, mt=mt, p=p: nc.tensor.matmul(
                    bo.t[:, 0:G], vv.t[:, mt, h * 128:(h + 1) * 128], p.t[:, mt, :], start=(mt == 0), stop=(mt == 1)),
                    reads=[vv.k, p.k], writes=[bo.k])
            for mt in range(2):
                P.op("pe", lambda bd=bd, mt=mt, p=p: nc.tensor.matmul(
                    bd.t[:, 0:G], ones.t[:, :], p.t[:, mt, :], start=(mt == 0), stop=(mt == 1)),
                    reads=[ones.k, p.k], writes=[bd.k])
            tf = tmpf.get()
            P.op("dve", lambda bd=bd, tf=tf: nc.vector.reciprocal(tf.t[:, :], bd.t[:, 0:G]), reads=[bd.k], writes=[tf.k])
            P.op("dve", lambda bo=bo, tf=tf, h=h: nc.vector.tensor_tensor(
                out=oT.t[:, h, :], in0=bo.t[:, 0:G], in1=tf.t[:, :], op=ALU.mult), reads=[bo.k, tf.k], writes=[oT.k])
        for cb in range(D // 512):
            banks = gemm(cx, res, "T", oT, 4, G, 512,
                         lambda wt, k0, nk, cb=cb: load_w(cx, wt, wo, kwo, k0, nk, cb * 512, 512))
            resid_out(cx, res, banks, x_a, ka, x_b, kb, r0, cb * 512, 512)

    load_g(K["ffn_norm"][layer, :])
    NFB = DFF // 256
    for g in range(NG):
        r0 = g * G
        norm_rows_k(cx, res, [x_b[r0 + tt * 128:r0 + (tt + 1) * 128, :] for tt in range(NTT)], kb, gbc, hT)
        for hf, (fb0, fb1) in enumerate(((0, 22), (22, NFB))):
            for fb in range(fb0, fb1):
                def wl(wt, k0, nk, fb=fb):
                    load_w(cx, wt, wg, kwg, k0, nk, fb * 256, 256, 0)
                    load_w(cx, wt, wu, kwu, k0, nk, fb * 256, 256, 256)
                banks = gemm(cx, res, "F", hT, KD, G, 512, wl)
                for s in range(2):
                    tf = tmpf.get()
                    bg, bu = banks[s], banks[2 + s]
                    P.op("act", lambda tf=tf, bg=bg: nc.scalar.activation(out=tf.t[:, :], in_=bg.t[:, 0:G], func=AF.Silu),
                         reads=[bg.k], writes=[tf.k])
                    P.op("dve", lambda tf=tf, bu=bu, c=(fb - fb0) * 2 + s: nc.vector.tensor_tensor(
                        out=big.t[:, c, :], in0=bu.t[:, 0:G], in1=tf.t[:, :], op=ALU.mult),
                        reads=[bu.k, tf.k], writes=[big.k])
            nch = (fb1 - fb0) * 2
            for cb in range(D // 512):
                banks = gemm(cx, res, "T", big, nch, G, 512,
                             lambda wt, k0, nk, cb=cb, fb0=fb0: load_w(cx, wt, wd, kwd, fb0 * 2 + k0, nk, cb * 512, 512))
                if hf == 0:
                    resid_out(cx, res, banks, x_b, kb, x_c, kc_, r0, cb * 512, 512)
                else:
                    resid_out(cx, res, banks, x_c, kc_, x_c, kc_, r0, cb * 512, 512)

    if final:
        load_g(K["final_norm"])
        for tt in range(TOK // 128):
            xt = res["xt"].get()
            xn = res["xn"].get()
            st = res["st"].get()
            src = x_c[tt * 128:(tt + 1) * 128, :]
            dst = x_outd[tt * 128:(tt + 1) * 128, :]
            P.dma("sp", lambda xt=xt, src=src: nc.sync.dma_start(out=xt.t[:, :], in_=src), reads=[kc_], writes=[xt.k])
            P.op("dve", lambda st=st: nc.vector.memset(st.t[:, :], 0.0), writes=[st.k])
            P.op("act", lambda xt=xt, xn=xn, st=st: nc.scalar.activation(
                out=xn.t[:, :], in_=xt.t[:, :], func=AF.Square, scale=float(D) ** -0.5, accum_out=st.t[:, 0:1]),
                reads=[xt.k, st.k], writes=[xn.k, st.k])
            rstd_from_ss(cx, st)
            P.op("dve", lambda xt=xt, st=st: nc.vector.scalar_tensor_tensor(
                out=xt.t[:, :], in0=xt.t[:, :], scalar=st.t[:, 2:3], in1=gbc.t[:, :], op0=ALU.mult, op1=ALU.mult),
                reads=[xt.k, st.k, gbc.k], writes=[xt.k])
            P.dma("sp", lambda xt=xt, dst=dst: nc.sync.dma_start(out=dst, in_=xt.t[:, :]), reads=[xt.k])


def stage_wout(cx, K, S, mix, kmix, mw, woutb, kwout, partial, delta, kdelta, rs=True, nring=2):
    nc, P = cx.nc, cx.P
    res = common_res(cx, K)
    mk = mw // 128
    mTr = cx.ring(nring, [128, mk, 512], BF16)
    ev = cx.ring(4, [128, 512], F32)
    kpart = Trk(multi=True)
    n = 0
    for g in range(S // 512):
        m = mTr.get()
        for tt in range(4):
            rows = slice(g * 512 + tt * 128, g * 512 + (tt + 1) * 128)
            pw = min(mw, 4096)
            for pc in range(mw // pw):
                xn = res["xn"].get()
                P.dma("sp", lambda xn=xn, rows=rows, pc=pc, pw=pw: nc.sync.dma_start(out=xn.t[:, 0:pw], in_=mix[rows, pc * pw:(pc + 1) * pw]),
                      reads=[kmix], writes=[xn.k])
                transpose_rows(cx, res, xn, pw // 128, m, tt, k0=pc * (pw // 128))
        for cb in range(D // 512):
            banks = gemm(cx, res, "T", m, mk, 512, 512,
                         lambda wt, k0, nk, cb=cb: load_w(cx, wt, woutb, kwout, k0, nk, cb * 512, 512))
            for tt in range(4):
                e = ev.get()
                n += 1
                if n % 2 == 0:
                    P.op("act", lambda e=e, b=banks[tt]: nc.scalar.copy(e.t[:, :], b.t[:, :]), reads=[banks[tt].k], writes=[e.k])
                else:
                    P.op("dve", lambda e=e, b=banks[tt]: nc.vector.tensor_copy(e.t[:, :], b.t[:, :]), reads=[banks[tt].k], writes=[e.k])
                dst = partial[g * 512 + tt * 128:g * 512 + (tt + 1) * 128, cb * 512:(cb + 1) * 512]
                P.dma("pool", lambda e=e, dst=dst: nc.gpsimd.dma_start(out=dst, in_=e.t[:, :]), reads=[e.k], writes=[kpart])
    if rs:
        CH = min(S, 2048)
        CWl = CH // NCORE
        for q in range(S // CH):
            src = partial[q * CH:(q + 1) * CH, :]
            dst = delta[q * CWl:(q + 1) * CWl, :]
            P.coll(lambda src=src, dst=dst: nc.gpsimd.collective_compute("ReduceScatter", ALU.add, replica_groups=RG, ins=[src], outs=[dst]),
                   reads=[kpart], writes=[kdelta])
    return kpart


def build_full(S, upto=99):
    TOK = S // NCORE
    G = min(512, TOK)
    nc = bass.Bass("TRN2", target_bir_lowering=False)
    cx = Cx(nc)
    P = cx.P
    K = {}
    for name, shape, dt in (
            ("x", [TOK, D], F32), ("w0", [D, W0C], F32), ("w1", [D, W1C], F32),
            ("wout0", [512, D], F32), ("wout1", [1024, D], F32),
            ("wq_s", [2, 512, XAW], F32), ("wk_s", [2, 512, XAW], F32), ("wv_s", [2, 512, XAW], F32),
            ("wo_s", [2, 64, D], F32), ("wg_s", [2, 512, DFF], F32), ("wu_s", [2, 512, DFF], F32),
            ("wd_s", [2, 1376, D], F32),
            ("mem", [256, D], F32), ("mem_norm", [D], F32), ("ev_norm", [D], F32), ("od_norm", [D], F32),
            ("xa_norm", [2, D], F32), ("ffn_norm", [2, D], F32), ("final_norm", [D], F32),
            ("ident", [128, 128], BF16), ("identf", [128, 128], F32), ("ones", [128, 128], BF16),
            ("onesf", [128, 128], F32), ("triu", [128, 128], F32), ("sltri", [128, 128], F32),
            ("trineg", [128, 128], F32), ("tri01", [128, 128], F32), ("cosT", [128, S], F32), ("sinT", [128, S], F32),
            ("selmat", [16, 16, 128], F32),
            ("ret_decT", [128, 128], F32), ("ret_qdec", [128, 128], F32), ("ret_sc", [128, 2], F32),
            ("fox_b", [128, 2], F32), ("m_convw", [128, 10, 4], F32), ("m_convb", [128, 10], F32),
            ("m_dtb", [16, 1], F32), ("m_alog", [128, 16], F32), ("m_dskip", [128, 16], F32), ("m_gnw", [1024], F32)):
        K[name] = cx.dram_in(name, shape, dt)
    x_out = cx.dram_out("x_out", [TOK, D], F32)
    K["x_a"] = cx.dram("x_a", [TOK, D], F32)
    K["x_b"] = cx.dram("x_b", [TOK, D], F32)
    K["x_c0"] = cx.dram("x_c0", [TOK, D], F32)
    K["x_c1"] = cx.dram("x_c1", [TOK, D], F32)
    kxc0, kxc1 = Trk(multi=True), Trk(multi=True)

    stg = cx.ring(3, [128, 4096], BF16)
    W = {}

    def prep(name, src, gather):
        R, N = src.shape
        sh = cx.dram(name + "_b", [R, N], BF16)
        ksh = Trk(multi=True)
        wprep(cx, stg, src, sh, ksh)
        if gather:
            full = cx.dram(name + "_g", [R * NCORE, N], BF16)
            kf = Trk()
            allgather(cx, sh, ksh, full, kf)
            W[name] = (full, kf)
        else:
            W[name] = (sh, ksh)
    prep("w0", K["w0"], False)
    prep("wout0", K["wout0"], False)
    for l in range(2):
        for nm in ("wq", "wk", "wv", "wo", "wg", "wu", "wd"):
            prep("%s%d" % (nm, l), K[nm + "_s"][l], True)
        if l == 0:
            prep("w1", K["w1"], False)
            prep("wout1", K["wout1"], False)
    if upto < 1:
        P.emit({})
        return nc
    hs = cx.dram("hs", [D, TOK], BF16)
    hg = cx.dram("hg", [NCORE * D, TOK], BF16)
    partial = cx.dram("partial", [S, D], F32)
    delta = cx.dram("delta", [TOK, D], F32)
    d0 = {"k": Trk(multi=True),
          "rqT": cx.dram("rqT", [256, S], BF16), "rkT": cx.dram("rkT", [256, S], BF16),
          "rv": cx.dram("rv", [S, 256], BF16), "rg": cx.dram("rg", [S, 256], BF16),
          "fqT": cx.dram("fqT", [2, 128, S], BF16), "fkT": cx.dram("fkT", [2, 128, S], BF16),
          "fv": cx.dram("fv", [2, S, 128], BF16), "ffT": cx.dram("ffT", [2, S], F32),
          "cn": cx.dram("cn", [2, S], F32)}
    d1 = {"k": Trk(multi=True),
          "z": cx.dram("z", [S, 1024], BF16), "xbc": cx.dram("xbc", [1280, S], BF16),
          "dtT": cx.dram("dtT", [16, S], F32)}
    mix0 = cx.dram("mix0", [S, 512], BF16)
    mix1 = cx.dram("mix1", [S, 1024], BF16)

    def run_layer(layer, x_src, kx, x_dst, kxd, final):
        khs, khg, kmix, kdelta = Trk(multi=True), Trk(), Trk(multi=True), Trk()
        cx.phase_barrier()
        stage_norm_gather(cx, K, x_src, kx, K["ev_norm"] if layer == 0 else K["od_norm"], hs, khs, hg, khg, TOK, G)
        cx.phase_barrier()
        if layer == 0:
            stage_a0(cx, K, S, TOK, G, hg, khg, W["w0"][0], W["w0"][1], d0, CW=min(S, 2048) // NCORE)
            cx.phase_barrier()
            stage_m0(cx, K, S, d0, mix0, kmix)
            cx.phase_barrier()
            stage_wout(cx, K, S, mix0, kmix, 512, W["wout0"][0], W["wout0"][1], partial, delta, kdelta)
        else:
            stage_a1(cx, K, S, TOK, G, hg, khg, W["w1"][0], W["w1"][1], d1, CW=min(S, 2048) // NCORE)
            cx.phase_barrier()
            stage_m1(cx, K, S, d1, mix1, kmix)
            cx.phase_barrier()
            stage_wout(cx, K, S, mix1, kmix, 1024, W["wout1"][0], W["wout1"][1], partial, delta, kdelta)
        cx.phase_barrier()
        stage_c(cx, K, TOK, G, layer, delta, kdelta, x_src, kx, x_dst, kxd, W, final)

    if upto == 3:
        run_layer(1, K["x"], Trk(), x_out, kxc1, True)
        cx.phase_barrier()
        for nm, ap in (("z", d1["z"]), ("xbc", d1["xbc"]), ("dtT", d1["dtT"]), ("mix1", mix1)):
            o = cx.dram_out("dbg_" + nm, list(ap.shape), ap.dtype)
            P.dma("sp", lambda o=o, ap=ap: nc.sync.dma_start(out=o, in_=ap))
        P.drain("sp")
        P.emit({})
        return nc
    run_layer(0, K["x"], Trk(), K["x_c0"], kxc0, False)
    if upto >= 2:
        run_layer(1, K["x_c0"], kxc0, x_out, kxc1, True)
    else:
        cx.phase_barrier()
        t = cx.sb([128, D], F32)
        for tt in range(TOK // 128):
            P.dma("sp", lambda tt=tt: nc.sync.dma_start(out=t.t[:, :], in_=K["x_c0"][tt * 128:(tt + 1) * 128, :]), reads=[kxc0], writes=[t.k])
            P.dma("sp", lambda tt=tt: nc.sync.dma_start(out=x_out[tt * 128:(tt + 1) * 128, :], in_=t.t[:, :]), reads=[t.k])
    P.drain("sp")
    P.drain("pool")
    P.emit({})
    return nc


def host_consts(S):
    bf = ml_dtypes.bfloat16
    c = {}
    c["ident"] = np.eye(128, dtype=np.float32).astype(bf)
    c["identf"] = np.eye(128, dtype=np.float32)
    c["ones"] = np.ones((128, 128), np.float32).astype(bf)
    c["onesf"] = np.ones((128, 128), np.float32)
    i = np.arange(128)
    c["triu"] = (i[:, None] <= i[None, :]).astype(np.float32)
    c["tri01"] = c["triu"].copy()
    c["sltri"] = (i[:, None] < i[None, :]).astype(np.float32)
    c["trineg"] = np.where(i[None, :] >= i[:, None], 0.0, -30000.0).astype(np.float32)
    inv = (10000.0 ** (-np.arange(128, dtype=np.float32) / 128)).astype(np.float32)
    ang = (np.arange(S, dtype=np.float32)[None, :] * inv[:, None]).astype(np.float32)
    c["cosT"] = np.cos(ang).astype(np.float32)
    c["sinT"] = np.sin(ang).astype(np.float32)
    sel = np.zeros((16, 16, 128), np.float32)
    for h in range(16):
        sel[h, h, :] = 1.0
    c["selmat"] = sel
    return c


def ret_consts(j):
    g = np.float32(1.0) - np.exp2(np.float32(-5.0 - j)).astype(np.float32)
    lg = np.log1p(-np.exp2(np.float32(-5.0 - j))).astype(np.float32)
    i = np.arange(128, dtype=np.float32)
    diff = i[None, :] - i[:, None]
    decT = np.where(diff >= 0, np.exp(lg * np.maximum(diff, 0)), 0.0).astype(np.float32)
    qdec = np.broadcast_to(np.exp(lg * (i + 1.0))[None, :], (128, 128)).astype(np.float32).copy()
    sc = np.stack([np.exp(lg * (127.0 - i)), np.full(128, np.exp(lg * 128.0))], 1).astype(np.float32)
    return decT, qdec, sc


def kernel_fused(**inp):
    x = np.asarray(inp["x"], np.float32)
    S = x.shape[1]
    TOK = S // NCORE
    f = lambda a: np.ascontiguousarray(np.asarray(a, np.float32))
    ev_w_in = np.asarray(inp["ev_w_in"][0])
    od_w_in = np.asarray(inp["od_w_in"][0])
    ev_w_out = np.asarray(inp["ev_w_out"][0])
    od_w_out = np.asarray(inp["od_w_out"][0])
    conv_w = np.asarray(inp["od_conv_w"][0])
    conv_b = np.asarray(inp["od_conv_b"][0])
    consts = host_consts(S)
    CW = min(S, 2048) // NCORE
    l = np.arange(TOK)
    gidx = [(l // CW) * (NCORE * CW) + j * CW + (l % CW) for j in range(NCORE)]
    maps = []
    for j in range(NCORE):
        m = dict(consts)
        m["x"] = f(x[0, gidx[j]])
        cols = np.concatenate([
            np.arange(j * 256, (j + 1) * 256), 2048 + np.arange(j * 256, (j + 1) * 256),
            4096 + np.arange(j * 256, (j + 1) * 256), 6144 + np.arange(j * 256, (j + 1) * 256),
            8192 + np.arange(j * 256, (j + 1) * 256), 10240 + np.arange(j * 256, (j + 1) * 256),
            12288 + np.arange(j * 256, (j + 1) * 256), 14336 + np.arange(2 * j, 2 * j + 2)])
        m["w0"] = f(ev_w_in[:, cols])
        cols1 = np.concatenate([
            np.arange(j * 1024, (j + 1) * 1024), 8192 + np.arange(j * 1024, (j + 1) * 1024),
            16384 + np.arange(j * 128, (j + 1) * 128), 16384 + 1024 + np.arange(j * 128, (j + 1) * 128),
            18432 + np.arange(j * 16, (j + 1) * 16)])
        m["w1"] = f(od_w_in[:, cols1])
        rows0 = np.concatenate([np.arange(j * 256, (j + 1) * 256), 2048 + np.arange(j * 256, (j + 1) * 256)])
        m["wout0"] = f(ev_w_out[rows0])
        m["wout1"] = f(od_w_out[j * 1024:(j + 1) * 1024])
        for nm, key, R in (("wq_s", "xa_wq", 512), ("wk_s", "xa_wk", 512), ("wv_s", "xa_wv", 512), ("wo_s", "xa_wo", 64),
                           ("wg_s", "ffn_w_gate", 512), ("wu_s", "ffn_w_up", 512), ("wd_s", "ffn_w_down", 1376)):
            m[nm] = f(np.asarray(inp[key])[:, j * R:(j + 1) * R])
        m["mem"] = f(inp["mem"][0])
        m["mem_norm"] = f(inp["mem_norm"])
        m["ev_norm"] = f(inp["ev_mix_norm"][0])
        m["od_norm"] = f(inp["od_mix_norm"][0])
        m["xa_norm"] = f(inp["xa_norm"])
        m["ffn_norm"] = f(inp["ffn_norm"])
        m["final_norm"] = f(inp["final_norm"])
        decT, qdec, sc = ret_consts(j)
        m["ret_decT"], m["ret_qdec"], m["ret_sc"] = decT, qdec, sc
        bfv = np.asarray(inp["ev_b_f"][0], np.float32)[2 * j:2 * j + 2]
        m["fox_b"] = f(np.broadcast_to(bfv[None, :], (128, 2)))
        ch = np.concatenate([j * 1024 + np.arange(1024), 8192 + j * 128 + np.arange(128), 8192 + 1024 + j * 128 + np.arange(128)])
        cwj = conv_w[:, ch]
        m["m_convw"] = f(cwj.T.reshape(10, 128, 4).transpose(1, 0, 2))
        m["m_convb"] = f(conv_b[ch].reshape(10, 128).T)
        hsl = slice(j * 16, (j + 1) * 16)
        m["m_dtb"] = f(np.asarray(inp["od_dt_bias"][0])[hsl].reshape(16, 1))
        m["m_alog"] = f(np.broadcast_to(np.asarray(inp["od_a_log"][0])[hsl][None, :], (128, 16)))
        m["m_dskip"] = f(np.broadcast_to(np.asarray(inp["od_d_skip"][0])[hsl][None, :], (128, 16)))
        m["m_gnw"] = f(np.asarray(inp["od_gn_w"][0])[j * 1024:(j + 1) * 1024])
        maps.append(m)
    nc = build_full(S)
    res = run_bass_kernel_spmd(nc, maps, core_ids=list(range(NCORE)))
    out = np.empty((S, D), np.float32)
    for j in range(NCORE):
        out[gidx[j]] = np.asarray(res.results[j]["x_out"], np.float32)
    global LAST_RES
    LAST_RES = res
    return out[None]


CONST_SHAPES = {
    "ident": ([128, 128], BF16), "identf": ([128, 128], F32), "ones": ([128, 128], BF16),
    "onesf": ([128, 128], F32), "triu": ([128, 128], F32), "sltri": ([128, 128], F32),
    "trineg": ([128, 128], F32), "tri01": ([128, 128], F32), "selmat": ([16, 16, 128], F32),
    "ret_decT": ([128, 128], F32), "ret_qdec": ([128, 128], F32), "ret_sc": ([128, 2], F32),
    "fox_b": ([128, 2], F32), "m_convw": ([128, 10, 4], F32), "m_convb": ([128, 10], F32),
    "m_dtb": ([16, 1], F32), "m_alog": ([128, 16], F32), "m_dskip": ([128, 16], F32), "m_gnw": ([1024], F32),
    "mem": ([256, D], F32), "mem_norm": ([D], F32), "xa_norm": ([2, D], F32), "ffn_norm": ([2, D], F32),
    "final_norm": ([D], F32), "g": ([D], F32),
}


def _declare(cx, names, extra):
    K = {}
    for n in names:
        shape, dt = CONST_SHAPES[n]
        K[n] = cx.dram_in(n, shape, dt)
    for n, (shape, dt) in extra.items():
        K[n] = cx.dram_in(n, shape, dt)
    return K


def build_p1(TOK, G):
    nc = bass.Bass("TRN2", target_bir_lowering=False)
    cx = Cx(nc)
    K = _declare(cx, ["ident", "g"], {"x": ([TOK, D], F32)})
    hs = cx.dram_out("hs", [D, TOK], BF16)
    stage_norm_gather(cx, K, K["x"], Trk(), K["g"], hs, Trk(multi=True), None, None, TOK, G, gather=False)
    cx.P.drain("sp")
    cx.P.drain("pool")
    cx.P.emit({})
    return nc


def build_p2(S, TOK, G, layer):
    nc = bass.Bass("TRN2", target_bir_lowering=False)
    cx = Cx(nc)
    if layer == 0:
        K = _declare(cx, ["ident", "identf", "triu", "sltri", "trineg", "ret_decT", "ret_qdec", "ret_sc", "fox_b"],
                     {"hg": ([NCORE * D, TOK], BF16), "w": ([D, W0C], F32), "cosT": ([128, S], F32), "sinT": ([128, S], F32)})
        wc, mw = W0C, 512
    else:
        K = _declare(cx, ["ident", "identf", "triu", "onesf", "selmat", "tri01", "m_convw", "m_convb", "m_dtb", "m_alog",
                          "m_dskip", "m_gnw"], {"hg": ([NCORE * D, TOK], BF16), "w": ([D, W1C], F32)})
        wc, mw = W1C, 1024
    mix = cx.dram_out("mix", [S, mw], BF16)
    stg = cx.ring(3, [128, 4096], BF16)
    wb = cx.dram("wb", [D, wc], BF16)
    kwb = Trk(multi=True)
    wprep(cx, stg, K["w"], wb, kwb)
    kmix = Trk(multi=True)
    cx.phase_barrier()
    if layer == 0:
        d0 = {"k": Trk(multi=True),
              "rqT": cx.dram("rqT", [256, S], BF16), "rkT": cx.dram("rkT", [256, S], BF16),
              "rv": cx.dram("rv", [S, 256], BF16), "rg": cx.dram("rg", [S, 256], BF16),
              "fqT": cx.dram("fqT", [2, 128, S], BF16), "fkT": cx.dram("fkT", [2, 128, S], BF16),
              "fv": cx.dram("fv", [2, S, 128], BF16), "ffT": cx.dram("ffT", [2, S], F32),
              "cn": cx.dram("cn", [2, S], F32)}
        stage_a0(cx, K, S, TOK, G, K["hg"], Trk(), wb, kwb, d0)
        cx.phase_barrier()
        stage_m0(cx, K, S, d0, mix, kmix)
    else:
        d1 = {"k": Trk(multi=True),
              "z": cx.dram("z", [S, 1024], BF16), "xbc": cx.dram("xbc", [1280, S], BF16),
              "dtT": cx.dram("dtT", [16, S], F32)}
        stage_a1(cx, K, S, TOK, G, K["hg"], Trk(), wb, kwb, d1)
        cx.phase_barrier()
        stage_m1(cx, K, S, d1, mix, kmix)
    cx.P.drain("sp")
    cx.P.drain("pool")
    cx.P.emit({})
    return nc


def build_p3(TOK, G, layer):
    final = layer == 1
    mwt = 4096 if layer == 0 else 8192
    nc = bass.Bass("TRN2", target_bir_lowering=False)
    cx = Cx(nc)
    names = ["ident", "ones", "mem", "mem_norm", "xa_norm", "ffn_norm"] + (["final_norm"] if final else ["g"])
    K = _declare(cx, names, {"x_in": ([TOK, D], F32), "mixed": ([TOK, mwt], BF16), "wout": ([mwt, D], F32),
                             "wq": ([D, XAW], F32), "wk": ([D, XAW], F32), "wv": ([D, XAW], F32), "wo": ([XAW, D], F32),
                             "wg": ([D, DFF], F32), "wu": ([D, DFF], F32), "wd": ([DFF, D], F32)})
    x_out = cx.dram_out("x_out", [TOK, D], F32)
    K["x_a"] = cx.dram("x_a", [TOK, D], F32)
    K["x_b"] = cx.dram("x_b", [TOK, D], F32)
    if final:
        K["x_c1"] = cx.dram("x_c1", [TOK, D], F32)
    else:
        K["x_c0"] = x_out
        hs = cx.dram_out("hs", [D, TOK], BF16)
    stg = cx.ring(3, [128, 4096], BF16)
    W = {}
    for nm in ("wout", "wq", "wk", "wv", "wo", "wg", "wu", "wd"):
        R, N = K[nm].shape
        b = cx.dram(nm + "_b", [R, N], BF16)
        kb = Trk(multi=True)
        wprep(cx, stg, K[nm], b, kb)
        W["%s%d" % (nm, layer)] = (b, kb)
    cx.phase_barrier()
    delta = cx.dram("delta", [TOK, D], F32)
    kdelta = stage_wout(cx, K, TOK, K["mixed"], Trk(), mwt, W["wout%d" % layer][0], W["wout%d" % layer][1],
                        delta, None, None, rs=False, nring=(2 if layer == 0 else 1))
    cx.phase_barrier()
    kxo = Trk(multi=True)
    stage_c(cx, K, TOK, G, layer, delta, kdelta, K["x_in"], Trk(), x_out, kxo, W, final)
    if not final:
        cx.phase_barrier()
        stage_norm_gather(cx, K, x_out, kxo, K["g"], hs, Trk(multi=True), None, None, TOK, G, gather=False)
    cx.P.drain("sp")
    cx.P.drain("pool")
    cx.P.emit({})
    return nc


def kernel_multi(**inp):
    bf = ml_dtypes.bfloat16
    x = np.asarray(inp["x"], np.float32)
    S = x.shape[1]
    TOK = S // NCORE
    G = min(512, TOK)
    cores = list(range(NCORE))
    f = lambda a: np.ascontiguousarray(np.asarray(a, np.float32))
    consts = host_consts(S)
    ev_w_in = np.asarray(inp["ev_w_in"][0])
    od_w_in = np.asarray(inp["od_w_in"][0])
    conv_w = np.asarray(inp["od_conv_w"][0])
    conv_b = np.asarray(inp["od_conv_b"][0])

    def pick(names):
        return {n: consts[n] for n in names}

    nc = build_p1(TOK, G)
    maps = [dict(pick(["ident"]), g=f(inp["ev_mix_norm"][0]), x=f(x[0, j * TOK:(j + 1) * TOK])) for j in cores]
    r = run_bass_kernel_spmd(nc, maps, core_ids=cores)
    hg = np.ascontiguousarray(np.concatenate([np.asarray(r.results[j]["hs"]) for j in cores], 0))
    nc = build_p2(S, TOK, G, 0)
    maps = []
    for j in cores:
        m = pick(["ident", "identf", "triu", "sltri", "trineg", "cosT", "sinT"])
        m["hg"] = hg
        cols = np.concatenate([
            np.arange(j * 256, (j + 1) * 256), 2048 + np.arange(j * 256, (j + 1) * 256),
            4096 + np.arange(j * 256, (j + 1) * 256), 6144 + np.arange(j * 256, (j + 1) * 256),
            8192 + np.arange(j * 256, (j + 1) * 256), 10240 + np.arange(j * 256, (j + 1) * 256),
            12288 + np.arange(j * 256, (j + 1) * 256), 14336 + np.arange(2 * j, 2 * j + 2)])
        m["w"] = f(ev_w_in[:, cols])
        m["ret_decT"], m["ret_qdec"], m["ret_sc"] = ret_consts(j)
        bfv = np.asarray(inp["ev_b_f"][0], np.float32)[2 * j:2 * j + 2]
        m["fox_b"] = f(np.broadcast_to(bfv[None, :], (128, 2)))
        maps.append(m)
    r = run_bass_kernel_spmd(nc, maps, core_ids=cores)
    mixes = [np.asarray(r.results[j]["mix"]) for j in cores]
    rows0 = np.concatenate([np.concatenate([np.arange(j * 256, (j + 1) * 256), 2048 + np.arange(j * 256, (j + 1) * 256)])
                            for j in cores])
    wout0 = f(np.asarray(inp["ev_w_out"][0])[rows0])

    def c_maps(layer, x_shards, mixes, wout, extra):
        maps = []
        for i in cores:
            m = pick(["ident", "ones"])
            m["x_in"] = x_shards[i]
            m["mixed"] = np.ascontiguousarray(np.concatenate([mixes[j][i * TOK:(i + 1) * TOK] for j in cores], 1))
            m["wout"] = wout
            for nm, key in (("wq", "xa_wq"), ("wk", "xa_wk"), ("wv", "xa_wv"), ("wo", "xa_wo"),
                            ("wg", "ffn_w_gate"), ("wu", "ffn_w_up"), ("wd", "ffn_w_down")):
                m[nm] = f(np.asarray(inp[key])[layer])
            m["mem"] = f(inp["mem"][0])
            m["mem_norm"] = f(inp["mem_norm"])
            m["xa_norm"] = f(inp["xa_norm"])
            m["ffn_norm"] = f(inp["ffn_norm"])
            m.update(extra)
            maps.append(m)
        return maps
    nc = build_p3(TOK, G, 0)
    maps = c_maps(0, [f(x[0, j * TOK:(j + 1) * TOK]) for j in cores], mixes, wout0, {"g": f(inp["od_mix_norm"][0])})
    r = run_bass_kernel_spmd(nc, maps, core_ids=cores)
    x1 = [np.asarray(r.results[j]["x_out"], np.float32) for j in cores]
    hg = np.ascontiguousarray(np.concatenate([np.asarray(r.results[j]["hs"]) for j in cores], 0))
    nc = build_p2(S, TOK, G, 1)
    maps = []
    for j in cores:
        m = pick(["ident", "identf", "triu", "onesf", "selmat", "tri01"])
        m["hg"] = hg
        cols1 = np.concatenate([
            np.arange(j * 1024, (j + 1) * 1024), 8192 + np.arange(j * 1024, (j + 1) * 1024),
            16384 + np.arange(j * 128, (j + 1) * 128), 16384 + 1024 + np.arange(j * 128, (j + 1) * 128),
            18432 + np.arange(j * 16, (j + 1) * 16)])
        m["w"] = f(od_w_in[:, cols1])
        ch = np.concatenate([j * 1024 + np.arange(1024), 8192 + j * 128 + np.arange(128), 8192 + 1024 + j * 128 + np.arange(128)])
        m["m_convw"] = f(conv_w[:, ch].T.reshape(10, 128, 4).transpose(1, 0, 2))
        m["m_convb"] = f(conv_b[ch].reshape(10, 128).T)
        hsl = slice(j * 16, (j + 1) * 16)
        m["m_dtb"] = f(np.asarray(inp["od_dt_bias"][0])[hsl].reshape(16, 1))
        m["m_alog"] = f(np.broadcast_to(np.asarray(inp["od_a_log"][0])[hsl][None, :], (128, 16)))
        m["m_dskip"] = f(np.broadcast_to(np.asarray(inp["od_d_skip"][0])[hsl][None, :], (128, 16)))
        m["m_gnw"] = f(np.asarray(inp["od_gn_w"][0])[j * 1024:(j + 1) * 1024])
        maps.append(m)
    r = run_bass_kernel_spmd(nc, maps, core_ids=cores)
    mixes = [np.asarray(r.results[j]["mix"]) for j in cores]
    nc = build_p3(TOK, G, 1)
    maps = c_maps(1, x1, mixes, f(inp["od_w_out"][0]), {"final_norm": f(inp["final_norm"])})
    r = run_bass_kernel_spmd(nc, maps, core_ids=cores)
    out = np.concatenate([np.asarray(r.results[j]["x_out"], np.float32) for j in cores], 0)
    return out[None]


def kernel(**inp):
    return kernel_multi(**inp)
```
